# Optimizing a Trainium2 kernel written in Bass

```python
import math
import jax, jax.numpy as jnp
from jax import lax
import numpy as np

D_MODEL = 2048
BATCH = 8
SEQ = 4096
DEPTH = 1
DEC_BATCH = 8
DEC_SEQ = 32
PAST_LEN = 2048

CHUNK = 64
Q_BLOCK = 128
EPS = 1e-6
N_HEADS = D_MODEL // 128
Q_LORA = 768
KV_LORA = 512
QK_NOPE = 128
QK_ROPE = 64
V_DIM = 128
ROPE_THETA = 10000.0
ATTN_SCALE = 1.0 / math.sqrt(QK_NOPE + QK_ROPE)
S5_WIDTH = D_MODEL
S5_GROUP = 16
S5_GROUPS = S5_WIDTH // S5_GROUP
S5_STATE = 64
D_FF = 4 * D_MODEL
IN_SPLITS = [Q_LORA, Q_LORA + KV_LORA, Q_LORA + KV_LORA + QK_ROPE,
             Q_LORA + KV_LORA + QK_ROPE + S5_WIDTH,
             Q_LORA + KV_LORA + QK_ROPE + S5_WIDTH + D_MODEL]
IN_COLS = Q_LORA + KV_LORA + QK_ROPE + S5_WIDTH + 2 * D_MODEL

kernel_name = 'hybrid_mla_s5_streaming_step'


def rmsnorm(x, g):
    xf = x.astype(jnp.float32)
    y = xf * lax.rsqrt(jnp.mean(xf * xf, axis=-1, keepdims=True) + EPS)
    return (y * g.astype(jnp.float32)).astype(x.dtype)


def rope(x, pos):
    half = x.shape[-1] // 2
    inv = ROPE_THETA ** (-jnp.arange(half, dtype=jnp.float32) / half)
    ang = pos.astype(jnp.float32)[:, None] * inv[None, :]
    cos = jnp.cos(ang)[None, :, None, :]
    sin = jnp.sin(ang)[None, :, None, :]
    x1 = x[..., :half].astype(jnp.float32)
    x2 = x[..., half:].astype(jnp.float32)
    return jnp.concatenate([x1 * cos - x2 * sin, x1 * sin + x2 * cos], axis=-1).astype(x.dtype)


def mixer_inputs(x, pos, norm_mix, w_in, norm_q, w_uq, norm_kv):
    bsz, L, _ = x.shape
    z = rmsnorm(x, norm_mix) @ w_in
    c_q, c_kv, k_r, u, g_a, g_b = jnp.split(z, IN_SPLITS, axis=-1)
    q = (rmsnorm(c_q, norm_q) @ w_uq).reshape(bsz, L, N_HEADS, QK_NOPE + QK_ROPE)
    q_nope = q[..., :QK_NOPE]
    q_rope = rope(q[..., QK_NOPE:], pos)
    ckv = rmsnorm(c_kv, norm_kv)
    krope = rope(k_r[:, :, None, :], pos)[:, :, 0, :]
    return q_nope, q_rope, ckv, krope, u, jax.nn.sigmoid(g_a), jax.nn.sigmoid(g_b)


def latent_attention_block(q_nope, q_rope, k_nope, krope, v, q_pos, k_pos):
    s = (jnp.einsum('bqhd,bkhd->bhqk', q_nope, k_nope).astype(jnp.float32)
         + jnp.einsum('bqhr,bkr->bhqk', q_rope, krope).astype(jnp.float32)) * ATTN_SCALE
    mask = (k_pos // CHUNK)[None, :] <= (q_pos // CHUNK)[:, None]
    p = jax.nn.softmax(jnp.where(mask[None, None], s, -jnp.inf), axis=-1).astype(v.dtype)
    return jnp.einsum('bhqk,bkhd->bqhd', p, v)


def mla_attention(q_nope, q_rope, ckv, krope, q_pos, k_pos, w_uk, w_uv):
    bsz, L = q_nope.shape[:2]
    k_nope = jnp.einsum('bkc,chd->bkhd', ckv, w_uk)
    v = jnp.einsum('bkc,chd->bkhd', ckv, w_uv)
    if L > Q_BLOCK and L % Q_BLOCK == 0:
        nb = L // Q_BLOCK
        qn = q_nope.reshape(bsz, nb, Q_BLOCK, N_HEADS, QK_NOPE).swapaxes(0, 1)
        qr = q_rope.reshape(bsz, nb, Q_BLOCK, N_HEADS, QK_ROPE).swapaxes(0, 1)
        qp = q_pos.reshape(nb, Q_BLOCK)
        o = lax.map(lambda a: latent_attention_block(a[0], a[1], k_nope, krope, v, a[2], k_pos), (qn, qr, qp))
        o = o.swapaxes(0, 1).reshape(bsz, L, N_HEADS, V_DIM)
    else:
        o = latent_attention_block(q_nope, q_rope, k_nope, krope, v, q_pos, k_pos)
    return o.reshape(bsz, L, N_HEADS * V_DIM)


def s5_discretize(a_re, a_im, log_dt, b_re, b_im):
    ar = a_re.astype(jnp.float32)
    ai = a_im.astype(jnp.float32)
    dt = jnp.exp(log_dt.astype(jnp.float32))
    mag = jnp.exp(ar * dt)
    abar_re = mag * jnp.cos(ai * dt)
    abar_im = mag * jnp.sin(ai * dt)
    den = ar * ar + ai * ai
    nr = abar_re - 1.0
    coef_re = (nr * ar + abar_im * ai) / den
    coef_im = (abar_im * ar - nr * ai) / den
    br = b_re.astype(jnp.float32)
    bi = b_im.astype(jnp.float32)
    bbar_re = coef_re[..., None] * br - coef_im[..., None] * bi
    bbar_im = coef_re[..., None] * bi + coef_im[..., None] * br
    return abar_re, abar_im, bbar_re, bbar_im


def _complex_affine_combine(e1, e2):
    a1r, a1i, b1r, b1i = e1
    a2r, a2i, b2r, b2i = e2
    return (a2r * a1r - a2i * a1i, a2r * a1i + a2i * a1r,
            a2r * b1r - a2i * b1i + b2r, a2r * b1i + a2i * b1r + b2i)


def s5_scan(u, h0_re, h0_im, abar_re, abar_im, bbar_re, bbar_im, c_re, c_im):
    bsz, L, _ = u.shape
    blk = CHUNK if L % CHUNK == 0 else L
    nb = L // blk
    ub = u.astype(jnp.float32).reshape(bsz, nb, blk, S5_GROUPS, S5_GROUP).swapaxes(0, 1)
    cr = c_re.astype(jnp.float32)
    ci = c_im.astype(jnp.float32)

    def step(carry, u_blk):
        hr, hi = carry
        bur = jnp.einsum('blgm,gpm->blgp', u_blk, bbar_re)
        bui = jnp.einsum('blgm,gpm->blgp', u_blk, bbar_im)
        bur = bur.at[:, 0].add(abar_re * hr - abar_im * hi)
        bui = bui.at[:, 0].add(abar_re * hi + abar_im * hr)
        ar = jnp.broadcast_to(abar_re, bur.shape)
        ai = jnp.broadcast_to(abar_im, bur.shape)
        _, _, sr, si = lax.associative_scan(_complex_affine_combine, (ar, ai, bur, bui), axis=1)
        y = jnp.einsum('blgp,gmp->blgm', sr, cr) - jnp.einsum('blgp,gmp->blgm', si, ci)
        return (sr[:, -1], si[:, -1]), y

    (hr, hi), ys = lax.scan(step, (h0_re.astype(jnp.float32), h0_im.astype(jnp.float32)), ub)
    return ys.swapaxes(0, 1).reshape(bsz, L, S5_WIDTH), hr, hi


def s5_branch(u, h0_re, h0_im, disc, c_re, c_im, d_skip, w_glu_a, w_glu_b):
    abar_re, abar_im, bbar_re, bbar_im = disc
    y, hr, hi = s5_scan(u, h0_re, h0_im, abar_re, abar_im, bbar_re, bbar_im, c_re, c_im)
    y = y.astype(u.dtype) + d_skip * u
    g = jax.nn.gelu(y)
    return (g @ w_glu_a) * jax.nn.sigmoid(g @ w_glu_b), hr, hi


def merge_and_mlp(x, attn, s5_out, g_a, g_b, w_o, norm_mlp, w_up, w_down):
    h = x + (g_a * attn + g_b * s5_out) @ w_o
    return h + jnp.square(jax.nn.relu(rmsnorm(h, norm_mlp) @ w_up)) @ w_down


def setup_inputs(seed: int = 0) -> dict:
    key = jax.random.key(seed)
    ks = jax.random.split(key, 32)
    f32 = jnp.float32

    def nrm(k, shape, scale):
        return jax.random.normal(k, shape, f32) * scale

    def gain(k, n):
        return 1.0 + 0.01 * jax.random.normal(k, (DEPTH, n), f32)

    n_idx = jnp.arange(S5_STATE, dtype=f32)
    a_re = -0.5 + 0.01 * jax.random.normal(ks[6], (DEPTH, S5_GROUPS, S5_STATE), f32)
    a_im = math.pi * n_idx + 0.01 * jax.random.normal(ks[7], (DEPTH, S5_GROUPS, S5_STATE), f32)
    log_dt = jax.random.uniform(ks[8], (DEPTH, S5_GROUPS, S5_STATE), f32,
                                minval=math.log(0.001), maxval=math.log(0.1))
    return {
        'x_prompt': nrm(ks[0], (BATCH, SEQ, D_MODEL), 1.0),
        'x_sample': nrm(ks[1], (DEC_BATCH, DEC_SEQ, D_MODEL), 1.0),
        'cache_ckv': nrm(ks[2], (DEPTH, DEC_BATCH, PAST_LEN, KV_LORA), 1.0),
        'cache_krope': nrm(ks[3], (DEPTH, DEC_BATCH, PAST_LEN, QK_ROPE), 1.0),
        'state_s5_re': nrm(ks[4], (DEPTH, DEC_BATCH, S5_GROUPS, S5_STATE), 0.1),
        'state_s5_im': nrm(ks[5], (DEPTH, DEC_BATCH, S5_GROUPS, S5_STATE), 0.1),
        'norm_mix': gain(ks[9], D_MODEL),
        'w_in': nrm(ks[10], (DEPTH, D_MODEL, IN_COLS), D_MODEL ** -0.5),
        'norm_q': gain(ks[11], Q_LORA),
        'w_uq': nrm(ks[12], (DEPTH, Q_LORA, N_HEADS * (QK_NOPE + QK_ROPE)), Q_LORA ** -0.5),
        'norm_kv': gain(ks[13], KV_LORA),
        'w_uk': nrm(ks[14], (DEPTH, KV_LORA, N_HEADS, QK_NOPE), KV_LORA ** -0.5),
        'w_uv': nrm(ks[15], (DEPTH, KV_LORA, N_HEADS, V_DIM), KV_LORA ** -0.5),
        's5_a_re': a_re,
        's5_a_im': a_im,
        's5_log_dt': log_dt,
        's5_b_re': nrm(ks[16], (DEPTH, S5_GROUPS, S5_STATE, S5_GROUP), (2.0 * S5_GROUP) ** -0.5),
        's5_b_im': nrm(ks[17], (DEPTH, S5_GROUPS, S5_STATE, S5_GROUP), (2.0 * S5_GROUP) ** -0.5),
        's5_c_re': nrm(ks[18], (DEPTH, S5_GROUPS, S5_GROUP, S5_STATE), (2.0 * S5_STATE) ** -0.5),
        's5_c_im': nrm(ks[19], (DEPTH, S5_GROUPS, S5_GROUP, S5_STATE), (2.0 * S5_STATE) ** -0.5),
        's5_d': nrm(ks[20], (DEPTH, S5_WIDTH), 1.0),
        'w_glu_a': nrm(ks[21], (DEPTH, S5_WIDTH, D_MODEL), S5_WIDTH ** -0.5),
        'w_glu_b': nrm(ks[22], (DEPTH, S5_WIDTH, D_MODEL), S5_WIDTH ** -0.5),
        'w_o': nrm(ks[23], (DEPTH, D_MODEL, D_MODEL), D_MODEL ** -0.5),
        'norm_mlp': gain(ks[24], D_MODEL),
        'w_up': nrm(ks[25], (DEPTH, D_MODEL, D_FF), D_MODEL ** -0.5),
        'w_down': nrm(ks[26], (DEPTH, D_FF, D_MODEL), D_FF ** -0.5),
        'norm_final': 1.0 + 0.01 * jax.random.normal(ks[27], (D_MODEL,), f32),
    }


def reference(x_prompt, x_sample, cache_ckv, cache_krope, state_s5_re, state_s5_im,
              norm_mix, w_in, norm_q, w_uq, norm_kv, w_uk, w_uv,
              s5_a_re, s5_a_im, s5_log_dt, s5_b_re, s5_b_im, s5_c_re, s5_c_im, s5_d,
              w_glu_a, w_glu_b, w_o, norm_mlp, w_up, w_down, norm_final):
    bsz_p, L_p, _ = x_prompt.shape
    L_s = x_sample.shape[1]
    past_len = cache_ckv.shape[2]
    pos_p = jnp.arange(L_p)
    pos_s = past_len + jnp.arange(L_s)
    k_pos_s = jnp.arange(past_len + L_s)
    sdt = state_s5_re.dtype

    hp, hs = x_prompt, x_sample
    ckv_p_l, kr_p_l, sre_p_l, sim_p_l = [], [], [], []
    ckv_s_l, kr_s_l, sre_s_l, sim_s_l = [], [], [], []
    for l in range(DEPTH):
        disc = s5_discretize(s5_a_re[l], s5_a_im[l], s5_log_dt[l], s5_b_re[l], s5_b_im[l])

        qn, qr, ckv, kr, u, ga, gb = mixer_inputs(hp, pos_p, norm_mix[l], w_in[l], norm_q[l], w_uq[l], norm_kv[l])
        attn = mla_attention(qn, qr, ckv, kr, pos_p, pos_p, w_uk[l], w_uv[l])
        h0 = jnp.zeros((bsz_p, S5_GROUPS, S5_STATE), jnp.float32)
        s5o, sr, si = s5_branch(u, h0, h0, disc, s5_c_re[l], s5_c_im[l], s5_d[l], w_glu_a[l], w_glu_b[l])
        hp = merge_and_mlp(hp, attn, s5o, ga, gb, w_o[l], norm_mlp[l], w_up[l], w_down[l])
        ckv_p_l.append(ckv)
        kr_p_l.append(kr)
        sre_p_l.append(sr.astype(sdt))
        sim_p_l.append(si.astype(sdt))

        qn, qr, ckv, kr, u, ga, gb = mixer_inputs(hs, pos_s, norm_mix[l], w_in[l], norm_q[l], w_uq[l], norm_kv[l])
        ckv_all = jnp.concatenate([cache_ckv[l].astype(ckv.dtype), ckv], axis=1)
        kr_all = jnp.concatenate([cache_krope[l].astype(kr.dtype), kr], axis=1)
        attn = mla_attention(qn, qr, ckv_all, kr_all, pos_s, k_pos_s, w_uk[l], w_uv[l])
        s5o, sr, si = s5_branch(u, state_s5_re[l], state_s5_im[l], disc, s5_c_re[l], s5_c_im[l], s5_d[l],
                                w_glu_a[l], w_glu_b[l])
        hs = merge_and_mlp(hs, attn, s5o, ga, gb, w_o[l], norm_mlp[l], w_up[l], w_down[l])
        ckv_s_l.append(ckv)
        kr_s_l.append(kr)
        sre_s_l.append(sr.astype(sdt))
        sim_s_l.append(si.astype(sdt))

    y_prompt = rmsnorm(hp, norm_final)
    y_sample = rmsnorm(hs, norm_final)
    ckv_prompt = jnp.stack(ckv_p_l, axis=0)
    krope_prompt = jnp.stack(kr_p_l, axis=0)
    s5_re_prompt = jnp.stack(sre_p_l, axis=0)
    s5_im_prompt = jnp.stack(sim_p_l, axis=0)
    ckv_sample = jnp.stack(ckv_s_l, axis=0)
    krope_sample = jnp.stack(kr_s_l, axis=0)
    s5_re_sample = jnp.stack(sre_s_l, axis=0)
    s5_im_sample = jnp.stack(sim_s_l, axis=0)
    return (y_prompt, y_sample, ckv_prompt, krope_prompt, s5_re_prompt, s5_im_prompt,
            ckv_sample, krope_sample, s5_re_sample, s5_im_sample)
```

```python
import math
import numpy as np
import concourse.bass as bass
import concourse.mybir as mybir
from concourse.bass_utils import run_bass_kernel_spmd

F32 = mybir.dt.float32
BF16 = mybir.dt.bfloat16
I32 = mybir.dt.int32
AF = mybir.ActivationFunctionType
ALU = mybir.AluOpType
AX = mybir.AxisListType

D = 2048
KC = D // 128
QL = 768
KVL = 512
ROPE = 64
NH = 16
DFF = 8192
EPS = 1e-6
IN_COLS = 7488
SCALE = 1.0 / math.sqrt(192.0)
TWO_PI = 2.0 * math.pi
GELU_C = math.sqrt(2.0 / math.pi)

ENGS = ["sp", "act", "pool", "pe", "dve"]
ND_SEM = {"sp": 12, "pool": 24, "act": 12}


class Buf:
    def __init__(self, ap, key):
        self.ap = ap
        self.key = key

    def __getitem__(self, idx):
        return Buf(self.ap[idx], self.key)

    def bitcast(self, dt):
        return Buf(self.ap.bitcast(dt), self.key)

    def rearrange(self, s, **kw):
        return Buf(self.ap.rearrange(s, **kw), self.key)

    def bc(self, shape):
        return Buf(self.ap.to_broadcast(shape), self.key)

    def rekey(self, key):
        return Buf(self.ap, key)


def _ap(x):
    return x.ap if isinstance(x, Buf) else x


class Op:
    __slots__ = ("eng", "fn", "deps", "is_dma", "sig", "phase", "pe_like", "idx")


class Ctx:
    def __init__(self, nc):
        self.nc = nc
        self.ops = []
        self.last_write = {}
        self.readers = {}
        self.phase = 0
        self.last_on_eng = {}
        self.last_dma_ops = []
        self.pending_barrier = {}

    def add(self, eng, fn, reads=(), writes=(), is_dma=False, extra_deps=(), pe_like=False):
        op = Op()
        op.eng = eng
        op.fn = fn
        op.is_dma = is_dma
        op.phase = self.phase
        op.pe_like = pe_like
        op.sig = None
        op.idx = len(self.ops)
        deps = set(extra_deps)
        for k in reads:
            if k is None:
                continue
            if k in self.last_write:
                deps.add(self.last_write[k])
        for k in writes:
            if k is None:
                continue
            if k in self.last_write:
                deps.add(self.last_write[k])
            for r in self.readers.get(k, ()):
                deps.add(r)
        if eng in self.pending_barrier:
            deps |= self.pending_barrier.pop(eng)
        deps.discard(op.idx)
        op.deps = deps
        for k in writes:
            if k is None:
                continue
            self.last_write[k] = op.idx
            self.readers[k] = []
        for k in reads:
            if k is None:
                continue
            if k in writes:
                continue
            self.readers.setdefault(k, []).append(op.idx)
        self.ops.append(op)
        self.last_on_eng[eng] = op.idx
        if is_dma:
            self.last_dma_ops.append(op.idx)
        return op.idx

    def barrier(self):
        deps = set(self.last_on_eng.values()) | set(self.last_dma_ops)
        self.last_dma_ops = []
        for e in ENGS:
            s = self.pending_barrier.get(e, set())
            self.pending_barrier[e] = s | deps
        self.phase += 1
        self.last_write = {}
        self.readers = {}

    def _keys(self, bufs):
        return [b.key for b in bufs if isinstance(b, Buf)]

    def dma(self, out, in_, eng="sp", extra_deps=(), noncontig=False):
        o, i = _ap(out), _ap(in_)
        nc = self.nc

        def fn(e):
            if noncontig:
                with nc.allow_non_contiguous_dma(reason="small strided setup load"):
                    return e.dma_start(out=o, in_=i)
            return e.dma_start(out=o, in_=i)

        return self.add(eng, fn, reads=self._keys([in_]), writes=self._keys([out]), is_dma=True, extra_deps=extra_deps)

    def mm(self, out, lhsT, rhs, start, stop):
        o, l, r = _ap(out), _ap(lhsT), _ap(rhs)

        def fn(e):
            return e.matmul(o, lhsT=l, rhs=r, start=start, stop=stop)

        return self.add("pe", fn, reads=self._keys([lhsT, rhs]), writes=self._keys([out]), pe_like=True)

    def tr(self, out, in_, ident):
        o, i, d = _ap(out), _ap(in_), _ap(ident)

        def fn(e):
            return e.transpose(out=o, in_=i, identity=d)

        return self.add("pe", fn, reads=self._keys([in_, ident]), writes=self._keys([out]), pe_like=True)

    def op(self, eng, name, outs, ins, **kw):
        oa = {k: _ap(v) for k, v in outs.items()}
        ia = {k: _ap(v) for k, v in ins.items()}

        def fn(e):
            return getattr(e, name)(**oa, **ia, **kw)

        return self.add(eng, fn, reads=self._keys(list(ins.values())), writes=self._keys(list(outs.values())))

    def emit(self, block):
        nc = self.nc
        ops = self.ops
        signaling = set()
        for op in ops:
            for d in op.deps:
                dop = ops[d]
                if dop.eng == "pe" and op.eng == "pe":
                    continue
                signaling.add(d)
        PG = 6
        nphase = self.phase // PG + 1
        sem_ctx = []
        esem = {}
        for e in ENGS:
            for p in range(nphase):
                cm = nc.semaphore(f"s_{e}_{p}")
                sem_ctx.append(cm)
                esem[(e, p)] = cm.__enter__()
        dsem = {}
        for e in ("sp", "pool", "act"):
            for j in range(ND_SEM[e]):
                cm = nc.semaphore(f"d_{e}_{j}")
                sem_ctx.append(cm)
                dsem[(e, j)] = cm.__enter__()
        cnt = {}
        dcnt = {e: 0 for e in ("sp", "pool", "act")}
        prev_same_sem = {}
        for op in ops:
            if op.is_dma:
                n = dcnt[op.eng]
                dcnt[op.eng] += 1
                nd = ND_SEM[op.eng]
                sem = dsem[(op.eng, n % nd)]
                val = 16 * (n // nd + 1)
                op.sig = (sem, val, 16)
                key = (op.eng, n % nd)
                if key in prev_same_sem:
                    op.deps.add(prev_same_sem[key])
                prev_same_sem[key] = op.idx
            elif op.idx in signaling:
                k = (op.eng, op.phase // PG)
                cnt[k] = cnt.get(k, 0) + 1
                op.sig = (esem[k], cnt[k], 1)
        per_eng = {e: [o for o in ops if o.eng == e] for e in ENGS}
        final_dma = [(dsem[k], 16 * ((dcnt[k[0]] - 1 - k[1]) // ND_SEM[k[0]] + 1)) for k in dsem if dcnt[k[0]] > k[1]]

        def run(e_name, eng):
            waited = {}
            for op in per_eng[e_name]:
                for d in sorted(op.deps):
                    dop = ops[d]
                    if dop.eng == "pe" and op.eng == "pe":
                        continue
                    sem, val, _ = dop.sig
                    sid = id(sem)
                    if waited.get(sid, 0) >= val:
                        continue
                    eng.wait_ge(sem, val)
                    waited[sid] = val
                ins = op.fn(eng)
                if op.sig is not None:
                    ins.then_inc(op.sig[0], op.sig[2])
            if e_name == "sp":
                for sem, val in final_dma:
                    eng.wait_ge(sem, val)

        @block.sync
        def _(e):
            run("sp", e)

        @block.scalar
        def _(e):
            run("act", e)

        @block.gpsimd
        def _(e):
            run("pool", e)

        @block.tensor
        def _(e):
            run("pe", e)

        @block.vector
        def _(e):
            run("dve", e)

        return sem_ctx


class Arena:
    def __init__(self, ap, nbytes):
        self.ap = ap
        self.nbytes = nbytes
        self.off = 0
        self.uid = 0

    def reset(self):
        self.off = 0

    def alloc(self, shape, dt, name):
        n = 1
        for s in shape:
            n *= s
        esz = 2 if dt == BF16 else 4
        nb = (n * esz + 31) // 32 * 32
        assert self.off + nb <= self.nbytes, f"SBUF arena overflow at {name}: {self.off}+{nb}>{self.nbytes}"
        a = self.ap[:, self.off // 4:(self.off + nb) // 4]
        self.off += nb
        if dt != F32:
            a = a.bitcast(dt)
        a = a[:, 0:n]
        if len(shape) == 2:
            a = a.rearrange("p (a b) -> p a b", b=shape[1])
        elif len(shape) == 3:
            a = a.rearrange("p (a b c) -> p a b c", b=shape[1], c=shape[2])
        self.uid += 1
        return Buf(a, f"{name}#{self.uid}")


def build_program(LP, LS, NPAST, debug=False):
    nc = bass.Bass("TRN2", target_bir_lowering=False)

    def din(name, shape, dt=F32):
        return nc.dram_tensor(name, shape, dt, kind="ExternalInput").ap()

    def dout(name, shape, dt=F32):
        return nc.dram_tensor(name, shape, dt, kind="ExternalOutput").ap()

    def dscr(name, shape, dt=BF16):
        return nc.dram_tensor(name, shape, dt, kind=("ExternalOutput" if debug else "Internal")).ap()

    I = {}
    I["x_prompt"] = din("x_prompt", [LP, D])
    I["x_sample"] = din("x_sample", [LS, D])
    I["cache_ckv"] = din("cache_ckv", [NPAST, KVL])
    I["cache_krope"] = din("cache_krope", [NPAST, ROPE])
    I["state_re"] = din("state_re", [128, 64])
    I["state_im"] = din("state_im", [128, 64])
    for nm, shp in [("norm_mix", [D]), ("w_in", [D, IN_COLS]), ("norm_q", [QL]), ("w_uq", [QL, 3072]),
                    ("norm_kv", [KVL]), ("w_uk", [KVL, D]), ("w_uv", [KVL, D]),
                    ("s5_a_re", [128, 64]), ("s5_a_im", [128, 64]), ("s5_log_dt", [128, 64]),
                    ("s5_b_re", [128, 64, 16]), ("s5_b_im", [128, 64, 16]),
                    ("s5_c_re", [128, 16, 64]), ("s5_c_im", [128, 16, 64]), ("s5_d", [D]),
                    ("w_glu_a", [D, D]), ("w_glu_b", [D, D]), ("w_o", [D, D]), ("norm_mlp", [D]),
                    ("w_up", [D, DFF]), ("w_down", [DFF, D]), ("norm_final", [D])]:
        I[nm] = din(nm, shp)

    O = {}
    O["y_p"] = dout("y_p", [LP, D])
    O["y_s"] = dout("y_s", [LS, D])
    O["ckv_p"] = dout("ckv_p", [LP, KVL])
    O["kr_p"] = dout("kr_p", [LP, ROPE])
    O["sre_p"] = dout("sre_p", [128, 64])
    O["sim_p"] = dout("sim_p", [128, 64])
    O["ckv_s"] = dout("ckv_s", [LS, KVL])
    O["kr_s"] = dout("kr_s", [LS, ROPE])
    O["sre_s"] = dout("sre_s", [128, 64])
    O["sim_s"] = dout("sim_s", [128, 64])

    W = {}
    for nm in ["w_in", "w_uq", "w_uk", "w_uv", "w_glu_a", "w_glu_b", "w_o", "w_up", "w_down"]:
        W[nm] = nc.dram_tensor(nm + "_bf", list(I[nm].shape), BF16, kind="Internal").ap()

    LMAX = max(LP, LS)
    NKMAX = max(LP, NPAST + LS)
    S = {}
    S["cqnT"] = dscr("cqnT_d", [QL, LMAX])
    S["ckvT"] = dscr("ckvT_d", [KVL, NKMAX])
    S["krT"] = dscr("krT_d", [ROPE, NKMAX])
    S["uT"] = dscr("uT_d", [D, LMAX])
    S["gaT"] = dscr("gaT_d", [D, LMAX])
    S["gbT"] = dscr("gbT_d", [D, LMAX])
    S["QT"] = dscr("QT_d", [D, LMAX])
    S["QrT"] = dscr("QrT_d", [NH * ROPE, LMAX])
    S["KT"] = dscr("KT_d", [D, NKMAX])
    S["V"] = dscr("V_d", [NKMAX, D])
    S["attnT"] = dscr("attnT_d", [D, LMAX])
    S["gT"] = dscr("gT_d", [D, LMAX])
    S["mT"] = dscr("mT_d", [D, LMAX])

    ARENA_BYTES = 204800
    stack = []
    arena_cm = nc.sbuf_tensor("arena", [128, ARENA_BYTES // 4], F32)
    arena_t = arena_cm.__enter__()
    stack.append(arena_cm)
    const_cm = nc.sbuf_tensor("consts", [128, 1024], F32)
    const_t = const_cm.__enter__()
    stack.append(const_cm)
    ps = []
    for i in range(8):
        cm = nc.psum_tensor(f"ps{i}", [128, 512], F32)
        t = cm.__enter__()
        stack.append(cm)
        ps.append(Buf(t[:], f"ps{i}"))

    cx = Ctx(nc)
    ar = Arena(arena_t[:], ARENA_BYTES)
    car = Arena(const_t[:], 4096)

    ident = car.alloc([128], BF16, "ident")
    cx.op("pool", "memset", {"ap": ident}, {}, constant=1.0)
    cx.op("pool", "affine_select", {"out": ident}, {"in_": ident}, pattern=[[-1, 128]], compare_op=ALU.is_equal,
          fill=0.0, base=0, channel_multiplier=1)
    ones = car.alloc([128], BF16, "ones")
    cx.op("pool", "memset", {"ap": ones}, {}, constant=1.0)

    wdeps = {}

    def cast_weight(nm, col_chunks=None):
        rows, cols = I[nm].shape
        wdeps[nm] = []
        if col_chunks is not None:
            for (c0_, c1_) in col_chunks:
                wdeps[nm].append(cx.dma(W[nm][:, c0_:c1_], I[nm][:, c0_:c1_], eng="pool"))
            return
        rc = max(128, (2048 * 2048) // cols // 128 * 128)
        for r0_ in range(0, rows, rc):
            r1_ = min(rows, r0_ + rc)
            wdeps[nm].append(cx.dma(W[nm][r0_:r1_, :], I[nm][r0_:r1_, :], eng="pool"))

    cast_weight("w_in", [(0, 1344), (1344, 1344 + 1536), (1344 + 1536, 1344 + 3072), (1344 + 3072, 1344 + 4608), (1344 + 4608, 7488)])
    deferred_casts = [["w_uq", "w_uk", "w_uv"], ["w_glu_a", "w_glu_b", "w_o", "w_up", "w_down"]]

    def wload(dst, nm, src_ap, eng="sp"):
        return cx.dma(dst, src_ap, eng=eng, extra_deps=wdeps[nm])

    def rstd_from_ss(ss, n, tmp, out):
        cx.op("act", "activation", {"out": tmp}, {"in_": ss}, func=AF.Sqrt, scale=1.0 / n, bias=EPS)
        cx.op("dve", "reciprocal", {"out": out}, {"in_": tmp})

    def range_reduce(eng, x, ki, kf):
        cx.op(eng, "tensor_scalar", {"out": ki}, {"in0": x}, scalar1=float(1.0 / TWO_PI), scalar2=None, op0=ALU.mult)
        cx.op(eng, "tensor_copy", {"out": kf}, {"in_": ki})
        cx.op(eng, "scalar_tensor_tensor", {"out": x}, {"in0": kf, "in1": x}, scalar=float(-TWO_PI), op0=ALU.mult,
              op1=ALU.add)
        cx.op(eng, "tensor_scalar", {"out": x}, {"in0": x}, scalar1=3.1415925, scalar2=-3.1415925, op0=ALU.min, op1=ALU.max)

    seqs = [
        dict(name="p", L=LP, pos0=0, NK=LP, x=I["x_prompt"], ckv_out=O["ckv_p"], kr_out=O["kr_p"], y=O["y_p"],
             sre=O["sre_p"], sim=O["sim_p"], cache=False),
        dict(name="s", L=LS, pos0=NPAST, NK=NPAST + LS, x=I["x_sample"], ckv_out=O["ckv_s"], kr_out=O["kr_s"],
             y=O["y_s"], sre=O["sre_s"], sim=O["sim_s"], cache=True),
    ]

    for sq in seqs:
        L = sq["L"]
        NK = sq["NK"]
        KOFF = NK - L
        pos0 = sq["pos0"]
        TB = min(512, L)
        NTB = L // TB
        TT = min(128, L)
        NTT = L // TT

        cx.barrier()
        ar.reset()
        HL = min(1024, L)
        NHALF = L // HL
        ntt_h = HL // TT
        g1T = ar.alloc([KC], F32, "g1T")
        cx.dma(g1T, I["norm_mix"].rearrange("(k p) -> p k", p=128), noncontig=True)
        gqT = ar.alloc([6], F32, "gqT")
        cx.dma(gqT, I["norm_q"].rearrange("(k p) -> p k", p=128), noncontig=True)
        gkv_b = ar.alloc([KVL], F32, "gkv_b")
        cx.dma(gkv_b, I["norm_kv"].partition_broadcast(128))
        wtok = ar.alloc([KC, 1344], BF16, "wtok")
        cx.dma(wtok, W["w_in"].rearrange("(k p) n -> p k n", p=128)[:, :, 0:1344], extra_deps=wdeps["w_in"][0:1])
        cosk = ar.alloc([NTT, 32], F32, "cosk")
        sink = ar.alloc([NTT, 32], F32, "sink")
        rki = ar.alloc([NTT, 32], I32, "rki")
        rkf = ar.alloc([NTT, 32], F32, "rkf")
        posk = ar.alloc([NTT], F32, "posk")
        invk = ar.alloc([32], F32, "invk")
        cx.op("pool", "iota", {"out": posk}, {}, pattern=[[TT, NTT]], base=pos0, channel_multiplier=1,
              allow_small_or_imprecise_dtypes=True)
        cx.op("pool", "iota", {"out": invk}, {}, pattern=[[1, 32]], base=0, channel_multiplier=0,
              allow_small_or_imprecise_dtypes=True)
        cx.op("act", "activation", {"out": invk}, {"in_": invk}, func=AF.Exp, scale=-math.log(10000.0) / 32.0)
        cx.op("dve", "tensor_tensor", {"out": sink}, {"in0": posk.ap.unsqueeze(2).to_broadcast([128, NTT, 32]) if False else Buf(posk.ap.unsqueeze(2).to_broadcast([128, NTT, 32]), posk.key),
                                                       "in1": Buf(invk.ap.unsqueeze(1).to_broadcast([128, NTT, 32]), invk.key)}, op=ALU.mult)
        cx.op("dve", "tensor_scalar", {"out": cosk}, {"in0": sink}, scalar1=float(math.pi / 2), scalar2=None, op0=ALU.add)
        range_reduce("dve", sink, rki, rkf)
        range_reduce("dve", cosk, rki, rkf)
        cx.op("act", "activation", {"out": sink}, {"in_": sink}, func=AF.Sin)
        cx.op("act", "activation", {"out": cosk}, {"in_": cosk}, func=AF.Sin)

        xnT = ar.alloc([KC, HL], BF16, "xnT")
        cqnT_h = ar.alloc([6, HL], BF16, "cqnT_h")
        ckvT_h = ar.alloc([4, HL], BF16, "ckvT_h")
        krT_h = ar.alloc([HL], BF16, "krT_h")
        xt = [ar.alloc([D], F32, f"xt{i}") for i in range(2)]
        xnb = [ar.alloc([D], BF16, f"xnb{i}") for i in range(2)]
        junk = ar.alloc([D], BF16, "junk")
        cqn = [ar.alloc([QL], BF16, f"cqn{i}") for i in range(2)]
        ckv32 = [ar.alloc([KVL], F32, f"ckv32{i}") for i in range(2)]
        ckvb = [ar.alloc([KVL], BF16, f"ckvb{i}") for i in range(2)]
        kr32 = [ar.alloc([ROPE], F32, f"kr32{i}") for i in range(2)]
        krt = [ar.alloc([4, 32], F32, f"krt{i}") for i in range(2)]
        krb = [ar.alloc([ROPE], BF16, f"krb{i}") for i in range(2)]
        stat = [ar.alloc([16], F32, f"stat{i}") for i in range(2)]
        wg = [ar.alloc([KC, 512], BF16, f"wg{i}") for i in range(2)]
        stg = [ar.alloc([HL], BF16, f"stg{i}") for i in range(3)]

        pT2 = [ps[4].bitcast(BF16), ps[5].bitcast(BF16)]
        pTq = ps[6].bitcast(BF16)
        pTk = ps[7].bitcast(BF16)
        xsrc = sq["x"]
        wv_in = W["w_in"].rearrange("(k p) n -> p k n", p=128)
        it = 0
        for h in range(NHALF):
            for tl in range(ntt_h):
                tt = h * ntt_h + tl
                r0 = tt * TT
                sl = it % 2
                it += 1
                X, XB, ST = xt[sl], xnb[sl], stat[sl]
                cx.dma(X[0:TT], xsrc[r0:r0 + TT, :])
                cx.op("act", "activation", {"out": junk[0:TT], "accum_out": ST[0:TT, 0:1]}, {"in_": X[0:TT]}, func=AF.Square)
                rstd_from_ss(ST[0:TT, 0:1], D, ST[0:TT, 1:2], ST[0:TT, 2:3])
                cx.op("act", "activation", {"out": XB[0:TT]}, {"in_": X[0:TT], "scale": ST[0:TT, 2:3]}, func=AF.Copy)
                for kc in range(KC):
                    cx.tr(pT2[kc // 8][:, (kc % 8) * 128:(kc % 8) * 128 + TT], XB[0:TT, kc * 128:(kc + 1) * 128], ident[0:TT, 0:TT])
                for hb in range(2):
                    cx.op("dve", "tensor_tensor",
                          {"out": xnT[:, hb * 8:(hb + 1) * 8, tl * TT:(tl + 1) * TT]},
                          {"in0": pT2[hb].rearrange("p (a b) -> p a b", b=128)[:, :, 0:TT],
                           "in1": Buf(g1T.ap[:, hb * 8:(hb + 1) * 8].unsqueeze(2).to_broadcast([128, 8, TT]), g1T.key)},
                          op=ALU.mult)
                groups = [(0, 512, ps[0]), (512, 256, ps[1]), (768, 512, ps[2]), (1280, 64, ps[3])]
                for (c0, cn, bank) in groups:
                    for kc in range(KC):
                        cx.mm(bank[0:TT, 0:cn], xnT[:, kc, tl * TT:(tl + 1) * TT], wtok[:, kc, c0:c0 + cn], kc == 0, kc == KC - 1)
                cx.op("act", "activation", {"out": junk[0:TT, 0:512], "accum_out": ST[0:TT, 3:4]}, {"in_": ps[0][0:TT, 0:512]}, func=AF.Square)
                cx.op("act", "activation", {"out": junk[0:TT, 0:256], "accum_out": ST[0:TT, 4:5]}, {"in_": ps[1][0:TT, 0:256]}, func=AF.Square)
                cx.op("dve", "tensor_tensor", {"out": ST[0:TT, 5:6]}, {"in0": ST[0:TT, 3:4], "in1": ST[0:TT, 4:5]}, op=ALU.add)
                rstd_from_ss(ST[0:TT, 5:6], QL, ST[0:TT, 6:7], ST[0:TT, 7:8])
                CQ = cqn[sl]
                cx.op("act", "activation", {"out": CQ[0:TT, 0:512]}, {"in_": ps[0][0:TT, 0:512], "scale": ST[0:TT, 7:8]}, func=AF.Copy)
                cx.op("act", "activation", {"out": CQ[0:TT, 512:768]}, {"in_": ps[1][0:TT, 0:256], "scale": ST[0:TT, 7:8]}, func=AF.Copy)
                for j in range(6):
                    cx.tr(pTq[:, j * 128:j * 128 + TT], CQ[0:TT, j * 128:(j + 1) * 128], ident[0:TT, 0:TT])
                cx.op("dve", "tensor_tensor", {"out": cqnT_h[:, :, tl * TT:(tl + 1) * TT]},
                      {"in0": pTq[:, 0:768].rearrange("p (a b) -> p a b", b=128)[:, :, 0:TT],
                       "in1": Buf(gqT.ap.unsqueeze(2).to_broadcast([128, 6, TT]), gqT.key)}, op=ALU.mult)
                cx.op("act", "activation", {"out": junk[0:TT, 0:512], "accum_out": ST[0:TT, 8:9]}, {"in_": ps[2][0:TT, 0:512]}, func=AF.Square)
                rstd_from_ss(ST[0:TT, 8:9], KVL, ST[0:TT, 9:10], ST[0:TT, 10:11])
                CK = ckv32[sl]
                cx.op("dve", "scalar_tensor_tensor", {"out": CK[0:TT]}, {"in0": ps[2][0:TT, 0:512], "in1": gkv_b[0:TT], "scalar": ST[0:TT, 10:11]},
                      op0=ALU.mult, op1=ALU.mult)
                cx.dma(sq["ckv_out"][r0:r0 + TT, :], CK[0:TT], eng="act")
                CB = ckvb[sl]
                cx.op("act", "activation", {"out": CB[0:TT]}, {"in_": CK[0:TT]}, func=AF.Copy)
                for j in range(4):
                    cx.tr(pTk[:, j * 128:j * 128 + TT], CB[0:TT, j * 128:(j + 1) * 128], ident[0:TT, 0:TT])
                KR, KT_, KB = kr32[sl], krt[sl], krb[sl]
                x1 = ps[3][0:TT, 0:32]
                x2 = ps[3][0:TT, 32:64]
                cs = cosk[0:TT, tt, :]
                sn = sink[0:TT, tt, :]
                cx.op("dve", "tensor_tensor", {"out": KT_[0:TT, 0, :]}, {"in0": x1, "in1": cs}, op=ALU.mult)
                cx.op("dve", "tensor_tensor", {"out": KT_[0:TT, 1, :]}, {"in0": x2, "in1": sn}, op=ALU.mult)
                cx.op("dve", "tensor_tensor", {"out": KT_[0:TT, 2, :]}, {"in0": x1, "in1": sn}, op=ALU.mult)
                cx.op("dve", "tensor_tensor", {"out": KT_[0:TT, 3, :]}, {"in0": x2, "in1": cs}, op=ALU.mult)
                cx.op("dve", "tensor_tensor", {"out": KR[0:TT, 0:32]}, {"in0": KT_[0:TT, 0, :], "in1": KT_[0:TT, 1, :]}, op=ALU.subtract)
                cx.op("dve", "tensor_tensor", {"out": KR[0:TT, 32:64]}, {"in0": KT_[0:TT, 2, :], "in1": KT_[0:TT, 3, :]}, op=ALU.add)
                cx.dma(sq["kr_out"][r0:r0 + TT, :], KR[0:TT], eng="act")
                cx.op("dve", "tensor_copy", {"out": KB[0:TT]}, {"in_": KR[0:TT]})
                cx.tr(pTk[0:64, 512:512 + TT], KB[0:TT, :], ident[0:TT, 0:TT])
                cx.op("dve", "tensor_copy", {"out": ckvT_h[:, :, tl * TT:(tl + 1) * TT]},
                      {"in_": pTk[:, 0:512].rearrange("p (a b) -> p a b", b=128)[:, :, 0:TT]})
                cx.op("dve", "tensor_copy", {"out": krT_h[0:64, tl * TT:(tl + 1) * TT]}, {"in_": pTk[0:64, 512:512 + TT]})
            c0 = h * HL
            cx.dma(S["cqnT"].rearrange("(k p) t -> p k t", p=128)[:, :, c0:c0 + HL], cqnT_h, eng="act")
            cx.dma(S["ckvT"].rearrange("(k p) t -> p k t", p=128)[:, :, KOFF + c0:KOFF + c0 + HL], ckvT_h, eng="act")
            cx.dma(S["krT"][:, KOFF + c0:KOFF + c0 + HL], krT_h[0:64], eng="act")
            ntb_h = HL // TB
            bank_i = 0
            stg_i = 0
            for sb in range(12):
                WG = wg[sb % 2]
                cx.dma(WG, wv_in[:, :, 1344 + sb * 512:1344 + (sb + 1) * 512], extra_deps=wdeps["w_in"][1 + sb // 3:2 + sb // 3])
                if deferred_casts and sb in (2, 8):
                    for nm_ in deferred_casts.pop(0):
                        cast_weight(nm_)
                for j in range(4):
                    cb = sb * 4 + j
                    SG = stg[stg_i % 3]
                    stg_i += 1
                    for tb in range(ntb_h):
                        bank = ps[bank_i % 4]
                        bank_i += 1
                        for kc in range(KC):
                            cx.mm(bank[:, 0:TB], WG[:, kc, j * 128:(j + 1) * 128], xnT[:, kc, tb * TB:(tb + 1) * TB], kc == 0, kc == KC - 1)
                        if cb < 16:
                            cx.op("dve", "tensor_copy", {"out": SG[:, tb * TB:(tb + 1) * TB]}, {"in_": bank[:, 0:TB]})
                        else:
                            cx.op("act", "activation", {"out": SG[:, tb * TB:(tb + 1) * TB]}, {"in_": bank[:, 0:TB]}, func=AF.Sigmoid)
                    dst = S["uT"] if cb < 16 else (S["gaT"] if cb < 32 else S["gbT"])
                    rr = (cb % 16) * 128
                    cx.dma(dst[rr:rr + 128, c0:c0 + HL], SG, eng="act")

        if debug == "A":
            break

        kblocks = [(k0, min(512, NK - k0)) for k0 in range(0, NK, 512)]
        ktiles = [(k0, min(128, NK - k0)) for k0 in range(0, NK, 128)]
        NKT = len(ktiles)

        if sq["cache"]:
            cx.barrier()
            ar.reset()
            c32 = [ar.alloc([KVL], F32, f"c32_{i}") for i in range(2)]
            k32 = [ar.alloc([ROPE], F32, f"k32_{i}") for i in range(2)]
            cb16 = [ar.alloc([KVL], BF16, f"cb16_{i}") for i in range(2)]
            kb16 = [ar.alloc([ROPE], BF16, f"kb16_{i}") for i in range(2)]
            cTs = ar.alloc([4, KOFF], BF16, "cTs")
            kTs = ar.alloc([KOFF], BF16, "kTs")
            pc = [ps[0].bitcast(BF16), ps[1].bitcast(BF16)]
            for tt in range(KOFF // 128):
                sl = tt % 2
                cx.dma(c32[sl], I["cache_ckv"][tt * 128:(tt + 1) * 128, :])
                cx.dma(k32[sl], I["cache_krope"][tt * 128:(tt + 1) * 128, :])
                cx.op("act", "activation", {"out": cb16[sl]}, {"in_": c32[sl]}, func=AF.Copy)
                cx.op("dve", "tensor_copy", {"out": kb16[sl]}, {"in_": k32[sl]})
                for j in range(4):
                    cx.tr(pc[sl][:, j * 128:(j + 1) * 128], cb16[sl][:, j * 128:(j + 1) * 128], ident)
                cx.tr(pc[sl][0:64, 512:640], kb16[sl], ident)
                cx.op("dve", "tensor_copy", {"out": cTs[:, :, tt * 128:(tt + 1) * 128]},
                      {"in_": pc[sl][:, 0:512].rearrange("p (a b) -> p a b", b=128)})
                cx.op("dve", "tensor_copy", {"out": kTs[0:64, tt * 128:(tt + 1) * 128]}, {"in_": pc[sl][0:64, 512:640]})
            cx.dma(S["ckvT"].rearrange("(k p) t -> p k t", p=128)[:, :, 0:KOFF], cTs, eng="act")
            cx.dma(S["krT"][:, 0:KOFF], kTs[0:64], eng="act")

        cx.barrier()
        ar.reset()
        cosF = ar.alloc([L], F32, "cosF")
        sinF = ar.alloc([L], F32, "sinF")
        mark = ar.off
        fki = ar.alloc([L], I32, "fki")
        fkf = ar.alloc([L], F32, "fkf")
        pcol = ar.alloc([2], F32, "pcol")
        cx.op("pool", "iota", {"out": pcol[:, 0:1]}, {}, pattern=[[0, 1]], base=0, channel_multiplier=1,
              allow_small_or_imprecise_dtypes=True)
        cx.op("dve", "tensor_scalar", {"out": pcol[:, 1:2]}, {"in0": pcol[:, 0:1]}, scalar1=32.0, scalar2=-32.0, op0=ALU.is_ge, op1=ALU.mult)
        cx.op("dve", "tensor_tensor", {"out": pcol[:, 0:1]}, {"in0": pcol[:, 0:1], "in1": pcol[:, 1:2]}, op=ALU.add)
        cx.op("act", "activation", {"out": pcol[:, 1:2]}, {"in_": pcol[:, 0:1]}, func=AF.Exp, scale=-math.log(10000.0) / 32.0)
        cx.op("pool", "iota", {"out": sinF}, {}, pattern=[[1, L]], base=pos0, channel_multiplier=0,
              allow_small_or_imprecise_dtypes=True)
        cx.op("dve", "tensor_scalar", {"out": sinF}, {"in0": sinF, "scalar1": pcol[:, 1:2]}, scalar2=None, op0=ALU.mult)
        cx.op("dve", "tensor_scalar", {"out": cosF}, {"in0": sinF}, scalar1=float(math.pi / 2), scalar2=None, op0=ALU.add)
        range_reduce("dve", sinF, fki, fkf)
        range_reduce("dve", cosF, fki, fkf)
        cx.op("act", "activation", {"out": sinF}, {"in_": sinF}, func=AF.Sin)
        cx.op("act", "activation", {"out": cosF}, {"in_": cosF}, func=AF.Sin)
        cx.barrier()
        ar.off = mark
        cqnT = ar.alloc([6, L], BF16, "cqnT")
        cx.dma(cqnT, S["cqnT"].rearrange("(k p) t -> p k t", p=128)[:, :, 0:L])
        wq = [ar.alloc([6, 768], BF16, f"wq{i}") for i in range(2)]
        wrot = [ar.alloc([6, 4, 64], BF16, f"wrot{i}") for i in range(2)]
        qst = [ar.alloc([L], BF16, f"qst{i}") for i in range(2)]
        qrst = [ar.alloc([L], BF16, f"qrst{i}") for i in range(2)]
        rt = [ar.alloc([2, TB], F32, f"rt{i}") for i in range(2)]
        bi_ = 0
        wquv = W["w_uq"].rearrange("(k p) n -> p k n", p=128)
        for hg in range(4):
            WQ, WR = wq[hg % 2], wrot[hg % 2]
            wload(WQ, "w_uq", wquv[:, :, hg * 768:(hg + 1) * 768])
            for hh in range(4):
                cx.op("pool", "tensor_scalar", {"out": WR[:, :, hh, 0:32]}, {"in0": WQ[:, :, hh * 192 + 160:hh * 192 + 192]},
                      scalar1=-1.0, scalar2=None, op0=ALU.mult)
                cx.op("pool", "tensor_copy", {"out": WR[:, :, hh, 32:64]}, {"in_": WQ[:, :, hh * 192 + 128:hh * 192 + 160]})
            for hh in range(4):
                h_ = hg * 4 + hh
                QS, QRS = qst[h_ % 2], qrst[h_ % 2]
                for tb in range(NTB):
                    t0 = tb * TB
                    bn = ps[bi_ % 2]
                    ba = ps[2 + (bi_ % 2) * 2]
                    bb = ps[3 + (bi_ % 2) * 2]
                    RT = rt[bi_ % 2]
                    bi_ += 1
                    for kc in range(6):
                        cx.mm(bn[:, 0:TB], WQ[:, kc, hh * 192:hh * 192 + 128], cqnT[:, kc, t0:t0 + TB], kc == 0, kc == 5)
                    cx.op("act", "activation", {"out": QS[:, t0:t0 + TB]}, {"in_": bn[:, 0:TB]}, func=AF.Copy)
                    for kc in range(6):
                        cx.mm(ba[0:64, 0:TB], WQ[:, kc, hh * 192 + 128:hh * 192 + 192], cqnT[:, kc, t0:t0 + TB], kc == 0, kc == 5)
                    for kc in range(6):
                        cx.mm(bb[0:64, 0:TB], WR[:, kc, hh, :], cqnT[:, kc, t0:t0 + TB], kc == 0, kc == 5)
                    cx.op("dve", "tensor_tensor", {"out": RT[0:64, 0, :]}, {"in0": ba[0:64, 0:TB], "in1": cosF[0:64, t0:t0 + TB]}, op=ALU.mult)
                    cx.op("dve", "tensor_tensor", {"out": RT[0:64, 1, :]}, {"in0": bb[0:64, 0:TB], "in1": sinF[0:64, t0:t0 + TB]}, op=ALU.mult)
                    cx.op("pool", "tensor_tensor", {"out": QRS[0:64, t0:t0 + TB]}, {"in0": RT[0:64, 0, :], "in1": RT[0:64, 1, :]}, op=ALU.add)
                cx.dma(S["QT"][h_ * 128:(h_ + 1) * 128, 0:L], QS, eng="act")
                cx.dma(S["QrT"][h_ * 64:(h_ + 1) * 64, 0:L], QRS[0:64], eng="act")
        cx.barrier()
        ar.reset()
        ckvT = ar.alloc([4, NK], BF16, "ckvT")
        cx.dma(ckvT, S["ckvT"].rearrange("(k p) t -> p k t", p=128)[:, :, 0:NK])
        wk = ar.alloc([4, D], BF16, "wk")
        wv = ar.alloc([4, D], BF16, "wv")
        wload(wk, "w_uk", W["w_uk"].rearrange("(k p) n -> p k n", p=128))
        wload(wv, "w_uv", W["w_uv"].rearrange("(k p) n -> p k n", p=128))
        kst = [ar.alloc([NK], BF16, f"kst{i}") for i in range(2)]
        vst = [ar.alloc([D], BF16, f"vst{i}") for i in range(2)]
        for h_ in range(NH):
            KS = kst[h_ % 2]
            for (k0, kn) in kblocks:
                bn = ps[bi_ % 4]
                bi_ += 1
                for kc in range(4):
                    cx.mm(bn[:, 0:kn], wk[:, kc, h_ * 128:(h_ + 1) * 128], ckvT[:, kc, k0:k0 + kn], kc == 0, kc == 3)
                cx.op("act" if (bi_ % 2) else "dve", "activation" if (bi_ % 2) else "tensor_copy",
                      {"out": KS[:, k0:k0 + kn]}, {"in_": bn[:, 0:kn]}, **({"func": AF.Copy} if (bi_ % 2) else {}))
            cx.dma(S["KT"][h_ * 128:(h_ + 1) * 128, 0:NK], KS, eng="act")
        for ti, (k0, kn) in enumerate(ktiles):
            VS = vst[ti % 2]
            for cb in range(4):
                bn = ps[bi_ % 4]
                bi_ += 1
                for kc in range(4):
                    cx.mm(bn[0:kn, :], ckvT[:, kc, k0:k0 + kn], wv[:, kc, cb * 512:(cb + 1) * 512], kc == 0, kc == 3)
                if cb % 2:
                    cx.op("act", "activation", {"out": VS[0:kn, cb * 512:(cb + 1) * 512]}, {"in_": bn[0:kn, :]}, func=AF.Copy)
                else:
                    cx.op("dve", "tensor_copy", {"out": VS[0:kn, cb * 512:(cb + 1) * 512]}, {"in_": bn[0:kn, :]})
            cx.dma(S["V"][k0:k0 + kn, :], VS[0:kn], eng="act")
        if debug == "Q":
            break

        cx.barrier()
        ar.reset()
        krT = ar.alloc([NK], BF16, "krT")
        cx.dma(krT[0:64], S["krT"][:, 0:NK])
        QTh = [ar.alloc([L], BF16, f"QTh{i}") for i in range(2)]
        QrTh = [ar.alloc([L], BF16, f"QrTh{i}") for i in range(2)]
        KTh = [ar.alloc([NK], BF16, f"KTh{i}") for i in range(2)]
        Vh = [ar.alloc([NKT, 128], BF16, f"Vh{i}") for i in range(2)]
        pts = [ar.alloc([512], BF16, f"pt{i}") for i in range(4)]
        rcp = [ar.alloc([512], F32, f"rcp{i}") for i in range(2)]
        ost = [ar.alloc([L], BF16, f"ost{i}") for i in range(2)]
        nfull = NK // 128

        def attn_loads(h_):
            sl = h_ % 2
            cx.dma(QTh[sl], S["QT"][h_ * 128:(h_ + 1) * 128, 0:L])
            cx.dma(QrTh[sl][0:64], S["QrT"][h_ * 64:(h_ + 1) * 64, 0:L])
            cx.dma(KTh[sl], S["KT"][h_ * 128:(h_ + 1) * 128, 0:NK])
            if nfull:
                cx.dma(Vh[sl][:, 0:nfull, :], S["V"][0:nfull * 128, h_ * 128:(h_ + 1) * 128].rearrange("(kt p) c -> p kt c", p=128))
            if NK % 128:
                rem = NK % 128
                cx.dma(Vh[sl][0:rem, nfull, :], S["V"][nfull * 128:NK, h_ * 128:(h_ + 1) * 128])

        attn_loads(0)
        attn_loads(1)
        jobs = []
        for h_ in range(NH):
            for qb in range(NTB):
                q0 = qb * TB
                if sq["cache"]:
                    tiles = [(k0, kn, False) for (k0, kn) in ktiles]
                else:
                    tiles = [(k0, kn, k0 >= q0) for (k0, kn) in ktiles if k0 < q0 + TB]
                for i, (k0, kn, diag) in enumerate(tiles):
                    jobs.append(dict(h=h_, qb=qb, q0=q0, i=i, n=len(tiles), k0=k0, kn=kn, diag=diag,
                                     qi=h_ * NTB + qb, first_of_head=(qb == 0 and i == 0),
                                     last_of_head=(qb == NTB - 1 and i == len(tiles) - 1)))
        LOOK = 2

        def s_stage(ji, jb):
            sl = jb["h"] % 2
            q0, k0, kn = jb["q0"], jb["k0"], jb["kn"]
            c0 = (k0 - q0) if jb["diag"] else 0
            SB = ps[ji % 4]
            PT = pts[ji % 4]
            cx.mm(SB[0:kn, c0:TB], KTh[sl][:, k0:k0 + kn], QTh[sl][:, q0 + c0:q0 + TB], True, False)
            cx.mm(SB[0:kn, c0:TB], krT[0:64, k0:k0 + kn], QrTh[sl][0:64, q0 + c0:q0 + TB], False, True)
            cx.op("act", "activation", {"out": PT[0:kn, c0:TB]}, {"in_": SB[0:kn, c0:TB]}, func=AF.Exp, scale=SCALE)
            if jb["diag"]:
                cx.op("dve", "memset", {"ap": PT[64:128, c0:c0 + 64]}, {}, constant=0.0)

        def pv_stage(ji, jb):
            sl = jb["h"] % 2
            q0, k0, kn = jb["q0"], jb["k0"], jb["kn"]
            c0 = (k0 - q0) if jb["diag"] else 0
            PT = pts[ji % 4]
            OB = ps[4 + 2 * (jb["qi"] % 2)]
            SM = ps[5 + 2 * (jb["qi"] % 2)]
            RC = rcp[jb["qi"] % 2]
            last = (jb["i"] == jb["n"] - 1)
            cx.mm(OB[:, c0:TB], Vh[sl][0:kn, k0 // 128, :], PT[0:kn, c0:TB], jb["i"] == 0, last)
            cx.mm(SM[:, c0:TB], ones[0:kn, :], PT[0:kn, c0:TB], jb["i"] == 0, last)
            if last:
                cx.op("dve", "reciprocal", {"out": RC[:, 0:TB]}, {"in_": SM[:, 0:TB]})
                cx.op("dve", "tensor_tensor", {"out": ost[sl][:, q0:q0 + TB]}, {"in0": OB[:, 0:TB], "in1": RC[:, 0:TB]}, op=ALU.mult)
            if jb["last_of_head"]:
                cx.dma(S["attnT"][jb["h"] * 128:(jb["h"] + 1) * 128, 0:L], ost[sl], eng="act")
                if jb["h"] + 2 < NH:
                    attn_loads(jb["h"] + 2)

        for ji in range(len(jobs) + LOOK):
            if ji < len(jobs):
                s_stage(ji, jobs[ji])
            if ji >= LOOK:
                pv_stage(ji - LOOK, jobs[ji - LOOK])
        if debug == "T":
            break

        cx.barrier()
        ar.reset()
        TBs = TB
        NA = TBs // 16
        v2 = lambda a: a.rearrange("(ct gl) p -> (gl p) ct", gl=2)
        are = ar.alloc([64], F32, "are")
        aim = ar.alloc([64], F32, "aim")
        dtt = ar.alloc([64], F32, "dtt")
        cx.dma(are, v2(I["s5_a_re"]), noncontig=True)
        cx.dma(aim, v2(I["s5_a_im"]), noncontig=True)
        cx.dma(dtt, v2(I["s5_log_dt"]), noncontig=True)
        dcy = ar.alloc([64], F32, "dcy")
        thr = ar.alloc([64], F32, "thr")
        sth = ar.alloc([64], F32, "sth")
        cth = ar.alloc([64], F32, "cth")
        tki = ar.alloc([64], I32, "tki")
        tkf = ar.alloc([64], F32, "tkf")
        t64 = [ar.alloc([64], F32, f"t64_{i}") for i in range(6)]
        cx.op("act", "activation", {"out": dtt}, {"in_": dtt}, func=AF.Exp)
        cx.op("dve", "tensor_tensor", {"out": dcy}, {"in0": are, "in1": dtt}, op=ALU.mult)
        cx.op("act", "activation", {"out": dcy}, {"in_": dcy}, func=AF.Exp)
        cx.op("dve", "tensor_tensor", {"out": thr}, {"in0": aim, "in1": dtt}, op=ALU.mult)
        cx.op("dve", "tensor_scalar", {"out": cth}, {"in0": thr}, scalar1=float(math.pi / 2), scalar2=None, op0=ALU.add)
        range_reduce("dve", thr, tki, tkf)
        range_reduce("dve", cth, tki, tkf)
        cx.op("act", "activation", {"out": sth}, {"in_": thr}, func=AF.Sin)
        cx.op("act", "activation", {"out": cth}, {"in_": cth}, func=AF.Sin)
        abr, abi, den, nr_, cre, cim = t64
        cx.op("dve", "tensor_tensor", {"out": abr}, {"in0": dcy, "in1": cth}, op=ALU.mult)
        cx.op("dve", "tensor_tensor", {"out": abi}, {"in0": dcy, "in1": sth}, op=ALU.mult)
        cx.op("dve", "tensor_tensor", {"out": den}, {"in0": are, "in1": are}, op=ALU.mult)
        cx.op("dve", "tensor_tensor", {"out": nr_}, {"in0": aim, "in1": aim}, op=ALU.mult)
        cx.op("dve", "tensor_tensor", {"out": den}, {"in0": den, "in1": nr_}, op=ALU.add)
        cx.op("dve", "reciprocal", {"out": den}, {"in_": den})
        cx.op("dve", "tensor_scalar", {"out": nr_}, {"in0": abr}, scalar1=-1.0, scalar2=None, op0=ALU.add)
        cx.op("dve", "tensor_tensor", {"out": cre}, {"in0": nr_, "in1": are}, op=ALU.mult)
        cx.op("dve", "tensor_tensor", {"out": cim}, {"in0": abi, "in1": aim}, op=ALU.mult)
        cx.op("dve", "tensor_tensor", {"out": cre}, {"in0": cre, "in1": cim}, op=ALU.add)
        cx.op("dve", "tensor_tensor", {"out": cre}, {"in0": cre, "in1": den}, op=ALU.mult)
        cx.op("dve", "tensor_tensor", {"out": cim}, {"in0": abi, "in1": are}, op=ALU.mult)
        cx.op("dve", "tensor_tensor", {"out": abr}, {"in0": nr_, "in1": aim}, op=ALU.mult)
        cx.op("dve", "tensor_tensor", {"out": cim}, {"in0": cim, "in1": abr}, op=ALU.subtract)
        cx.op("dve", "tensor_tensor", {"out": cim}, {"in0": cim, "in1": den}, op=ALU.mult)
        v3 = lambda a: a.rearrange("(ct gl) p m -> (gl p) ct m", gl=2)
        BTr = ar.alloc([64, 128], BF16, "BTr")
        BTi = ar.alloc([64, 128], BF16, "BTi")
        CTr = ar.alloc([64, 128], BF16, "CTr")
        CTi = ar.alloc([64, 128], BF16, "CTi")
        mark = ar.off
        bre = ar.alloc([64, 16], F32, "bre")
        bim = ar.alloc([64, 16], F32, "bim")
        bt1 = ar.alloc([64, 16], F32, "bt1")
        bt2 = ar.alloc([64, 16], F32, "bt2")
        cx.dma(bre, v3(I["s5_b_re"]))
        cx.dma(bim, v3(I["s5_b_im"]))
        bcast = lambda b: Buf(b.ap.unsqueeze(2).to_broadcast([128, 64, 16]), b.key)
        xpr = ar.alloc([64, 128], BF16, "xpr")
        xpi = ar.alloc([64, 128], BF16, "xpi")
        cx.op("pool", "memset", {"ap": xpr}, {}, constant=0.0)
        cx.op("pool", "memset", {"ap": xpi}, {}, constant=0.0)
        cx.op("dve", "tensor_tensor", {"out": bt1}, {"in0": bre, "in1": bcast(cre)}, op=ALU.mult)
        cx.op("dve", "tensor_tensor", {"out": bt2}, {"in0": bim, "in1": bcast(cim)}, op=ALU.mult)
        cx.op("dve", "tensor_tensor", {"out": bt1}, {"in0": bt1, "in1": bt2}, op=ALU.subtract)
        cx.op("dve", "tensor_tensor", {"out": bt2}, {"in0": bim, "in1": bcast(cre)}, op=ALU.mult)
        cx.op("dve", "tensor_tensor", {"out": bim}, {"in0": bre, "in1": bcast(cim)}, op=ALU.mult)
        cx.op("dve", "tensor_tensor", {"out": bt2}, {"in0": bt2, "in1": bim}, op=ALU.add)
        for (src, dst) in ((bt1, xpr), (bt2, xpi)):
            s4 = src.rearrange("p (c j) m -> p c j m", j=4)
            d4 = dst.rearrange("p (c j) n -> p c j n", j=4)
            for j in range(4):
                for gl in range(2):
                    cx.op("dve", "tensor_copy", {"out": d4[gl * 64:(gl + 1) * 64, :, j, 32 * j + 16 * gl:32 * j + 16 * gl + 16]},
                          {"in_": s4[gl * 64:(gl + 1) * 64, :, j, :]})
        pb = [ps[0].bitcast(BF16), ps[1].bitcast(BF16)]
        gi_ = 0
        for (src, dst) in ((xpr, BTr), (xpi, BTi)):
            for c8 in range(8):
                P_ = pb[gi_ % 2]
                gi_ += 1
                for i in range(8):
                    cx.tr(P_[:, i * 128:(i + 1) * 128], src[:, c8 * 8 + i, :], ident)
                cx.op("dve", "tensor_copy", {"out": dst[:, c8 * 8:(c8 + 1) * 8, :]}, {"in_": P_.rearrange("p (a b) -> p a b", b=128)})
        cx.barrier()
        ar.off = mark
        ypr = ar.alloc([64, 128], F32, "ypr")
        ypi = ar.alloc([64, 128], F32, "ypi")
        cx.op("pool", "memset", {"ap": ypr[0:32]}, {}, constant=0.0)
        cx.op("pool", "memset", {"ap": ypi[0:32]}, {}, constant=0.0)
        for (srcd, dst) in ((I["s5_c_re"], ypr), (I["s5_c_im"], ypi)):
            c4 = srcd.rearrange("(ct gl) m p -> gl m ct p", gl=2)
            for gl in range(2):
                cx.dma(dst[gl * 16:(gl + 1) * 16, :, gl * 64:(gl + 1) * 64], c4[gl])
        ybr = ar.alloc([64, 128], BF16, "ybr")
        ybi = ar.alloc([64, 128], BF16, "ybi")
        cx.op("act", "activation", {"out": ybr[0:32]}, {"in_": ypr[0:32]}, func=AF.Copy)
        cx.op("act", "activation", {"out": ybi[0:32]}, {"in_": ypi[0:32]}, func=AF.Copy, scale=-1.0)
        cx.op("pool", "memset", {"ap": CTr}, {}, constant=0.0)
        cx.op("pool", "memset", {"ap": CTi}, {}, constant=0.0)
        for (src, dst) in ((ybr, CTr), (ybi, CTi)):
            d4 = dst.rearrange("p (c j) n -> p c j n", j=4)
            for c16 in range(4):
                P_ = pb[gi_ % 2]
                gi_ += 1
                for i in range(16):
                    cx.tr(P_[:, i * 32:(i + 1) * 32], src[0:32, c16 * 16 + i, :], ident[0:32, 0:32])
                p4 = P_[:, 0:512].rearrange("p (c j n) -> p c j n", j=4, n=32)
                for j in range(4):
                    cx.op("dve", "tensor_copy", {"out": d4[:, c16 * 4:(c16 + 1) * 4, j, 32 * j:32 * j + 32]}, {"in_": p4[:, :, j, :]})
        cx.barrier()
        ar.off = mark
        dT = ar.alloc([KC], F32, "dT")
        cx.dma(dT, I["s5_d"].rearrange("(k p) -> p k", p=128), noncontig=True)
        NAB = NA + 16
        mult = ar.alloc([NAB], F32, "mult")
        cx.op("pool", "iota", {"out": mult[:, 0:NA]}, {}, pattern=[[16, NA]], base=0, channel_multiplier=0, allow_small_or_imprecise_dtypes=True)
        cx.op("pool", "iota", {"out": mult[:, NA:NAB]}, {}, pattern=[[1, 16]], base=1, channel_multiplier=0, allow_small_or_imprecise_dtypes=True)
        sAB = ar.alloc([64, NAB], F32, "sAB")
        cAB = ar.alloc([64, NAB], F32, "cAB")
        car_r = ar.alloc([64], F32, "car_r")
        car_i = ar.alloc([64], F32, "car_i")
        mark2 = ar.off
        aki = ar.alloc([64, NAB], I32, "aki")
        akf = ar.alloc([64, NAB], F32, "akf")
        cx.op("dve", "tensor_tensor", {"out": sAB}, {"in0": Buf(thr.ap.unsqueeze(2).to_broadcast([128, 64, NAB]), thr.key),
                                                       "in1": Buf(mult.ap.unsqueeze(1).to_broadcast([128, 64, NAB]), mult.key)}, op=ALU.mult)
        cx.op("dve", "tensor_scalar", {"out": cAB}, {"in0": sAB}, scalar1=float(math.pi / 2), scalar2=None, op0=ALU.add)
        range_reduce("dve", sAB, aki, akf)
        range_reduce("dve", cAB, aki, akf)
        cx.op("act", "activation", {"out": sAB}, {"in_": sAB}, func=AF.Sin)
        cx.op("act", "activation", {"out": cAB}, {"in_": cAB}, func=AF.Sin)
        cx.barrier()
        ar.off = mark2
        if sq["cache"]:
            cx.dma(car_r, v2(I["state_re"]), noncontig=True)
            cx.dma(car_i, v2(I["state_im"]), noncontig=True)
        else:
            cx.op("pool", "memset", {"ap": car_r}, {}, constant=0.0)
            cx.op("pool", "memset", {"ap": car_i}, {}, constant=0.0)
        uch = [ar.alloc([L], BF16, f"uch{i}") for i in range(2)]
        tabc = [ar.alloc([NA, 16], F32, f"tabc{j}") for j in range(4)]
        tabs = [ar.alloc([NA, 16], F32, f"tabs{j}") for j in range(4)]
        tw = [ar.alloc([NA, 16], F32, f"tw{j}") for j in range(2)]
        NW = 4
        A1 = [ar.alloc([TBs], F32, f"A1_{i}") for i in range(NW)]
        A2 = [ar.alloc([TBs], F32, f"A2_{i}") for i in range(NW)]
        A3 = [ar.alloc([TBs], F32, f"A3_{i}") for i in range(NW)]
        A4 = [ar.alloc([TBs], F32, f"A4_{i}") for i in range(NW)]
        hrb = [ar.alloc([TBs], BF16, f"hrb{i}") for i in range(NW)]
        hib = [ar.alloc([TBs], BF16, f"hib{i}") for i in range(NW)]
        yv = [ar.alloc([TBs], F32, f"yv{i}") for i in range(2)]
        y2 = [ar.alloc([TBs], F32, f"y2{i}") for i in range(2)]
        gst = [ar.alloc([L], BF16, f"gst{i}") for i in range(2)]
        yi_ = 0
        for ch in range(KC):
            U = uch[ch % 2]
            cx.dma(U, S["uT"][ch * 128:(ch + 1) * 128, 0:L])
            for j in range(4):
                ct = ch * 4 + j
                cA = Buf(cAB.ap[:, ct, 0:NA].unsqueeze(2).to_broadcast([128, NA, 16]), cAB.key)
                sA = Buf(sAB.ap[:, ct, 0:NA].unsqueeze(2).to_broadcast([128, NA, 16]), sAB.key)
                cB = Buf(cAB.ap[:, ct, NA:NAB].unsqueeze(1).to_broadcast([128, NA, 16]), cAB.key)
                sB = Buf(sAB.ap[:, ct, NA:NAB].unsqueeze(1).to_broadcast([128, NA, 16]), sAB.key)
                cx.op("pool", "tensor_tensor", {"out": tabc[j]}, {"in0": cA, "in1": cB}, op=ALU.mult)
                cx.op("pool", "tensor_tensor", {"out": tw[0]}, {"in0": sA, "in1": sB}, op=ALU.mult)
                cx.op("pool", "tensor_tensor", {"out": tabc[j]}, {"in0": tabc[j], "in1": tw[0]}, op=ALU.subtract)
                cx.op("pool", "tensor_tensor", {"out": tabs[j]}, {"in0": sA, "in1": cB}, op=ALU.mult)
                cx.op("pool", "tensor_tensor", {"out": tw[1]}, {"in0": cA, "in1": sB}, op=ALU.mult)
                cx.op("pool", "tensor_tensor", {"out": tabs[j]}, {"in0": tabs[j], "in1": tw[1]}, op=ALU.add)
            GS = gst[ch % 2]
            Cs = [tabc[j].rearrange("p a b -> p (a b)") for j in range(4)]
            Ss = [tabs[j].rearrange("p a b -> p (a b)") for j in range(4)]
            for tb in range(NTB):
                t0 = tb * TB
                YB = ps[6 + (yi_ % 2)]
                for j in range(4):
                    ct = ch * 4 + j
                    Pr = ps[(j % 3) * 2]
                    Pi = ps[(j % 3) * 2 + 1]
                    cx.mm(Pr[:, 0:TB], BTr[:, ct, :], U[:, t0:t0 + TB], True, True)
                    cx.mm(Pi[:, 0:TB], BTi[:, ct, :], U[:, t0:t0 + TB], True, True)
                    cx.op("act", "activation", {"out": A1[j]}, {"in_": Pr[:, 0:TB]}, func=AF.Copy)
                    cx.op("act", "activation", {"out": A2[j]}, {"in_": Pi[:, 0:TB]}, func=AF.Copy)
                for j in range(4):
                    C_, S_ = Cs[j], Ss[j]
                    cx.op("dve", "tensor_tensor", {"out": A3[j]}, {"in0": A1[j], "in1": S_}, op=ALU.mult)
                    cx.op("dve", "tensor_tensor", {"out": A4[j]}, {"in0": A2[j], "in1": C_}, op=ALU.mult)
                    cx.op("dve", "tensor_tensor", {"out": A1[j]}, {"in0": A1[j], "in1": C_}, op=ALU.mult)
                    cx.op("dve", "tensor_tensor", {"out": A2[j]}, {"in0": A2[j], "in1": S_}, op=ALU.mult)
                    cx.op("dve", "tensor_tensor", {"out": A1[j]}, {"in0": A1[j], "in1": A2[j]}, op=ALU.add)
                    cx.op("dve", "tensor_tensor", {"out": A4[j]}, {"in0": A4[j], "in1": A3[j]}, op=ALU.subtract)
                for j in range(4):
                    ct = ch * 4 + j
                    dk = Buf(dcy.ap[:, ct:ct + 1].to_broadcast([128, TB]), dcy.key)
                    cx.op("dve", "tensor_tensor_scan", {"out": A2[j]}, {"data0": dk, "data1": A1[j], "initial": car_r[:, ct:ct + 1]}, op0=ALU.mult, op1=ALU.add)
                    cx.op("dve", "tensor_tensor_scan", {"out": A3[j]}, {"data0": dk, "data1": A4[j], "initial": car_i[:, ct:ct + 1]}, op0=ALU.mult, op1=ALU.add)
                for j in range(4):
                    ct = ch * 4 + j
                    C_, S_ = Cs[j], Ss[j]
                    cx.op("dve", "tensor_tensor", {"out": A1[j]}, {"in0": A2[j], "in1": C_}, op=ALU.mult)
                    cx.op("dve", "tensor_tensor", {"out": A4[j]}, {"in0": A3[j], "in1": S_}, op=ALU.mult)
                    cx.op("dve", "tensor_tensor", {"out": A1[j]}, {"in0": A1[j], "in1": A4[j]}, op=ALU.subtract)
                    cx.op("act", "activation", {"out": hrb[j]}, {"in_": A1[j]}, func=AF.Copy)
                    cx.op("act", "activation", {"out": car_r[:, ct:ct + 1]}, {"in_": A1[j][:, TB - 1:TB]}, func=AF.Copy)
                    cx.op("dve", "tensor_tensor", {"out": A4[j]}, {"in0": A2[j], "in1": S_}, op=ALU.mult)
                    cx.op("dve", "tensor_tensor", {"out": A2[j]}, {"in0": A3[j], "in1": C_}, op=ALU.mult)
                    cx.op("dve", "tensor_tensor", {"out": A4[j]}, {"in0": A4[j], "in1": A2[j]}, op=ALU.add)
                    cx.op("act", "activation", {"out": hib[j]}, {"in_": A4[j]}, func=AF.Copy)
                    cx.op("act", "activation", {"out": car_i[:, ct:ct + 1]}, {"in_": A4[j][:, TB - 1:TB]}, func=AF.Copy)
                    cx.mm(YB[:, 0:TB], CTr[:, ct, :], hrb[j], j == 0, False)
                    cx.mm(YB[:, 0:TB], CTi[:, ct, :], hib[j], False, j == 3)
                Y, Y2 = yv[yi_ % 2], y2[yi_ % 2]
                yi_ += 1
                cx.op("dve", "scalar_tensor_tensor", {"out": Y}, {"in0": U[:, t0:t0 + TB], "scalar": dT[:, ch:ch + 1], "in1": YB[:, 0:TB]}, op0=ALU.mult, op1=ALU.add)
                cx.op("act", "activation", {"out": Y2}, {"in_": Y}, func=AF.Square)
                cx.op("dve", "tensor_scalar", {"out": Y2}, {"in0": Y2}, scalar1=0.044715, scalar2=1.0, op0=ALU.mult, op1=ALU.add)
                cx.op("pool", "tensor_tensor", {"out": Y2}, {"in0": Y2, "in1": Y}, op=ALU.mult)
                cx.op("act", "activation", {"out": Y2}, {"in_": Y2}, func=AF.Sigmoid, scale=2.0 * GELU_C)
                cx.op("pool", "tensor_tensor", {"out": GS[:, t0:t0 + TB]}, {"in0": Y2, "in1": Y}, op=ALU.mult)
            cx.dma(S["gT"][ch * 128:(ch + 1) * 128, 0:L], GS, eng="act")
        cx.dma(v2(sq["sre"]), car_r, eng="act", noncontig=True)
        cx.dma(v2(sq["sim"]), car_i, eng="act", noncontig=True)
        if debug == "S":
            break

        cx.barrier()
        ar.reset()
        HL2 = min(2048, L)
        NH2 = L // HL2
        gTh = ar.alloc([KC, HL2], BF16, "gTh")
        wa = [ar.alloc([KC, 512], BF16, f"wa{i}") for i in range(2)]
        wb_ = [ar.alloc([KC, 512], BF16, f"wb{i}") for i in range(2)]
        gat = [ar.alloc([TB], BF16, f"gat{i}") for i in range(3)]
        gbt = [ar.alloc([TB], BF16, f"gbt{i}") for i in range(3)]
        att = [ar.alloc([TB], BF16, f"att{i}") for i in range(3)]
        sgt = [ar.alloc([TB], F32, f"sgt{i}") for i in range(2)]
        s5t = [ar.alloc([TB], BF16, f"s5t{i}") for i in range(2)]
        m1t = [ar.alloc([TB], BF16, f"m1t{i}") for i in range(2)]
        mst = [ar.alloc([HL2], BF16, f"mst{i}") for i in range(2)]
        wav = W["w_glu_a"].rearrange("(k p) n -> p k n", p=128)
        wbv = W["w_glu_b"].rearrange("(k p) n -> p k n", p=128)
        li = 0
        bi_ = 0
        for hf in range(NH2):
            c0 = hf * HL2
            cx.dma(gTh, S["gT"].rearrange("(k p) t -> p k t", p=128)[:, :, c0:c0 + HL2])
            for sb in range(4):
                WA, WB = wa[sb % 2], wb_[sb % 2]
                wload(WA, "w_glu_a", wav[:, :, sb * 512:(sb + 1) * 512])
                wload(WB, "w_glu_b", wbv[:, :, sb * 512:(sb + 1) * 512])
                for j in range(4):
                    cb = sb * 4 + j
                    MS = mst[cb % 2]
                    for tb in range(HL2 // TB):
                        t0 = tb * TB
                        g0 = c0 + t0
                        k3 = li % 3
                        k2 = li % 2
                        li += 1
                        cx.dma(gat[k3], S["gaT"][cb * 128:(cb + 1) * 128, g0:g0 + TB])
                        cx.dma(gbt[k3], S["gbT"][cb * 128:(cb + 1) * 128, g0:g0 + TB])
                        cx.dma(att[k3], S["attnT"][cb * 128:(cb + 1) * 128, g0:g0 + TB])
                        BA = ps[(bi_ % 4) * 2]
                        BB = ps[(bi_ % 4) * 2 + 1]
                        bi_ += 1
                        for kc in range(KC):
                            cx.mm(BA[:, 0:TB], WA[:, kc, j * 128:(j + 1) * 128], gTh[:, kc, t0:t0 + TB], kc == 0, kc == KC - 1)
                        for kc in range(KC):
                            cx.mm(BB[:, 0:TB], WB[:, kc, j * 128:(j + 1) * 128], gTh[:, kc, t0:t0 + TB], kc == 0, kc == KC - 1)
                        cx.op("act", "activation", {"out": sgt[k2]}, {"in_": BB[:, 0:TB]}, func=AF.Sigmoid)
                        cx.op("dve", "tensor_tensor", {"out": s5t[k2]}, {"in0": BA[:, 0:TB], "in1": sgt[k2]}, op=ALU.mult)
                        cx.op("pool", "tensor_tensor", {"out": m1t[k2]}, {"in0": gat[k3], "in1": att[k3]}, op=ALU.mult)
                        cx.op("pool", "tensor_tensor", {"out": s5t[k2]}, {"in0": s5t[k2], "in1": gbt[k3]}, op=ALU.mult)
                        cx.op("pool", "tensor_tensor", {"out": MS[:, t0:t0 + TB]}, {"in0": m1t[k2], "in1": s5t[k2]}, op=ALU.add)
                    cx.dma(S["mT"][cb * 128:(cb + 1) * 128, c0:c0 + HL2], MS, eng="act")
        if debug == "G":
            break

        cx.barrier()
        ar.reset()
        TS_ = min(512, L)
        NT4 = TS_ // TT
        g2T = ar.alloc([KC], F32, "g2T")
        cx.dma(g2T, I["norm_mlp"].rearrange("(k p) -> p k", p=128), noncontig=True)
        gfin = ar.alloc([D], F32, "gfin")
        cx.dma(gfin, I["norm_final"].partition_broadcast(128))
        mTs = ar.alloc([KC, TS_], BF16, "mTs")
        hbuf = [ar.alloc([D], F32, f"hbuf{i}") for i in range(NT4)]
        hnb = [ar.alloc([D], BF16, f"hnb{i}") for i in range(2)]
        junk2 = ar.alloc([D], BF16, "junk2")
        hnT = mTs
        hid = ar.alloc([64, TS_], BF16, "hid")
        wo_ = [ar.alloc([4, 512], BF16, f"wo{i}") for i in range(3)]
        wu_ = [ar.alloc([KC, 256], BF16, f"wu{i}") for i in range(2)]
        wd_ = [ar.alloc([8, 512], BF16, f"wd{i}") for i in range(2)]
        rl = [ar.alloc([TS_], F32, f"rl{i}") for i in range(2)]
        st2 = ar.alloc([NT4, 8], F32, "st2")
        wov = W["w_o"].rearrange("(k p) n -> p k n", p=128)
        wuv = W["w_up"].rearrange("(k p) n -> p k n", p=128)
        wdv = W["w_down"].rearrange("(k p) n -> p k n", p=128)
        pT2 = [ps[4].bitcast(BF16), ps[5].bitcast(BF16)]
        wi_o = 0
        wi_d = 0
        bi_ = 0
        for st_ in range(L // TS_):
            s0 = st_ * TS_
            cx.dma(mTs, S["mT"].rearrange("(k p) t -> p k t", p=128)[:, :, s0:s0 + TS_])
            for t_ in range(NT4):
                cx.dma(hbuf[t_][0:TT], sq["x"][s0 + t_ * TT:s0 + (t_ + 1) * TT, :])
            for cb in range(4):
                for kg in range(4):
                    WO = wo_[wi_o % 3]
                    wi_o += 1
                    wload(WO, "w_o", wov[:, kg * 4:(kg + 1) * 4, cb * 512:(cb + 1) * 512])
                    for t_ in range(NT4):
                        for i in range(4):
                            kc = kg * 4 + i
                            cx.mm(ps[t_][0:TT, :], mTs[:, kc, t_ * TT:(t_ + 1) * TT], WO[:, i, :], kc == 0, kc == KC - 1)
                for t_ in range(NT4):
                    cx.op("dve", "tensor_tensor", {"out": hbuf[t_][0:TT, cb * 512:(cb + 1) * 512]},
                          {"in0": ps[t_][0:TT, :], "in1": hbuf[t_][0:TT, cb * 512:(cb + 1) * 512]}, op=ALU.add)
            for t_ in range(NT4):
                HB = hnb[t_ % 2]
                cx.op("act", "activation", {"out": junk2[0:TT], "accum_out": st2[0:TT, t_, 0:1]}, {"in_": hbuf[t_][0:TT]}, func=AF.Square)
                rstd_from_ss(st2[0:TT, t_, 0:1], D, st2[0:TT, t_, 1:2], st2[0:TT, t_, 2:3])
                cx.op("act", "activation", {"out": HB[0:TT]}, {"in_": hbuf[t_][0:TT], "scale": st2[0:TT, t_, 2:3]}, func=AF.Copy)
                for kc in range(KC):
                    cx.tr(pT2[kc // 8][:, (kc % 8) * 128:(kc % 8) * 128 + TT], HB[0:TT, kc * 128:(kc + 1) * 128], ident[0:TT, 0:TT])
                for hb in range(2):
                    cx.op("dve", "tensor_tensor",
                          {"out": hnT[:, hb * 8:(hb + 1) * 8, t_ * TT:(t_ + 1) * TT]},
                          {"in0": pT2[hb].rearrange("p (a b) -> p a b", b=128)[:, :, 0:TT],
                           "in1": Buf(g2T.ap[:, hb * 8:(hb + 1) * 8].unsqueeze(2).to_broadcast([128, 8, TT]), g2T.key)},
                          op=ALU.mult)
            for sb in range(32):
                WU = wu_[sb % 2]
                wload(WU, "w_up", wuv[:, :, sb * 256:(sb + 1) * 256])
                for j in range(2):
                    fb = sb * 2 + j
                    bn = ps[bi_ % 4]
                    RL = rl[bi_ % 2]
                    bi_ += 1
                    for kc in range(KC):
                        cx.mm(bn[:, 0:TS_], WU[:, kc, j * 128:(j + 1) * 128], hnT[:, kc, :], kc == 0, kc == KC - 1)
                    cx.op("act", "activation", {"out": RL}, {"in_": bn[:, 0:TS_]}, func=AF.Relu)
                    cx.op("pool" if (fb % 2) else "dve", "tensor_tensor", {"out": hid[:, fb, :]}, {"in0": RL, "in1": RL}, op=ALU.mult)
            for cb in range(4):
                for kg in range(8):
                    WD = wd_[wi_d % 2]
                    wi_d += 1
                    wload(WD, "w_down", wdv[:, kg * 8:(kg + 1) * 8, cb * 512:(cb + 1) * 512])
                    for t_ in range(NT4):
                        for i in range(8):
                            fk = kg * 8 + i
                            cx.mm(ps[t_][0:TT, :], hid[:, fk, t_ * TT:(t_ + 1) * TT], WD[:, i, :], fk == 0, fk == 63)
                for t_ in range(NT4):
                    cx.op("dve", "tensor_tensor", {"out": hbuf[t_][0:TT, cb * 512:(cb + 1) * 512]},
                          {"in0": ps[t_][0:TT, :], "in1": hbuf[t_][0:TT, cb * 512:(cb + 1) * 512]}, op=ALU.add)
            for t_ in range(NT4):
                cx.op("act", "activation", {"out": junk2[0:TT], "accum_out": st2[0:TT, t_, 3:4]}, {"in_": hbuf[t_][0:TT]}, func=AF.Square)
                rstd_from_ss(st2[0:TT, t_, 3:4], D, st2[0:TT, t_, 4:5], st2[0:TT, t_, 5:6])
                cx.op("dve", "scalar_tensor_tensor", {"out": hbuf[t_][0:TT]}, {"in0": hbuf[t_][0:TT], "scalar": st2[0:TT, t_, 5:6], "in1": gfin[0:TT]},
                      op0=ALU.mult, op1=ALU.mult)
                cx.dma(sq["y"][s0 + t_ * TT:s0 + (t_ + 1) * TT, :], hbuf[t_][0:TT], eng="act")
    with nc.Block() as block:
        sems = cx.emit(block)
    for cm in reversed(sems):
        cm.__exit__(None, None, None)
    for cm in reversed(stack):
        cm.__exit__(None, None, None)
    return nc


LP_FULL, LS_FULL, NPAST_FULL = 4096, 32, 2048
_NC_CACHE = {}


def make_in_maps(inputs, LP, LS, NPAST, ncores):
    maps = []
    f = lambda a: np.ascontiguousarray(np.asarray(a, dtype=np.float32))
    for b in range(ncores):
        m = {
            "x_prompt": f(inputs["x_prompt"][b, :LP]),
            "x_sample": f(inputs["x_sample"][b, :LS]),
            "cache_ckv": f(inputs["cache_ckv"][0, b, :NPAST]),
            "cache_krope": f(inputs["cache_krope"][0, b, :NPAST]),
            "state_re": f(inputs["state_s5_re"][0, b]),
            "state_im": f(inputs["state_s5_im"][0, b]),
        }
        for nm in ["norm_mix", "w_in", "norm_q", "w_uq", "norm_kv", "s5_a_re", "s5_a_im", "s5_log_dt", "s5_b_re",
                   "s5_b_im", "s5_c_re", "s5_c_im", "s5_d", "w_glu_a", "w_glu_b", "w_o", "norm_mlp", "w_up", "w_down"]:
            m[nm] = f(inputs[nm][0])
        m["w_uk"] = f(np.asarray(inputs["w_uk"][0]).reshape(KVL, D))
        m["w_uv"] = f(np.asarray(inputs["w_uv"][0]).reshape(KVL, D))
        m["norm_final"] = f(inputs["norm_final"])
        maps.append(m)
    return maps


def kernel(**inputs):
    n = 8
    key = (LP_FULL, LS_FULL, NPAST_FULL)
    if key not in _NC_CACHE:
        _NC_CACHE[key] = build_program(*key)
    nc = _NC_CACHE[key]
    maps = make_in_maps(inputs, LP_FULL, LS_FULL, NPAST_FULL, n)
    res = run_bass_kernel_spmd(nc, maps, core_ids=list(range(n)))
    R = res.results
    st = lambda k: np.stack([np.asarray(R[b][k], dtype=np.float32) for b in range(n)], axis=0)
    return (st("y_p"), st("y_s"), st("ckv_p")[None], st("kr_p")[None], st("sre_p")[None], st("sim_p")[None],
            st("ckv_s")[None], st("kr_s")[None], st("sre_s")[None], st("sim_s")[None])
```

```python
import math
import numpy as np
import concourse.bass as bass
import concourse.mybir as mybir
from concourse.bass_utils import run_bass_kernel_spmd

F32 = mybir.dt.float32
BF16 = mybir.dt.bfloat16
I32 = mybir.dt.int32
AF = mybir.ActivationFunctionType
ALU = mybir.AluOpType
AX = mybir.AxisListType

D = 2048
KC = D // 128
QL = 768
KVL = 512
ROPE = 64
NH = 16
DFF = 8192
EPS = 1e-6
IN_COLS = 7488
SCALE = 1.0 / math.sqrt(192.0)
TWO_PI = 2.0 * math.pi
GELU_C = math.sqrt(2.0 / math.pi)

ENGS = ["sp", "act", "pool", "pe", "dve"]
ND_SEM = {"sp": 12, "pool": 44}


class Buf:
    def __init__(self, ap, key):
        self.ap = ap
        self.key = key

    def __getitem__(self, idx):
        return Buf(self.ap[idx], self.key)

    def bitcast(self, dt):
        return Buf(self.ap.bitcast(dt), self.key)

    def rearrange(self, s, **kw):
        return Buf(self.ap.rearrange(s, **kw), self.key)

    def bc(self, shape):
        return Buf(self.ap.to_broadcast(shape), self.key)

    def rekey(self, key):
        return Buf(self.ap, key)


def _ap(x):
    return x.ap if isinstance(x, Buf) else x


class Op:
    __slots__ = ("eng", "fn", "deps", "is_dma", "sig", "phase", "pe_like", "idx")


class Ctx:
    def __init__(self, nc):
        self.nc = nc
        self.ops = []
        self.last_write = {}
        self.readers = {}
        self.phase = 0
        self.last_on_eng = {}
        self.last_dma_ops = []
        self.pending_barrier = {}

    def add(self, eng, fn, reads=(), writes=(), is_dma=False, extra_deps=(), pe_like=False):
        op = Op()
        op.eng = eng
        op.fn = fn
        op.is_dma = is_dma
        op.phase = self.phase
        op.pe_like = pe_like
        op.sig = None
        op.idx = len(self.ops)
        deps = set(extra_deps)
        for k in reads:
            if k is None:
                continue
            if k in self.last_write:
                deps.add(self.last_write[k])
        for k in writes:
            if k is None:
                continue
            if k in self.last_write:
                deps.add(self.last_write[k])
            for r in self.readers.get(k, ()):
                deps.add(r)
        if eng in self.pending_barrier:
            deps |= self.pending_barrier.pop(eng)
        deps.discard(op.idx)
        op.deps = deps
        for k in writes:
            if k is None:
                continue
            self.last_write[k] = op.idx
            self.readers[k] = []
        for k in reads:
            if k is None:
                continue
            if k in writes:
                continue
            self.readers.setdefault(k, []).append(op.idx)
        self.ops.append(op)
        self.last_on_eng[eng] = op.idx
        if is_dma:
            self.last_dma_ops.append(op.idx)
        return op.idx

    def barrier(self):
        deps = set(self.last_on_eng.values()) | set(self.last_dma_ops)
        self.last_dma_ops = []
        for e in ENGS:
            s = self.pending_barrier.get(e, set())
            self.pending_barrier[e] = s | deps
        self.phase += 1
        self.last_write = {}
        self.readers = {}

    def _keys(self, bufs):
        out = []
        for b in bufs:
            if isinstance(b, Buf):
                if isinstance(b.key, (list, tuple)):
                    out.extend(b.key)
                else:
                    out.append(b.key)
        return out

    def dma(self, out, in_, eng="sp", extra_deps=(), noncontig=False):
        o, i = _ap(out), _ap(in_)
        nc = self.nc

        def fn(e):
            if noncontig:
                with nc.allow_non_contiguous_dma(reason="small strided setup load"):
                    return e.dma_start(out=o, in_=i)
            return e.dma_start(out=o, in_=i)

        return self.add(eng, fn, reads=self._keys([in_]), writes=self._keys([out]), is_dma=True, extra_deps=extra_deps)

    def mm(self, out, lhsT, rhs, start, stop):
        o, l, r = _ap(out), _ap(lhsT), _ap(rhs)

        def fn(e):
            return e.matmul(o, lhsT=l, rhs=r, start=start, stop=stop)

        return self.add("pe", fn, reads=self._keys([lhsT, rhs]), writes=self._keys([out]), pe_like=True)

    def tr(self, out, in_, ident):
        o, i, d = _ap(out), _ap(in_), _ap(ident)

        def fn(e):
            return e.transpose(out=o, in_=i, identity=d)

        return self.add("pe", fn, reads=self._keys([in_, ident]), writes=self._keys([out]), pe_like=True)

    def op(self, eng, name, outs, ins, **kw):
        oa = {k: _ap(v) for k, v in outs.items()}
        ia = {k: _ap(v) for k, v in ins.items()}

        def fn(e):
            return getattr(e, name)(**oa, **ia, **kw)

        return self.add(eng, fn, reads=self._keys(list(ins.values())), writes=self._keys(list(outs.values())))

    def emit(self, block):
        nc = self.nc
        ops = self.ops
        signaling = set()
        for op in ops:
            for d in op.deps:
                dop = ops[d]
                if dop.eng == "pe" and op.eng == "pe":
                    continue
                signaling.add(d)
        PG = 6
        nphase = self.phase // PG + 1
        sem_ctx = []
        esem = {}
        for e in ENGS:
            for p in range(nphase):
                cm = nc.semaphore(f"s_{e}_{p}")
                sem_ctx.append(cm)
                esem[(e, p)] = cm.__enter__()
        dsem = {}
        for e in ("sp", "pool"):
            for j in range(ND_SEM[e]):
                cm = nc.semaphore(f"d_{e}_{j}")
                sem_ctx.append(cm)
                dsem[(e, j)] = cm.__enter__()
        cnt = {}
        dcnt = {e: 0 for e in ("sp", "pool")}
        prev_same_sem = {}
        for op in ops:
            if op.is_dma:
                n = dcnt[op.eng]
                dcnt[op.eng] += 1
                nd = ND_SEM[op.eng]
                sem = dsem[(op.eng, n % nd)]
                val = 16 * (n // nd + 1)
                op.sig = (sem, val, 16)
                key = (op.eng, n % nd)
                if key in prev_same_sem:
                    op.deps.add(prev_same_sem[key])
                prev_same_sem[key] = op.idx
            elif op.idx in signaling:
                k = (op.eng, op.phase // PG)
                cnt[k] = cnt.get(k, 0) + 1
                op.sig = (esem[k], cnt[k], 1)
        per_eng = {e: [o for o in ops if o.eng == e] for e in ENGS}
        final_dma = [(dsem[k], 16 * ((dcnt[k[0]] - 1 - k[1]) // ND_SEM[k[0]] + 1)) for k in dsem if dcnt[k[0]] > k[1]]

        def run(e_name, eng):
            waited = {}
            for op in per_eng[e_name]:
                for d in sorted(op.deps):
                    dop = ops[d]
                    if dop.eng == "pe" and op.eng == "pe":
                        continue
                    sem, val, _ = dop.sig
                    sid = id(sem)
                    if waited.get(sid, 0) >= val:
                        continue
                    eng.wait_ge(sem, val)
                    waited[sid] = val
                ins = op.fn(eng)
                if op.sig is not None:
                    ins.then_inc(op.sig[0], op.sig[2])
            if e_name == "sp":
                for sem, val in final_dma:
                    eng.wait_ge(sem, val)

        @block.sync
        def _(e):
            run("sp", e)

        @block.scalar
        def _(e):
            run("act", e)

        @block.gpsimd
        def _(e):
            run("pool", e)

        @block.tensor
        def _(e):
            run("pe", e)

        @block.vector
        def _(e):
            run("dve", e)

        return sem_ctx


class Arena:
    def __init__(self, ap, nbytes):
        self.ap = ap
        self.nbytes = nbytes
        self.off = 0
        self.uid = 0

    def reset(self):
        self.off = 0

    def alloc(self, shape, dt, name):
        n = 1
        for s in shape:
            n *= s
        esz = 2 if dt == BF16 else 4
        nb = (n * esz + 31) // 32 * 32
        assert self.off + nb <= self.nbytes, f"SBUF arena overflow at {name}: {self.off}+{nb}>{self.nbytes}"
        a = self.ap[:, self.off // 4:(self.off + nb) // 4]
        self.off += nb
        if dt != F32:
            a = a.bitcast(dt)
        a = a[:, 0:n]
        if len(shape) == 2:
            a = a.rearrange("p (a b) -> p a b", b=shape[1])
        elif len(shape) == 3:
            a = a.rearrange("p (a b c) -> p a b c", b=shape[1], c=shape[2])
        self.uid += 1
        return Buf(a, f"{name}#{self.uid}")


def build_program(LP, LS, NPAST, debug=False):
    nc = bass.Bass("TRN2", target_bir_lowering=False)

    def din(name, shape, dt=F32):
        return nc.dram_tensor(name, shape, dt, kind="ExternalInput").ap()

    def dout(name, shape, dt=F32):
        return nc.dram_tensor(name, shape, dt, kind="ExternalOutput").ap()

    def dscr(name, shape, dt=BF16):
        return nc.dram_tensor(name, shape, dt, kind=("ExternalOutput" if debug else "Internal")).ap()

    I = {}
    I["x_prompt"] = din("x_prompt", [LP, D])
    I["x_sample"] = din("x_sample", [LS, D])
    I["cache_ckv"] = din("cache_ckv", [NPAST, KVL])
    I["cache_krope"] = din("cache_krope", [NPAST, ROPE])
    I["state_re"] = din("state_re", [128, 64])
    I["state_im"] = din("state_im", [128, 64])
    for nm, shp in [("norm_mix", [D]), ("w_in", [D, IN_COLS]), ("norm_q", [QL]), ("w_uq", [QL, 3072]),
                    ("norm_kv", [KVL]), ("w_uk", [KVL, D]), ("w_uv", [KVL, D]),
                    ("s5_a_re", [128, 64]), ("s5_a_im", [128, 64]), ("s5_log_dt", [128, 64]),
                    ("s5_b_re", [128, 64, 16]), ("s5_b_im", [128, 64, 16]),
                    ("s5_c_re", [128, 16, 64]), ("s5_c_im", [128, 16, 64]), ("s5_d", [D]),
                    ("w_glu_a", [D, D]), ("w_glu_b", [D, D]), ("w_o", [D, D]), ("norm_mlp", [D]),
                    ("w_up", [D, DFF]), ("w_down", [DFF, D]), ("norm_final", [D])]:
        I[nm] = din(nm, shp)

    O = {}
    O["y_p"] = dout("y_p", [LP, D])
    O["y_s"] = dout("y_s", [LS, D])
    O["ckv_p"] = dout("ckv_p", [LP, KVL])
    O["kr_p"] = dout("kr_p", [LP, ROPE])
    O["sre_p"] = dout("sre_p", [128, 64])
    O["sim_p"] = dout("sim_p", [128, 64])
    O["ckv_s"] = dout("ckv_s", [LS, KVL])
    O["kr_s"] = dout("kr_s", [LS, ROPE])
    O["sre_s"] = dout("sre_s", [128, 64])
    O["sim_s"] = dout("sim_s", [128, 64])

    W = {}
    for nm in ["w_in", "w_uq", "w_uk", "w_uv", "w_glu_a", "w_glu_b", "w_o", "w_up", "w_down"]:
        W[nm] = nc.dram_tensor(nm + "_bf", list(I[nm].shape), BF16, kind="Internal").ap()

    LMAX = max(LP, LS)
    NKMAX = max(LP, NPAST + LS)
    S = {}
    S["cqnT"] = dscr("cqnT_d", [QL, LMAX])
    S["ckvT"] = dscr("ckvT_d", [KVL, NKMAX])
    S["krT"] = dscr("krT_d", [ROPE, NKMAX])
    S["uT"] = dscr("uT_d", [D, LMAX])
    S["gaT"] = dscr("gaT_d", [D, LMAX])
    S["gbT"] = dscr("gbT_d", [D, LMAX])
    S["QT"] = dscr("QT_d", [D, LMAX])
    S["QrT"] = dscr("QrT_d", [NH * ROPE, LMAX])
    S["KT"] = dscr("KT_d", [D, NKMAX])
    S["V"] = dscr("V_d", [NKMAX, D])
    S["attnT"] = dscr("attnT_d", [D, LMAX])
    S["gT"] = dscr("gT_d", [D, LMAX])
    S["mT"] = dscr("mT_d", [D, LMAX])

    ARENA_BYTES = 204800
    stack = []
    arena_cm = nc.sbuf_tensor("arena", [128, ARENA_BYTES // 4], F32)
    arena_t = arena_cm.__enter__()
    stack.append(arena_cm)
    const_cm = nc.sbuf_tensor("consts", [128, 1024], F32)
    const_t = const_cm.__enter__()
    stack.append(const_cm)
    ps = []
    for i in range(8):
        cm = nc.psum_tensor(f"ps{i}", [128, 512], F32)
        t = cm.__enter__()
        stack.append(cm)
        ps.append(Buf(t[:], f"ps{i}"))

    cx = Ctx(nc)
    ar = Arena(arena_t[:], ARENA_BYTES)
    car = Arena(const_t[:], 4096)

    ident = car.alloc([128], BF16, "ident")
    cx.op("pool", "memset", {"ap": ident}, {}, constant=1.0)
    cx.op("pool", "affine_select", {"out": ident}, {"in_": ident}, pattern=[[-1, 128]], compare_op=ALU.is_equal,
          fill=0.0, base=0, channel_multiplier=1)
    ones = car.alloc([128], BF16, "ones")
    cx.op("pool", "memset", {"ap": ones}, {}, constant=1.0)

    wdeps = {}

    def cast_weight(nm, col_chunks=None):
        rows, cols = I[nm].shape
        wdeps[nm] = []
        if col_chunks is not None:
            for (c0_, c1_) in col_chunks:
                wdeps[nm].append(cx.dma(W[nm][:, c0_:c1_], I[nm][:, c0_:c1_], eng="pool"))
            return
        rc = max(128, (2048 * 2048) // cols // 128 * 128)
        for r0_ in range(0, rows, rc):
            r1_ = min(rows, r0_ + rc)
            wdeps[nm].append(cx.dma(W[nm][r0_:r1_, :], I[nm][r0_:r1_, :], eng="pool"))

    cast_weight("w_in", [(0, 1344), (1344, 1344 + 1536), (1344 + 1536, 1344 + 3072), (1344 + 3072, 1344 + 4608), (1344 + 4608, 7488)])
    deferred_casts = [["w_uq", "w_uk", "w_uv"], ["w_glu_a", "w_glu_b", "w_o", "w_up", "w_down"]]

    def wload(dst, nm, src_ap, eng="sp"):
        return cx.dma(dst, src_ap, eng=eng, extra_deps=wdeps[nm])

    def rstd_from_ss(ss, n, tmp, out):
        cx.op("act", "activation", {"out": tmp}, {"in_": ss}, func=AF.Sqrt, scale=1.0 / n, bias=EPS)
        cx.op("dve", "reciprocal", {"out": out}, {"in_": tmp})

    def range_reduce(eng, x, ki, kf):
        cx.op(eng, "tensor_scalar", {"out": ki}, {"in0": x}, scalar1=float(1.0 / TWO_PI), scalar2=None, op0=ALU.mult)
        cx.op(eng, "tensor_copy", {"out": kf}, {"in_": ki})
        cx.op(eng, "scalar_tensor_tensor", {"out": x}, {"in0": kf, "in1": x}, scalar=float(-TWO_PI), op0=ALU.mult,
              op1=ALU.add)
        cx.op(eng, "tensor_scalar", {"out": x}, {"in0": x}, scalar1=3.1415925, scalar2=-3.1415925, op0=ALU.min, op1=ALU.max)

    seqs = [
        dict(name="p", L=LP, pos0=0, NK=LP, x=I["x_prompt"], ckv_out=O["ckv_p"], kr_out=O["kr_p"], y=O["y_p"],
             sre=O["sre_p"], sim=O["sim_p"], cache=False),
        dict(name="s", L=LS, pos0=NPAST, NK=NPAST + LS, x=I["x_sample"], ckv_out=O["ckv_s"], kr_out=O["kr_s"],
             y=O["y_s"], sre=O["sre_s"], sim=O["sim_s"], cache=True),
    ]

    for sq in seqs:
        L = sq["L"]
        NK = sq["NK"]
        KOFF = NK - L
        pos0 = sq["pos0"]
        TB = min(512, L)
        NTB = L // TB
        TT = min(128, L)
        NTT = L // TT

        cx.barrier()
        ar.reset()
        HL = min(1024, L)
        NHALF = L // HL
        ntt_h = HL // TT
        g1T = ar.alloc([KC], F32, "g1T")
        cx.dma(g1T, I["norm_mix"].rearrange("(k p) -> p k", p=128), noncontig=True)
        gqT = ar.alloc([6], F32, "gqT")
        cx.dma(gqT, I["norm_q"].rearrange("(k p) -> p k", p=128), noncontig=True)
        gkv_b = ar.alloc([KVL], F32, "gkv_b")
        cx.dma(gkv_b, I["norm_kv"].partition_broadcast(128))
        wtok = ar.alloc([KC, 1344], BF16, "wtok")
        cx.dma(wtok, W["w_in"].rearrange("(k p) n -> p k n", p=128)[:, :, 0:1344], extra_deps=wdeps["w_in"][0:1])
        cosk = ar.alloc([NTT, 32], F32, "cosk")
        sink = ar.alloc([NTT, 32], F32, "sink")
        rki = ar.alloc([NTT, 32], I32, "rki")
        rkf = ar.alloc([NTT, 32], F32, "rkf")
        posk = ar.alloc([NTT], F32, "posk")
        invk = ar.alloc([32], F32, "invk")
        cx.op("pool", "iota", {"out": posk}, {}, pattern=[[TT, NTT]], base=pos0, channel_multiplier=1,
              allow_small_or_imprecise_dtypes=True)
        cx.op("pool", "iota", {"out": invk}, {}, pattern=[[1, 32]], base=0, channel_multiplier=0,
              allow_small_or_imprecise_dtypes=True)
        cx.op("act", "activation", {"out": invk}, {"in_": invk}, func=AF.Exp, scale=-math.log(10000.0) / 32.0)
        cx.op("dve", "tensor_tensor", {"out": sink}, {"in0": posk.ap.unsqueeze(2).to_broadcast([128, NTT, 32]) if False else Buf(posk.ap.unsqueeze(2).to_broadcast([128, NTT, 32]), posk.key),
                                                       "in1": Buf(invk.ap.unsqueeze(1).to_broadcast([128, NTT, 32]), invk.key)}, op=ALU.mult)
        cx.op("dve", "tensor_scalar", {"out": cosk}, {"in0": sink}, scalar1=float(math.pi / 2), scalar2=None, op0=ALU.add)
        range_reduce("dve", sink, rki, rkf)
        range_reduce("dve", cosk, rki, rkf)
        cx.op("act", "activation", {"out": sink}, {"in_": sink}, func=AF.Sin)
        cx.op("act", "activation", {"out": cosk}, {"in_": cosk}, func=AF.Sin)

        xnT = ar.alloc([KC, HL], BF16, "xnT")
        cqnT_h = ar.alloc([6, HL], BF16, "cqnT_h")
        ckvT_h = ar.alloc([4, HL], BF16, "ckvT_h")
        krT_h = ar.alloc([HL], BF16, "krT_h")
        xt = [ar.alloc([D], F32, f"xt{i}") for i in range(2)]
        xnb = [ar.alloc([D], BF16, f"xnb{i}") for i in range(2)]
        junk = ar.alloc([D], BF16, "junk")
        cqn = [ar.alloc([QL], BF16, f"cqn{i}") for i in range(2)]
        ckv32 = [ar.alloc([KVL], F32, f"ckv32{i}") for i in range(2)]
        ckvb = [ar.alloc([KVL], BF16, f"ckvb{i}") for i in range(2)]
        kr32 = [ar.alloc([ROPE], F32, f"kr32{i}") for i in range(2)]
        krt = [ar.alloc([4, 32], F32, f"krt{i}") for i in range(2)]
        krb = [ar.alloc([ROPE], BF16, f"krb{i}") for i in range(2)]
        stat = [ar.alloc([16], F32, f"stat{i}") for i in range(2)]
        wg = [ar.alloc([KC, 512], BF16, f"wg{i}") for i in range(2)]
        stg = [ar.alloc([HL], BF16, f"stg{i}") for i in range(3)]

        pT2 = [ps[4].bitcast(BF16), ps[5].bitcast(BF16)]
        pTq = ps[6].bitcast(BF16)
        pTk = ps[7].bitcast(BF16)
        xsrc = sq["x"]
        wv_in = W["w_in"].rearrange("(k p) n -> p k n", p=128)
        it = 0
        for h in range(NHALF):
            for tl in range(ntt_h):
                tt = h * ntt_h + tl
                r0 = tt * TT
                sl = it % 2
                it += 1
                X, XB, ST = xt[sl], xnb[sl], stat[sl]
                cx.dma(X[0:TT], xsrc[r0:r0 + TT, :])
                cx.op("act", "activation", {"out": junk[0:TT], "accum_out": ST[0:TT, 0:1]}, {"in_": X[0:TT]}, func=AF.Square)
                rstd_from_ss(ST[0:TT, 0:1], D, ST[0:TT, 1:2], ST[0:TT, 2:3])
                cx.op("act", "activation", {"out": XB[0:TT]}, {"in_": X[0:TT], "scale": ST[0:TT, 2:3]}, func=AF.Copy)
                for kc in range(KC):
                    cx.tr(pT2[kc // 8][:, (kc % 8) * 128:(kc % 8) * 128 + TT], XB[0:TT, kc * 128:(kc + 1) * 128], ident[0:TT, 0:TT])
                for hb in range(2):
                    cx.op("dve", "tensor_tensor",
                          {"out": xnT[:, hb * 8:(hb + 1) * 8, tl * TT:(tl + 1) * TT]},
                          {"in0": pT2[hb].rearrange("p (a b) -> p a b", b=128)[:, :, 0:TT],
                           "in1": Buf(g1T.ap[:, hb * 8:(hb + 1) * 8].unsqueeze(2).to_broadcast([128, 8, TT]), g1T.key)},
                          op=ALU.mult)
                groups = [(0, 512, ps[0]), (512, 256, ps[1]), (768, 512, ps[2]), (1280, 64, ps[3])]
                for (c0, cn, bank) in groups:
                    for kc in range(KC):
                        cx.mm(bank[0:TT, 0:cn], xnT[:, kc, tl * TT:(tl + 1) * TT], wtok[:, kc, c0:c0 + cn], kc == 0, kc == KC - 1)
                cx.op("act", "activation", {"out": junk[0:TT, 0:512], "accum_out": ST[0:TT, 3:4]}, {"in_": ps[0][0:TT, 0:512]}, func=AF.Square)
                cx.op("act", "activation", {"out": junk[0:TT, 0:256], "accum_out": ST[0:TT, 4:5]}, {"in_": ps[1][0:TT, 0:256]}, func=AF.Square)
                cx.op("dve", "tensor_tensor", {"out": ST[0:TT, 5:6]}, {"in0": ST[0:TT, 3:4], "in1": ST[0:TT, 4:5]}, op=ALU.add)
                rstd_from_ss(ST[0:TT, 5:6], QL, ST[0:TT, 6:7], ST[0:TT, 7:8])
                CQ = cqn[sl]
                cx.op("act", "activation", {"out": CQ[0:TT, 0:512]}, {"in_": ps[0][0:TT, 0:512], "scale": ST[0:TT, 7:8]}, func=AF.Copy)
                cx.op("act", "activation", {"out": CQ[0:TT, 512:768]}, {"in_": ps[1][0:TT, 0:256], "scale": ST[0:TT, 7:8]}, func=AF.Copy)
                for j in range(6):
                    cx.tr(pTq[:, j * 128:j * 128 + TT], CQ[0:TT, j * 128:(j + 1) * 128], ident[0:TT, 0:TT])
                cx.op("dve", "tensor_tensor", {"out": cqnT_h[:, :, tl * TT:(tl + 1) * TT]},
                      {"in0": pTq[:, 0:768].rearrange("p (a b) -> p a b", b=128)[:, :, 0:TT],
                       "in1": Buf(gqT.ap.unsqueeze(2).to_broadcast([128, 6, TT]), gqT.key)}, op=ALU.mult)
                cx.op("act", "activation", {"out": junk[0:TT, 0:512], "accum_out": ST[0:TT, 8:9]}, {"in_": ps[2][0:TT, 0:512]}, func=AF.Square)
                rstd_from_ss(ST[0:TT, 8:9], KVL, ST[0:TT, 9:10], ST[0:TT, 10:11])
                CK = ckv32[sl]
                cx.op("dve", "scalar_tensor_tensor", {"out": CK[0:TT]}, {"in0": ps[2][0:TT, 0:512], "in1": gkv_b[0:TT], "scalar": ST[0:TT, 10:11]},
                      op0=ALU.mult, op1=ALU.mult)
                cx.dma(sq["ckv_out"][r0:r0 + TT, :], CK[0:TT], eng="pool")
                CB = ckvb[sl]
                cx.op("act", "activation", {"out": CB[0:TT]}, {"in_": CK[0:TT]}, func=AF.Copy)
                for j in range(4):
                    cx.tr(pTk[:, j * 128:j * 128 + TT], CB[0:TT, j * 128:(j + 1) * 128], ident[0:TT, 0:TT])
                KR, KT_, KB = kr32[sl], krt[sl], krb[sl]
                x1 = ps[3][0:TT, 0:32]
                x2 = ps[3][0:TT, 32:64]
                cs = cosk[0:TT, tt, :]
                sn = sink[0:TT, tt, :]
                cx.op("dve", "tensor_tensor", {"out": KT_[0:TT, 0, :]}, {"in0": x1, "in1": cs}, op=ALU.mult)
                cx.op("dve", "tensor_tensor", {"out": KT_[0:TT, 1, :]}, {"in0": x2, "in1": sn}, op=ALU.mult)
                cx.op("dve", "tensor_tensor", {"out": KT_[0:TT, 2, :]}, {"in0": x1, "in1": sn}, op=ALU.mult)
                cx.op("dve", "tensor_tensor", {"out": KT_[0:TT, 3, :]}, {"in0": x2, "in1": cs}, op=ALU.mult)
                cx.op("dve", "tensor_tensor", {"out": KR[0:TT, 0:32]}, {"in0": KT_[0:TT, 0, :], "in1": KT_[0:TT, 1, :]}, op=ALU.subtract)
                cx.op("dve", "tensor_tensor", {"out": KR[0:TT, 32:64]}, {"in0": KT_[0:TT, 2, :], "in1": KT_[0:TT, 3, :]}, op=ALU.add)
                cx.dma(sq["kr_out"][r0:r0 + TT, :], KR[0:TT], eng="pool")
                cx.op("dve", "tensor_copy", {"out": KB[0:TT]}, {"in_": KR[0:TT]})
                cx.tr(pTk[0:64, 512:512 + TT], KB[0:TT, :], ident[0:TT, 0:TT])
                cx.op("dve", "tensor_copy", {"out": ckvT_h[:, :, tl * TT:(tl + 1) * TT]},
                      {"in_": pTk[:, 0:512].rearrange("p (a b) -> p a b", b=128)[:, :, 0:TT]})
                cx.op("dve", "tensor_copy", {"out": krT_h[0:64, tl * TT:(tl + 1) * TT]}, {"in_": pTk[0:64, 512:512 + TT]})
            c0 = h * HL
            cx.dma(S["cqnT"].rearrange("(k p) t -> p k t", p=128)[:, :, c0:c0 + HL], cqnT_h, eng="pool")
            cx.dma(S["ckvT"].rearrange("(k p) t -> p k t", p=128)[:, :, KOFF + c0:KOFF + c0 + HL], ckvT_h, eng="pool")
            cx.dma(S["krT"][:, KOFF + c0:KOFF + c0 + HL], krT_h[0:64], eng="pool")
            ntb_h = HL // TB
            bank_i = 0
            stg_i = 0
            for sb in range(12):
                WG = wg[sb % 2]
                cx.dma(WG, wv_in[:, :, 1344 + sb * 512:1344 + (sb + 1) * 512], extra_deps=wdeps["w_in"][1 + sb // 3:2 + sb // 3])
                if deferred_casts and sb == 2 and len(deferred_casts) == 2:
                    for nm_ in deferred_casts.pop(0):
                        cast_weight(nm_)
                for j in range(4):
                    cb = sb * 4 + j
                    SG = stg[stg_i % 3]
                    stg_i += 1
                    for tb in range(ntb_h):
                        bank = ps[bank_i % 4]
                        bank_i += 1
                        for kc in range(KC):
                            cx.mm(bank[:, 0:TB], WG[:, kc, j * 128:(j + 1) * 128], xnT[:, kc, tb * TB:(tb + 1) * TB], kc == 0, kc == KC - 1)
                        if cb < 16:
                            cx.op("dve", "tensor_copy", {"out": SG[:, tb * TB:(tb + 1) * TB]}, {"in_": bank[:, 0:TB]})
                        else:
                            cx.op("act", "activation", {"out": SG[:, tb * TB:(tb + 1) * TB]}, {"in_": bank[:, 0:TB]}, func=AF.Sigmoid)
                    dst = S["uT"] if cb < 16 else (S["gaT"] if cb < 32 else S["gbT"])
                    rr = (cb % 16) * 128
                    cx.dma(dst[rr:rr + 128, c0:c0 + HL], SG, eng="pool")

        if debug == "A":
            break

        kblocks = [(k0, min(512, NK - k0)) for k0 in range(0, NK, 512)]
        ktiles = [(k0, min(128, NK - k0)) for k0 in range(0, NK, 128)]
        NKT = len(ktiles)

        if sq["cache"]:
            cx.barrier()
            ar.reset()
            c32 = [ar.alloc([KVL], F32, f"c32_{i}") for i in range(2)]
            k32 = [ar.alloc([ROPE], F32, f"k32_{i}") for i in range(2)]
            cb16 = [ar.alloc([KVL], BF16, f"cb16_{i}") for i in range(2)]
            kb16 = [ar.alloc([ROPE], BF16, f"kb16_{i}") for i in range(2)]
            cTs = ar.alloc([4, KOFF], BF16, "cTs")
            kTs = ar.alloc([KOFF], BF16, "kTs")
            pc = [ps[0].bitcast(BF16), ps[1].bitcast(BF16)]
            for tt in range(KOFF // 128):
                sl = tt % 2
                cx.dma(c32[sl], I["cache_ckv"][tt * 128:(tt + 1) * 128, :])
                cx.dma(k32[sl], I["cache_krope"][tt * 128:(tt + 1) * 128, :])
                cx.op("act", "activation", {"out": cb16[sl]}, {"in_": c32[sl]}, func=AF.Copy)
                cx.op("dve", "tensor_copy", {"out": kb16[sl]}, {"in_": k32[sl]})
                for j in range(4):
                    cx.tr(pc[sl][:, j * 128:(j + 1) * 128], cb16[sl][:, j * 128:(j + 1) * 128], ident)
                cx.tr(pc[sl][0:64, 512:640], kb16[sl], ident)
                cx.op("dve", "tensor_copy", {"out": cTs[:, :, tt * 128:(tt + 1) * 128]},
                      {"in_": pc[sl][:, 0:512].rearrange("p (a b) -> p a b", b=128)})
                cx.op("dve", "tensor_copy", {"out": kTs[0:64, tt * 128:(tt + 1) * 128]}, {"in_": pc[sl][0:64, 512:640]})
            cx.dma(S["ckvT"].rearrange("(k p) t -> p k t", p=128)[:, :, 0:KOFF], cTs, eng="pool")
            cx.dma(S["krT"][:, 0:KOFF], kTs[0:64], eng="pool")

        cx.barrier()
        ar.reset()
        cosF = ar.alloc([L], F32, "cosF")
        sinF = ar.alloc([L], F32, "sinF")
        mark = ar.off
        fki = ar.alloc([L], I32, "fki")
        fkf = ar.alloc([L], F32, "fkf")
        pcol = ar.alloc([2], F32, "pcol")
        cx.op("pool", "iota", {"out": pcol[:, 0:1]}, {}, pattern=[[0, 1]], base=0, channel_multiplier=1,
              allow_small_or_imprecise_dtypes=True)
        cx.op("dve", "tensor_scalar", {"out": pcol[:, 1:2]}, {"in0": pcol[:, 0:1]}, scalar1=32.0, scalar2=-32.0, op0=ALU.is_ge, op1=ALU.mult)
        cx.op("dve", "tensor_tensor", {"out": pcol[:, 0:1]}, {"in0": pcol[:, 0:1], "in1": pcol[:, 1:2]}, op=ALU.add)
        cx.op("act", "activation", {"out": pcol[:, 1:2]}, {"in_": pcol[:, 0:1]}, func=AF.Exp, scale=-math.log(10000.0) / 32.0)
        cx.op("pool", "iota", {"out": sinF}, {}, pattern=[[1, L]], base=pos0, channel_multiplier=0,
              allow_small_or_imprecise_dtypes=True)
        cx.op("dve", "tensor_scalar", {"out": sinF}, {"in0": sinF, "scalar1": pcol[:, 1:2]}, scalar2=None, op0=ALU.mult)
        cx.op("dve", "tensor_scalar", {"out": cosF}, {"in0": sinF}, scalar1=float(math.pi / 2), scalar2=None, op0=ALU.add)
        range_reduce("dve", sinF, fki, fkf)
        range_reduce("dve", cosF, fki, fkf)
        cx.op("act", "activation", {"out": sinF}, {"in_": sinF}, func=AF.Sin)
        cx.op("act", "activation", {"out": cosF}, {"in_": cosF}, func=AF.Sin)
        cx.barrier()
        ar.off = mark
        cqnT = ar.alloc([6, L], BF16, "cqnT")
        cx.dma(cqnT, S["cqnT"].rearrange("(k p) t -> p k t", p=128)[:, :, 0:L])
        wq = [ar.alloc([6, 768], BF16, f"wq{i}") for i in range(2)]
        wrot = [ar.alloc([6, 4, 64], BF16, f"wrot{i}") for i in range(2)]
        qst = [ar.alloc([L], BF16, f"qst{i}") for i in range(2)]
        qrst = [ar.alloc([L], BF16, f"qrst{i}") for i in range(2)]
        rt = [ar.alloc([2, TB], F32, f"rt{i}") for i in range(2)]
        bi_ = 0
        wquv = W["w_uq"].rearrange("(k p) n -> p k n", p=128)
        for hg in range(4):
            WQ, WR = wq[hg % 2], wrot[hg % 2]
            wload(WQ, "w_uq", wquv[:, :, hg * 768:(hg + 1) * 768])
            for hh in range(4):
                cx.op("pool", "tensor_scalar", {"out": WR[:, :, hh, 0:32]}, {"in0": WQ[:, :, hh * 192 + 160:hh * 192 + 192]},
                      scalar1=-1.0, scalar2=None, op0=ALU.mult)
                cx.op("pool", "tensor_copy", {"out": WR[:, :, hh, 32:64]}, {"in_": WQ[:, :, hh * 192 + 128:hh * 192 + 160]})
            for hh in range(4):
                h_ = hg * 4 + hh
                QS, QRS = qst[h_ % 2], qrst[h_ % 2]
                for tb in range(NTB):
                    t0 = tb * TB
                    bn = ps[bi_ % 2]
                    ba = ps[2 + (bi_ % 2) * 2]
                    bb = ps[3 + (bi_ % 2) * 2]
                    RT = rt[bi_ % 2]
                    bi_ += 1
                    for kc in range(6):
                        cx.mm(bn[:, 0:TB], WQ[:, kc, hh * 192:hh * 192 + 128], cqnT[:, kc, t0:t0 + TB], kc == 0, kc == 5)
                    cx.op("act", "activation", {"out": QS[:, t0:t0 + TB]}, {"in_": bn[:, 0:TB]}, func=AF.Copy)
                    for kc in range(6):
                        cx.mm(ba[0:64, 0:TB], WQ[:, kc, hh * 192 + 128:hh * 192 + 192], cqnT[:, kc, t0:t0 + TB], kc == 0, kc == 5)
                    for kc in range(6):
                        cx.mm(bb[0:64, 0:TB], WR[:, kc, hh, :], cqnT[:, kc, t0:t0 + TB], kc == 0, kc == 5)
                    cx.op("dve", "tensor_tensor", {"out": RT[0:64, 0, :]}, {"in0": ba[0:64, 0:TB], "in1": cosF[0:64, t0:t0 + TB]}, op=ALU.mult)
                    cx.op("dve", "tensor_tensor", {"out": RT[0:64, 1, :]}, {"in0": bb[0:64, 0:TB], "in1": sinF[0:64, t0:t0 + TB]}, op=ALU.mult)
                    cx.op("pool", "tensor_tensor", {"out": QRS[0:64, t0:t0 + TB]}, {"in0": RT[0:64, 0, :], "in1": RT[0:64, 1, :]}, op=ALU.add)
                cx.dma(S["QT"][h_ * 128:(h_ + 1) * 128, 0:L], QS, eng="pool")
                cx.dma(S["QrT"][h_ * 64:(h_ + 1) * 64, 0:L], QRS[0:64], eng="pool")
        cx.barrier()
        ar.reset()
        ckvT = ar.alloc([4, NK], BF16, "ckvT")
        cx.dma(ckvT, S["ckvT"].rearrange("(k p) t -> p k t", p=128)[:, :, 0:NK])
        wk = ar.alloc([4, D], BF16, "wk")
        wv = ar.alloc([4, D], BF16, "wv")
        wload(wk, "w_uk", W["w_uk"].rearrange("(k p) n -> p k n", p=128))
        wload(wv, "w_uv", W["w_uv"].rearrange("(k p) n -> p k n", p=128))
        kst = [ar.alloc([NK], BF16, f"kst{i}") for i in range(2)]
        vst = [ar.alloc([D], BF16, f"vst{i}") for i in range(2)]
        for h_ in range(NH):
            KS = kst[h_ % 2]
            for (k0, kn) in kblocks:
                bn = ps[bi_ % 4]
                bi_ += 1
                for kc in range(4):
                    cx.mm(bn[:, 0:kn], wk[:, kc, h_ * 128:(h_ + 1) * 128], ckvT[:, kc, k0:k0 + kn], kc == 0, kc == 3)
                cx.op("act" if (bi_ % 2) else "dve", "activation" if (bi_ % 2) else "tensor_copy",
                      {"out": KS[:, k0:k0 + kn]}, {"in_": bn[:, 0:kn]}, **({"func": AF.Copy} if (bi_ % 2) else {}))
            cx.dma(S["KT"][h_ * 128:(h_ + 1) * 128, 0:NK], KS, eng="pool")
        for ti, (k0, kn) in enumerate(ktiles):
            VS = vst[ti % 2]
            for cb in range(4):
                bn = ps[bi_ % 4]
                bi_ += 1
                for kc in range(4):
                    cx.mm(bn[0:kn, :], ckvT[:, kc, k0:k0 + kn], wv[:, kc, cb * 512:(cb + 1) * 512], kc == 0, kc == 3)
                if cb % 2:
                    cx.op("act", "activation", {"out": VS[0:kn, cb * 512:(cb + 1) * 512]}, {"in_": bn[0:kn, :]}, func=AF.Copy)
                else:
                    cx.op("dve", "tensor_copy", {"out": VS[0:kn, cb * 512:(cb + 1) * 512]}, {"in_": bn[0:kn, :]})
            cx.dma(S["V"][k0:k0 + kn, :], VS[0:kn], eng="pool")
        if debug == "Q":
            break

        cx.barrier()
        ar.reset()
        krT = ar.alloc([NK], BF16, "krT")
        cx.dma(krT[0:64], S["krT"][:, 0:NK])
        QTh = [ar.alloc([L], BF16, f"QTh{i}") for i in range(2)]
        QrTh = [ar.alloc([L], BF16, f"QrTh{i}") for i in range(2)]
        KTh = [ar.alloc([NK], BF16, f"KTh{i}") for i in range(2)]
        Vh = [ar.alloc([NKT, 128], BF16, f"Vh{i}") for i in range(2)]
        pts = [ar.alloc([512], BF16, f"pt{i}") for i in range(4)]
        rcp = [ar.alloc([512], F32, f"rcp{i}") for i in range(2)]
        ost = [ar.alloc([L], BF16, f"ost{i}") for i in range(2)]
        nfull = NK // 128

        def attn_loads(h_):
            sl = h_ % 2
            cx.dma(QTh[sl], S["QT"][h_ * 128:(h_ + 1) * 128, 0:L])
            cx.dma(QrTh[sl][0:64], S["QrT"][h_ * 64:(h_ + 1) * 64, 0:L])
            cx.dma(KTh[sl], S["KT"][h_ * 128:(h_ + 1) * 128, 0:NK])
            if nfull:
                cx.dma(Vh[sl][:, 0:nfull, :], S["V"][0:nfull * 128, h_ * 128:(h_ + 1) * 128].rearrange("(kt p) c -> p kt c", p=128))
            if NK % 128:
                rem = NK % 128
                cx.dma(Vh[sl][0:rem, nfull, :], S["V"][nfull * 128:NK, h_ * 128:(h_ + 1) * 128])

        attn_loads(0)
        attn_loads(1)
        jobs = []
        for h_ in range(NH):
            for qb in range(NTB):
                q0 = qb * TB
                if sq["cache"]:
                    tiles = [(k0, kn, False) for (k0, kn) in ktiles]
                else:
                    tiles = [(k0, kn, k0 >= q0) for (k0, kn) in ktiles if k0 < q0 + TB]
                for i, (k0, kn, diag) in enumerate(tiles):
                    jobs.append(dict(h=h_, qb=qb, q0=q0, i=i, n=len(tiles), k0=k0, kn=kn, diag=diag,
                                     qi=h_ * NTB + qb, first_of_head=(qb == 0 and i == 0),
                                     last_of_head=(qb == NTB - 1 and i == len(tiles) - 1)))
        LOOK = 2

        def s_stage(ji, jb):
            sl = jb["h"] % 2
            q0, k0, kn = jb["q0"], jb["k0"], jb["kn"]
            c0 = (k0 - q0) if jb["diag"] else 0
            SB = ps[ji % 4]
            PT = pts[ji % 4]
            cx.mm(SB[0:kn, c0:TB], KTh[sl][:, k0:k0 + kn], QTh[sl][:, q0 + c0:q0 + TB], True, False)
            cx.mm(SB[0:kn, c0:TB], krT[0:64, k0:k0 + kn], QrTh[sl][0:64, q0 + c0:q0 + TB], False, True)
            cx.op("act", "activation", {"out": PT[0:kn, c0:TB]}, {"in_": SB[0:kn, c0:TB]}, func=AF.Exp, scale=SCALE)
            if jb["diag"]:
                cx.op("dve", "memset", {"ap": PT[64:128, c0:c0 + 64]}, {}, constant=0.0)

        def pv_stage(ji, jb):
            sl = jb["h"] % 2
            q0, k0, kn = jb["q0"], jb["k0"], jb["kn"]
            c0 = (k0 - q0) if jb["diag"] else 0
            PT = pts[ji % 4]
            OB = ps[4 + 2 * (jb["qi"] % 2)]
            SM = ps[5 + 2 * (jb["qi"] % 2)]
            RC = rcp[jb["qi"] % 2]
            last = (jb["i"] == jb["n"] - 1)
            cx.mm(OB[:, c0:TB], Vh[sl][0:kn, k0 // 128, :], PT[0:kn, c0:TB], jb["i"] == 0, last)
            cx.mm(SM[:, c0:TB], ones[0:kn, :], PT[0:kn, c0:TB], jb["i"] == 0, last)
            if last:
                cx.op("dve", "reciprocal", {"out": RC[:, 0:TB]}, {"in_": SM[:, 0:TB]})
                cx.op("dve", "tensor_tensor", {"out": ost[sl][:, q0:q0 + TB]}, {"in0": OB[:, 0:TB], "in1": RC[:, 0:TB]}, op=ALU.mult)
            if jb["last_of_head"]:
                cx.dma(S["attnT"][jb["h"] * 128:(jb["h"] + 1) * 128, 0:L], ost[sl], eng="pool")
                if deferred_casts and len(deferred_casts) == 1 and deferred_casts[0]:
                    cast_weight(deferred_casts[0].pop(0))
                if jb["h"] + 2 < NH:
                    attn_loads(jb["h"] + 2)

        for ji in range(len(jobs) + LOOK):
            if ji < len(jobs):
                s_stage(ji, jobs[ji])
            if ji >= LOOK:
                pv_stage(ji - LOOK, jobs[ji - LOOK])
        if debug == "T":
            break

        cx.barrier()
        ar.reset()
        TBs = TB
        NA = TBs // 16
        v2 = lambda a: a.rearrange("(ct gl) p -> (gl p) ct", gl=2)
        are = ar.alloc([64], F32, "are")
        aim = ar.alloc([64], F32, "aim")
        dtt = ar.alloc([64], F32, "dtt")
        cx.dma(are, v2(I["s5_a_re"]), noncontig=True)
        cx.dma(aim, v2(I["s5_a_im"]), noncontig=True)
        cx.dma(dtt, v2(I["s5_log_dt"]), noncontig=True)
        dcy = ar.alloc([64], F32, "dcy")
        thr = ar.alloc([64], F32, "thr")
        sth = ar.alloc([64], F32, "sth")
        cth = ar.alloc([64], F32, "cth")
        tki = ar.alloc([64], I32, "tki")
        tkf = ar.alloc([64], F32, "tkf")
        t64 = [ar.alloc([64], F32, f"t64_{i}") for i in range(6)]
        cx.op("act", "activation", {"out": dtt}, {"in_": dtt}, func=AF.Exp)
        cx.op("dve", "tensor_tensor", {"out": dcy}, {"in0": are, "in1": dtt}, op=ALU.mult)
        cx.op("act", "activation", {"out": dcy}, {"in_": dcy}, func=AF.Exp)
        cx.op("dve", "tensor_tensor", {"out": thr}, {"in0": aim, "in1": dtt}, op=ALU.mult)
        cx.op("dve", "tensor_scalar", {"out": cth}, {"in0": thr}, scalar1=float(math.pi / 2), scalar2=None, op0=ALU.add)
        range_reduce("dve", thr, tki, tkf)
        range_reduce("dve", cth, tki, tkf)
        cx.op("act", "activation", {"out": sth}, {"in_": thr}, func=AF.Sin)
        cx.op("act", "activation", {"out": cth}, {"in_": cth}, func=AF.Sin)
        abr, abi, den, nr_, cre, cim = t64
        cx.op("dve", "tensor_tensor", {"out": abr}, {"in0": dcy, "in1": cth}, op=ALU.mult)
        cx.op("dve", "tensor_tensor", {"out": abi}, {"in0": dcy, "in1": sth}, op=ALU.mult)
        cx.op("dve", "tensor_tensor", {"out": den}, {"in0": are, "in1": are}, op=ALU.mult)
        cx.op("dve", "tensor_tensor", {"out": nr_}, {"in0": aim, "in1": aim}, op=ALU.mult)
        cx.op("dve", "tensor_tensor", {"out": den}, {"in0": den, "in1": nr_}, op=ALU.add)
        cx.op("dve", "reciprocal", {"out": den}, {"in_": den})
        cx.op("dve", "tensor_scalar", {"out": nr_}, {"in0": abr}, scalar1=-1.0, scalar2=None, op0=ALU.add)
        cx.op("dve", "tensor_tensor", {"out": cre}, {"in0": nr_, "in1": are}, op=ALU.mult)
        cx.op("dve", "tensor_tensor", {"out": cim}, {"in0": abi, "in1": aim}, op=ALU.mult)
        cx.op("dve", "tensor_tensor", {"out": cre}, {"in0": cre, "in1": cim}, op=ALU.add)
        cx.op("dve", "tensor_tensor", {"out": cre}, {"in0": cre, "in1": den}, op=ALU.mult)
        cx.op("dve", "tensor_tensor", {"out": cim}, {"in0": abi, "in1": are}, op=ALU.mult)
        cx.op("dve", "tensor_tensor", {"out": abr}, {"in0": nr_, "in1": aim}, op=ALU.mult)
        cx.op("dve", "tensor_tensor", {"out": cim}, {"in0": cim, "in1": abr}, op=ALU.subtract)
        cx.op("dve", "tensor_tensor", {"out": cim}, {"in0": cim, "in1": den}, op=ALU.mult)
        v3 = lambda a: a.rearrange("(ct gl) p m -> (gl p) ct m", gl=2)
        BTr = ar.alloc([64, 128], BF16, "BTr")
        BTi = ar.alloc([64, 128], BF16, "BTi")
        CTr = ar.alloc([64, 128], BF16, "CTr")
        CTi = ar.alloc([64, 128], BF16, "CTi")
        mark = ar.off
        bre = ar.alloc([64, 16], F32, "bre")
        bim = ar.alloc([64, 16], F32, "bim")
        bt1 = ar.alloc([64, 16], F32, "bt1")
        bt2 = ar.alloc([64, 16], F32, "bt2")
        cx.dma(bre, v3(I["s5_b_re"]))
        cx.dma(bim, v3(I["s5_b_im"]))
        bcast = lambda b: Buf(b.ap.unsqueeze(2).to_broadcast([128, 64, 16]), b.key)
        xpr = ar.alloc([64, 128], BF16, "xpr")
        xpi = ar.alloc([64, 128], BF16, "xpi")
        cx.op("pool", "memset", {"ap": xpr}, {}, constant=0.0)
        cx.op("pool", "memset", {"ap": xpi}, {}, constant=0.0)
        cx.op("dve", "tensor_tensor", {"out": bt1}, {"in0": bre, "in1": bcast(cre)}, op=ALU.mult)
        cx.op("dve", "tensor_tensor", {"out": bt2}, {"in0": bim, "in1": bcast(cim)}, op=ALU.mult)
        cx.op("dve", "tensor_tensor", {"out": bt1}, {"in0": bt1, "in1": bt2}, op=ALU.subtract)
        cx.op("dve", "tensor_tensor", {"out": bt2}, {"in0": bim, "in1": bcast(cre)}, op=ALU.mult)
        cx.op("dve", "tensor_tensor", {"out": bim}, {"in0": bre, "in1": bcast(cim)}, op=ALU.mult)
        cx.op("dve", "tensor_tensor", {"out": bt2}, {"in0": bt2, "in1": bim}, op=ALU.add)
        for (src, dst) in ((bt1, xpr), (bt2, xpi)):
            s4 = src.rearrange("p (c j) m -> p c j m", j=4)
            d4 = dst.rearrange("p (c j) n -> p c j n", j=4)
            for j in range(4):
                for gl in range(2):
                    cx.op("dve", "tensor_copy", {"out": d4[gl * 64:(gl + 1) * 64, :, j, 32 * j + 16 * gl:32 * j + 16 * gl + 16]},
                          {"in_": s4[gl * 64:(gl + 1) * 64, :, j, :]})
        pb = [ps[0].bitcast(BF16), ps[1].bitcast(BF16)]
        gi_ = 0
        for (src, dst) in ((xpr, BTr), (xpi, BTi)):
            for c8 in range(8):
                P_ = pb[gi_ % 2]
                gi_ += 1
                for i in range(8):
                    cx.tr(P_[:, i * 128:(i + 1) * 128], src[:, c8 * 8 + i, :], ident)
                cx.op("dve", "tensor_copy", {"out": dst[:, c8 * 8:(c8 + 1) * 8, :]}, {"in_": P_.rearrange("p (a b) -> p a b", b=128)})
        cx.barrier()
        ar.off = mark
        ypr = ar.alloc([64, 128], F32, "ypr")
        ypi = ar.alloc([64, 128], F32, "ypi")
        cx.op("pool", "memset", {"ap": ypr[0:32]}, {}, constant=0.0)
        cx.op("pool", "memset", {"ap": ypi[0:32]}, {}, constant=0.0)
        for (srcd, dst) in ((I["s5_c_re"], ypr), (I["s5_c_im"], ypi)):
            c4 = srcd.rearrange("(ct gl) m p -> gl m ct p", gl=2)
            for gl in range(2):
                cx.dma(dst[gl * 16:(gl + 1) * 16, :, gl * 64:(gl + 1) * 64], c4[gl])
        ybr = ar.alloc([64, 128], BF16, "ybr")
        ybi = ar.alloc([64, 128], BF16, "ybi")
        cx.op("act", "activation", {"out": ybr[0:32]}, {"in_": ypr[0:32]}, func=AF.Copy)
        cx.op("act", "activation", {"out": ybi[0:32]}, {"in_": ypi[0:32]}, func=AF.Copy, scale=-1.0)
        cx.op("pool", "memset", {"ap": CTr}, {}, constant=0.0)
        cx.op("pool", "memset", {"ap": CTi}, {}, constant=0.0)
        for (src, dst) in ((ybr, CTr), (ybi, CTi)):
            d4 = dst.rearrange("p (c j) n -> p c j n", j=4)
            for c16 in range(4):
                P_ = pb[gi_ % 2]
                gi_ += 1
                for i in range(16):
                    cx.tr(P_[:, i * 32:(i + 1) * 32], src[0:32, c16 * 16 + i, :], ident[0:32, 0:32])
                p4 = P_[:, 0:512].rearrange("p (c j n) -> p c j n", j=4, n=32)
                for j in range(4):
                    cx.op("dve", "tensor_copy", {"out": d4[:, c16 * 4:(c16 + 1) * 4, j, 32 * j:32 * j + 32]}, {"in_": p4[:, :, j, :]})
        cx.barrier()
        ar.off = mark
        dT = ar.alloc([KC], F32, "dT")
        cx.dma(dT, I["s5_d"].rearrange("(k p) -> p k", p=128), noncontig=True)
        NAB = NA + 16
        mult = ar.alloc([NAB], F32, "mult")
        cx.op("pool", "iota", {"out": mult[:, 0:NA]}, {}, pattern=[[16, NA]], base=0, channel_multiplier=0, allow_small_or_imprecise_dtypes=True)
        cx.op("pool", "iota", {"out": mult[:, NA:NAB]}, {}, pattern=[[1, 16]], base=1, channel_multiplier=0, allow_small_or_imprecise_dtypes=True)
        sAB = ar.alloc([64, NAB], F32, "sAB")
        cAB = ar.alloc([64, NAB], F32, "cAB")
        car_r = ar.alloc([64], F32, "car_r")
        car_i = ar.alloc([64], F32, "car_i")
        mark2 = ar.off
        aki = ar.alloc([64, NAB], I32, "aki")
        akf = ar.alloc([64, NAB], F32, "akf")
        cx.op("dve", "tensor_tensor", {"out": sAB}, {"in0": Buf(thr.ap.unsqueeze(2).to_broadcast([128, 64, NAB]), thr.key),
                                                       "in1": Buf(mult.ap.unsqueeze(1).to_broadcast([128, 64, NAB]), mult.key)}, op=ALU.mult)
        cx.op("dve", "tensor_scalar", {"out": cAB}, {"in0": sAB}, scalar1=float(math.pi / 2), scalar2=None, op0=ALU.add)
        range_reduce("dve", sAB, aki, akf)
        range_reduce("dve", cAB, aki, akf)
        cx.op("act", "activation", {"out": sAB}, {"in_": sAB}, func=AF.Sin)
        cx.op("act", "activation", {"out": cAB}, {"in_": cAB}, func=AF.Sin)
        cx.barrier()
        ar.off = mark2
        if sq["cache"]:
            cx.dma(car_r, v2(I["state_re"]), noncontig=True)
            cx.dma(car_i, v2(I["state_im"]), noncontig=True)
        else:
            cx.op("pool", "memset", {"ap": car_r}, {}, constant=0.0)
            cx.op("pool", "memset", {"ap": car_i}, {}, constant=0.0)
        uch = [ar.alloc([L], BF16, f"uch{i}") for i in range(2)]
        tabc4 = ar.alloc([4, NA, 16], F32, "tabc4")
        tabs4 = ar.alloc([4, NA, 16], F32, "tabs4")
        tw = [ar.alloc([NA, 16], F32, f"tw{j}") for j in range(2)]
        def alloc4(shape, dt, name):
            big = ar.alloc([4] + shape, dt, name)
            parts = [Buf(big.ap[:, j], f"{big.key}/{j}") for j in range(4)]
            allb = Buf(big.ap, [p.key for p in parts])
            return parts, allb

        A1, A1a = alloc4([TBs], F32, "A1")
        A2, A2a = alloc4([TBs], F32, "A2")
        A3, A3a = alloc4([TBs], F32, "A3")
        A4, A4a = alloc4([TBs], F32, "A4")
        hrb, hrba = alloc4([TBs], BF16, "hrb")
        hib, hiba = alloc4([TBs], BF16, "hib")
        yv = [ar.alloc([TBs], F32, f"yv{i}") for i in range(2)]
        y2 = [ar.alloc([TBs], F32, f"y2{i}") for i in range(2)]
        gst = [ar.alloc([L], BF16, f"gst{i}") for i in range(2)]
        yi_ = 0
        for ch in range(KC):
            U = uch[ch % 2]
            cx.dma(U, S["uT"][ch * 128:(ch + 1) * 128, 0:L])
            for j in range(4):
                ct = ch * 4 + j
                cA = Buf(cAB.ap[:, ct, 0:NA].unsqueeze(2).to_broadcast([128, NA, 16]), cAB.key)
                sA = Buf(sAB.ap[:, ct, 0:NA].unsqueeze(2).to_broadcast([128, NA, 16]), sAB.key)
                cB = Buf(cAB.ap[:, ct, NA:NAB].unsqueeze(1).to_broadcast([128, NA, 16]), cAB.key)
                sB = Buf(sAB.ap[:, ct, NA:NAB].unsqueeze(1).to_broadcast([128, NA, 16]), sAB.key)
                TC = tabc4[:, j]
                TS2 = tabs4[:, j]
                cx.op("pool", "tensor_tensor", {"out": TC}, {"in0": cA, "in1": cB}, op=ALU.mult)
                cx.op("pool", "tensor_tensor", {"out": tw[0]}, {"in0": sA, "in1": sB}, op=ALU.mult)
                cx.op("pool", "tensor_tensor", {"out": TC}, {"in0": TC, "in1": tw[0]}, op=ALU.subtract)
                cx.op("pool", "tensor_tensor", {"out": TS2}, {"in0": sA, "in1": cB}, op=ALU.mult)
                cx.op("pool", "tensor_tensor", {"out": tw[1]}, {"in0": cA, "in1": sB}, op=ALU.mult)
                cx.op("pool", "tensor_tensor", {"out": TS2}, {"in0": TS2, "in1": tw[1]}, op=ALU.add)
            GS = gst[ch % 2]
            Call = tabc4.rearrange("p j a b -> p j (a b)")
            Sall = tabs4.rearrange("p j a b -> p j (a b)")
            for tb in range(NTB):
                t0 = tb * TB
                YB = ps[6 + (yi_ % 2)]
                for j in range(4):
                    ct = ch * 4 + j
                    Pr = ps[(j % 3) * 2]
                    Pi = ps[(j % 3) * 2 + 1]
                    cx.mm(Pr[:, 0:TB], BTr[:, ct, :], U[:, t0:t0 + TB], True, True)
                    cx.mm(Pi[:, 0:TB], BTi[:, ct, :], U[:, t0:t0 + TB], True, True)
                    cx.op("act", "activation", {"out": A1[j]}, {"in_": Pr[:, 0:TB]}, func=AF.Copy)
                    cx.op("act", "activation", {"out": A2[j]}, {"in_": Pi[:, 0:TB]}, func=AF.Copy)
                cx.op("dve", "tensor_tensor", {"out": A3a}, {"in0": A1a, "in1": Sall}, op=ALU.mult)
                cx.op("dve", "tensor_tensor", {"out": A4a}, {"in0": A2a, "in1": Call}, op=ALU.mult)
                cx.op("dve", "tensor_tensor", {"out": A1a}, {"in0": A1a, "in1": Call}, op=ALU.mult)
                cx.op("dve", "tensor_tensor", {"out": A2a}, {"in0": A2a, "in1": Sall}, op=ALU.mult)
                cx.op("dve", "tensor_tensor", {"out": A1a}, {"in0": A1a, "in1": A2a}, op=ALU.add)
                cx.op("dve", "tensor_tensor", {"out": A4a}, {"in0": A4a, "in1": A3a}, op=ALU.subtract)
                for j in range(4):
                    ct = ch * 4 + j
                    dk = Buf(dcy.ap[:, ct:ct + 1].to_broadcast([128, TB]), dcy.key)
                    cx.op("dve", "tensor_tensor_scan", {"out": A2[j]}, {"data0": dk, "data1": A1[j], "initial": car_r[:, ct:ct + 1]}, op0=ALU.mult, op1=ALU.add)
                    cx.op("dve", "tensor_tensor_scan", {"out": A3[j]}, {"data0": dk, "data1": A4[j], "initial": car_i[:, ct:ct + 1]}, op0=ALU.mult, op1=ALU.add)
                cx.op("dve", "tensor_tensor", {"out": A1a}, {"in0": A2a, "in1": Call}, op=ALU.mult)
                cx.op("dve", "tensor_tensor", {"out": A4a}, {"in0": A3a, "in1": Sall}, op=ALU.mult)
                cx.op("dve", "tensor_tensor", {"out": A1a}, {"in0": A1a, "in1": A4a}, op=ALU.subtract)
                cx.op("act", "activation", {"out": hrba}, {"in_": A1a}, func=AF.Copy)
                cx.op("act", "activation", {"out": car_r[:, ch * 4:(ch + 1) * 4]}, {"in_": A1a[:, :, TB - 1]}, func=AF.Copy)
                cx.op("dve", "tensor_tensor", {"out": A4a}, {"in0": A2a, "in1": Sall}, op=ALU.mult)
                cx.op("dve", "tensor_tensor", {"out": A2a}, {"in0": A3a, "in1": Call}, op=ALU.mult)
                cx.op("dve", "tensor_tensor", {"out": A4a}, {"in0": A4a, "in1": A2a}, op=ALU.add)
                cx.op("act", "activation", {"out": hiba}, {"in_": A4a}, func=AF.Copy)
                cx.op("act", "activation", {"out": car_i[:, ch * 4:(ch + 1) * 4]}, {"in_": A4a[:, :, TB - 1]}, func=AF.Copy)
                for j in range(4):
                    ct = ch * 4 + j
                    cx.mm(YB[:, 0:TB], CTr[:, ct, :], hrb[j], j == 0, False)
                    cx.mm(YB[:, 0:TB], CTi[:, ct, :], hib[j], False, j == 3)
                Y, Y2 = yv[yi_ % 2], y2[yi_ % 2]
                yi_ += 1
                cx.op("dve", "scalar_tensor_tensor", {"out": Y}, {"in0": U[:, t0:t0 + TB], "scalar": dT[:, ch:ch + 1], "in1": YB[:, 0:TB]}, op0=ALU.mult, op1=ALU.add)
                cx.op("act", "activation", {"out": Y2}, {"in_": Y}, func=AF.Square)
                cx.op("dve", "tensor_scalar", {"out": Y2}, {"in0": Y2}, scalar1=0.044715, scalar2=1.0, op0=ALU.mult, op1=ALU.add)
                cx.op("pool", "tensor_tensor", {"out": Y2}, {"in0": Y2, "in1": Y}, op=ALU.mult)
                cx.op("act", "activation", {"out": Y2}, {"in_": Y2}, func=AF.Sigmoid, scale=2.0 * GELU_C)
                cx.op("pool", "tensor_tensor", {"out": GS[:, t0:t0 + TB]}, {"in0": Y2, "in1": Y}, op=ALU.mult)
            cx.dma(S["gT"][ch * 128:(ch + 1) * 128, 0:L], GS, eng="pool")
        cx.dma(v2(sq["sre"]), car_r, eng="pool", noncontig=True)
        cx.dma(v2(sq["sim"]), car_i, eng="pool", noncontig=True)
        if debug == "S":
            break

        cx.barrier()
        ar.reset()
        HL2 = min(2048, L)
        NH2 = L // HL2
        gTh = ar.alloc([KC, HL2], BF16, "gTh")
        wa = [ar.alloc([KC, 512], BF16, f"wa{i}") for i in range(2)]
        wb_ = [ar.alloc([KC, 512], BF16, f"wb{i}") for i in range(2)]
        gat = [ar.alloc([TB], BF16, f"gat{i}") for i in range(3)]
        gbt = [ar.alloc([TB], BF16, f"gbt{i}") for i in range(3)]
        att = [ar.alloc([TB], BF16, f"att{i}") for i in range(3)]
        sgt = [ar.alloc([TB], F32, f"sgt{i}") for i in range(2)]
        s5t = [ar.alloc([TB], BF16, f"s5t{i}") for i in range(2)]
        m1t = [ar.alloc([TB], BF16, f"m1t{i}") for i in range(2)]
        mst = [ar.alloc([HL2], BF16, f"mst{i}") for i in range(2)]
        wav = W["w_glu_a"].rearrange("(k p) n -> p k n", p=128)
        wbv = W["w_glu_b"].rearrange("(k p) n -> p k n", p=128)
        li = 0
        bi_ = 0
        for hf in range(NH2):
            c0 = hf * HL2
            cx.dma(gTh, S["gT"].rearrange("(k p) t -> p k t", p=128)[:, :, c0:c0 + HL2])
            for sb in range(4):
                WA, WB = wa[sb % 2], wb_[sb % 2]
                wload(WA, "w_glu_a", wav[:, :, sb * 512:(sb + 1) * 512])
                wload(WB, "w_glu_b", wbv[:, :, sb * 512:(sb + 1) * 512])
                for j in range(4):
                    cb = sb * 4 + j
                    MS = mst[cb % 2]
                    for tb in range(HL2 // TB):
                        t0 = tb * TB
                        g0 = c0 + t0
                        k3 = li % 3
                        k2 = li % 2
                        li += 1
                        cx.dma(gat[k3], S["gaT"][cb * 128:(cb + 1) * 128, g0:g0 + TB])
                        cx.dma(gbt[k3], S["gbT"][cb * 128:(cb + 1) * 128, g0:g0 + TB])
                        cx.dma(att[k3], S["attnT"][cb * 128:(cb + 1) * 128, g0:g0 + TB])
                        BA = ps[(bi_ % 4) * 2]
                        BB = ps[(bi_ % 4) * 2 + 1]
                        bi_ += 1
                        for kc in range(KC):
                            cx.mm(BA[:, 0:TB], WA[:, kc, j * 128:(j + 1) * 128], gTh[:, kc, t0:t0 + TB], kc == 0, kc == KC - 1)
                        for kc in range(KC):
                            cx.mm(BB[:, 0:TB], WB[:, kc, j * 128:(j + 1) * 128], gTh[:, kc, t0:t0 + TB], kc == 0, kc == KC - 1)
                        cx.op("act", "activation", {"out": sgt[k2]}, {"in_": BB[:, 0:TB]}, func=AF.Sigmoid)
                        cx.op("dve", "tensor_tensor", {"out": s5t[k2]}, {"in0": BA[:, 0:TB], "in1": sgt[k2]}, op=ALU.mult)
                        cx.op("pool", "tensor_tensor", {"out": m1t[k2]}, {"in0": gat[k3], "in1": att[k3]}, op=ALU.mult)
                        cx.op("pool", "tensor_tensor", {"out": s5t[k2]}, {"in0": s5t[k2], "in1": gbt[k3]}, op=ALU.mult)
                        cx.op("pool", "tensor_tensor", {"out": MS[:, t0:t0 + TB]}, {"in0": m1t[k2], "in1": s5t[k2]}, op=ALU.add)
                    cx.dma(S["mT"][cb * 128:(cb + 1) * 128, c0:c0 + HL2], MS, eng="pool")
        if debug == "G":
            break

        cx.barrier()
        ar.reset()
        TS_ = min(512, L)
        NT4 = TS_ // TT
        g2T = ar.alloc([KC], F32, "g2T")
        cx.dma(g2T, I["norm_mlp"].rearrange("(k p) -> p k", p=128), noncontig=True)
        gfin = ar.alloc([D], F32, "gfin")
        cx.dma(gfin, I["norm_final"].partition_broadcast(128))
        mTs = ar.alloc([KC, TS_], BF16, "mTs")
        hbuf = [ar.alloc([D], F32, f"hbuf{i}") for i in range(NT4)]
        hnb = [ar.alloc([D], BF16, f"hnb{i}") for i in range(2)]
        junk2 = ar.alloc([D], BF16, "junk2")
        hnT = mTs
        hid = ar.alloc([64, TS_], BF16, "hid")
        wo_ = [ar.alloc([4, 512], BF16, f"wo{i}") for i in range(3)]
        wu_ = [ar.alloc([KC, 256], BF16, f"wu{i}") for i in range(2)]
        wd_ = [ar.alloc([8, 512], BF16, f"wd{i}") for i in range(2)]
        rl = [ar.alloc([TS_], F32, f"rl{i}") for i in range(2)]
        st2 = ar.alloc([NT4, 8], F32, "st2")
        wov = W["w_o"].rearrange("(k p) n -> p k n", p=128)
        wuv = W["w_up"].rearrange("(k p) n -> p k n", p=128)
        wdv = W["w_down"].rearrange("(k p) n -> p k n", p=128)
        pT2 = [ps[4].bitcast(BF16), ps[5].bitcast(BF16)]
        wi_o = 0
        wi_d = 0
        bi_ = 0
        for st_ in range(L // TS_):
            s0 = st_ * TS_
            cx.dma(mTs, S["mT"].rearrange("(k p) t -> p k t", p=128)[:, :, s0:s0 + TS_])
            for t_ in range(NT4):
                cx.dma(hbuf[t_][0:TT], sq["x"][s0 + t_ * TT:s0 + (t_ + 1) * TT, :])
            for cb in range(4):
                for kg in range(4):
                    WO = wo_[wi_o % 3]
                    wi_o += 1
                    wload(WO, "w_o", wov[:, kg * 4:(kg + 1) * 4, cb * 512:(cb + 1) * 512])
                    for t_ in range(NT4):
                        for i in range(4):
                            kc = kg * 4 + i
                            cx.mm(ps[t_][0:TT, :], mTs[:, kc, t_ * TT:(t_ + 1) * TT], WO[:, i, :], kc == 0, kc == KC - 1)
                for t_ in range(NT4):
                    cx.op("dve", "tensor_tensor", {"out": hbuf[t_][0:TT, cb * 512:(cb + 1) * 512]},
                          {"in0": ps[t_][0:TT, :], "in1": hbuf[t_][0:TT, cb * 512:(cb + 1) * 512]}, op=ALU.add)
            for t_ in range(NT4):
                HB = hnb[t_ % 2]
                cx.op("act", "activation", {"out": junk2[0:TT], "accum_out": st2[0:TT, t_, 0:1]}, {"in_": hbuf[t_][0:TT]}, func=AF.Square)
                rstd_from_ss(st2[0:TT, t_, 0:1], D, st2[0:TT, t_, 1:2], st2[0:TT, t_, 2:3])
                cx.op("act", "activation", {"out": HB[0:TT]}, {"in_": hbuf[t_][0:TT], "scale": st2[0:TT, t_, 2:3]}, func=AF.Copy)
                for kc in range(KC):
                    cx.tr(pT2[kc // 8][:, (kc % 8) * 128:(kc % 8) * 128 + TT], HB[0:TT, kc * 128:(kc + 1) * 128], ident[0:TT, 0:TT])
                for hb in range(2):
                    cx.op("dve", "tensor_tensor",
                          {"out": hnT[:, hb * 8:(hb + 1) * 8, t_ * TT:(t_ + 1) * TT]},
                          {"in0": pT2[hb].rearrange("p (a b) -> p a b", b=128)[:, :, 0:TT],
                           "in1": Buf(g2T.ap[:, hb * 8:(hb + 1) * 8].unsqueeze(2).to_broadcast([128, 8, TT]), g2T.key)},
                          op=ALU.mult)
            for sb in range(32):
                WU = wu_[sb % 2]
                wload(WU, "w_up", wuv[:, :, sb * 256:(sb + 1) * 256])
                for j in range(2):
                    fb = sb * 2 + j
                    bn = ps[bi_ % 4]
                    RL = rl[bi_ % 2]
                    bi_ += 1
                    for kc in range(KC):
                        cx.mm(bn[:, 0:TS_], WU[:, kc, j * 128:(j + 1) * 128], hnT[:, kc, :], kc == 0, kc == KC - 1)
                    cx.op("act", "activation", {"out": RL}, {"in_": bn[:, 0:TS_]}, func=AF.Relu)
                    cx.op("pool" if (fb % 2) else "dve", "tensor_tensor", {"out": hid[:, fb, :]}, {"in0": RL, "in1": RL}, op=ALU.mult)
            for cb in range(4):
                for kg in range(8):
                    WD = wd_[wi_d % 2]
                    wi_d += 1
                    wload(WD, "w_down", wdv[:, kg * 8:(kg + 1) * 8, cb * 512:(cb + 1) * 512])
                    for t_ in range(NT4):
                        for i in range(8):
                            fk = kg * 8 + i
                            cx.mm(ps[t_][0:TT, :], hid[:, fk, t_ * TT:(t_ + 1) * TT], WD[:, i, :], fk == 0, fk == 63)
                for t_ in range(NT4):
                    cx.op("dve", "tensor_tensor", {"out": hbuf[t_][0:TT, cb * 512:(cb + 1) * 512]},
                          {"in0": ps[t_][0:TT, :], "in1": hbuf[t_][0:TT, cb * 512:(cb + 1) * 512]}, op=ALU.add)
            for t_ in range(NT4):
                cx.op("act", "activation", {"out": junk2[0:TT], "accum_out": st2[0:TT, t_, 3:4]}, {"in_": hbuf[t_][0:TT]}, func=AF.Square)
                rstd_from_ss(st2[0:TT, t_, 3:4], D, st2[0:TT, t_, 4:5], st2[0:TT, t_, 5:6])
                cx.op("dve", "scalar_tensor_tensor", {"out": hbuf[t_][0:TT]}, {"in0": hbuf[t_][0:TT], "scalar": st2[0:TT, t_, 5:6], "in1": gfin[0:TT]},
                      op0=ALU.mult, op1=ALU.mult)
                cx.dma(sq["y"][s0 + t_ * TT:s0 + (t_ + 1) * TT, :], hbuf[t_][0:TT], eng="pool")
    with nc.Block() as block:
        sems = cx.emit(block)
    for cm in reversed(sems):
        cm.__exit__(None, None, None)
    for cm in reversed(stack):
        cm.__exit__(None, None, None)
    return nc


LP_FULL, LS_FULL, NPAST_FULL = 4096, 32, 2048
_NC_CACHE = {}


def make_in_maps(inputs, LP, LS, NPAST, ncores):
    maps = []
    f = lambda a: np.ascontiguousarray(np.asarray(a, dtype=np.float32))
    for b in range(ncores):
        m = {
            "x_prompt": f(inputs["x_prompt"][b, :LP]),
            "x_sample": f(inputs["x_sample"][b, :LS]),
            "cache_ckv": f(inputs["cache_ckv"][0, b, :NPAST]),
            "cache_krope": f(inputs["cache_krope"][0, b, :NPAST]),
            "state_re": f(inputs["state_s5_re"][0, b]),
            "state_im": f(inputs["state_s5_im"][0, b]),
        }
        for nm in ["norm_mix", "w_in", "norm_q", "w_uq", "norm_kv", "s5_a_re", "s5_a_im", "s5_log_dt", "s5_b_re",
                   "s5_b_im", "s5_c_re", "s5_c_im", "s5_d", "w_glu_a", "w_glu_b", "w_o", "norm_mlp", "w_up", "w_down"]:
            m[nm] = f(inputs[nm][0])
        m["w_uk"] = f(np.asarray(inputs["w_uk"][0]).reshape(KVL, D))
        m["w_uv"] = f(np.asarray(inputs["w_uv"][0]).reshape(KVL, D))
        m["norm_final"] = f(inputs["norm_final"])
        maps.append(m)
    return maps


def kernel(**inputs):
    n = 8
    key = (LP_FULL, LS_FULL, NPAST_FULL)
    if key not in _NC_CACHE:
        _NC_CACHE[key] = build_program(*key)
    nc = _NC_CACHE[key]
    maps = make_in_maps(inputs, LP_FULL, LS_FULL, NPAST_FULL, n)
    res = run_bass_kernel_spmd(nc, maps, core_ids=list(range(n)))
    R = res.results
    st = lambda k: np.stack([np.asarray(R[b][k], dtype=np.float32) for b in range(n)], axis=0)
    return (st("y_p"), st("y_s"), st("ckv_p")[None], st("kr_p")[None], st("sre_p")[None], st("sim_p")[None],
            st("ckv_s")[None], st("kr_s")[None], st("sre_s")[None], st("sim_s")[None])
```

```python
import math
import numpy as np
import concourse.bass as bass
import concourse.mybir as mybir
from concourse.bass_utils import run_bass_kernel_spmd

F32 = mybir.dt.float32
BF16 = mybir.dt.bfloat16
I32 = mybir.dt.int32
AF = mybir.ActivationFunctionType
ALU = mybir.AluOpType
AX = mybir.AxisListType

D = 2048
KC = D // 128
QL = 768
KVL = 512
ROPE = 64
NH = 16
DFF = 8192
EPS = 1e-6
IN_COLS = 7488
SCALE = 1.0 / math.sqrt(192.0)
TWO_PI = 2.0 * math.pi
GELU_C = math.sqrt(2.0 / math.pi)

ENGS = ["sp", "act", "pool", "pe", "dve"]
ND_SEM = {"sp": 12, "pool": 44}


class Buf:
    def __init__(self, ap, key):
        self.ap = ap
        self.key = key

    def __getitem__(self, idx):
        return Buf(self.ap[idx], self.key)

    def bitcast(self, dt):
        return Buf(self.ap.bitcast(dt), self.key)

    def rearrange(self, s, **kw):
        return Buf(self.ap.rearrange(s, **kw), self.key)

    def bc(self, shape):
        return Buf(self.ap.to_broadcast(shape), self.key)

    def rekey(self, key):
        return Buf(self.ap, key)


def _ap(x):
    return x.ap if isinstance(x, Buf) else x


class Op:
    __slots__ = ("eng", "fn", "deps", "is_dma", "sig", "phase", "pe_like", "idx")


class Ctx:
    def __init__(self, nc):
        self.nc = nc
        self.ops = []
        self.last_write = {}
        self.readers = {}
        self.phase = 0
        self.last_on_eng = {}
        self.last_dma_ops = []
        self.pending_barrier = {}

    def add(self, eng, fn, reads=(), writes=(), is_dma=False, extra_deps=(), pe_like=False):
        op = Op()
        op.eng = eng
        op.fn = fn
        op.is_dma = is_dma
        op.phase = self.phase
        op.pe_like = pe_like
        op.sig = None
        op.idx = len(self.ops)
        deps = set(extra_deps)
        for k in reads:
            if k is None:
                continue
            if k in self.last_write:
                deps.add(self.last_write[k])
        for k in writes:
            if k is None:
                continue
            if k in self.last_write:
                deps.add(self.last_write[k])
            for r in self.readers.get(k, ()):
                deps.add(r)
        if eng in self.pending_barrier:
            deps |= self.pending_barrier.pop(eng)
        deps.discard(op.idx)
        op.deps = deps
        for k in writes:
            if k is None:
                continue
            self.last_write[k] = op.idx
            self.readers[k] = []
        for k in reads:
            if k is None:
                continue
            if k in writes:
                continue
            self.readers.setdefault(k, []).append(op.idx)
        self.ops.append(op)
        self.last_on_eng[eng] = op.idx
        if is_dma:
            self.last_dma_ops.append(op.idx)
        return op.idx

    def barrier(self):
        deps = set(self.last_on_eng.values()) | set(self.last_dma_ops)
        self.last_dma_ops = []
        for e in ENGS:
            s = self.pending_barrier.get(e, set())
            self.pending_barrier[e] = s | deps
        self.phase += 1
        self.last_write = {}
        self.readers = {}

    def _keys(self, bufs):
        out = []
        for b in bufs:
            if isinstance(b, Buf):
                if isinstance(b.key, (list, tuple)):
                    out.extend(b.key)
                else:
                    out.append(b.key)
        return out

    def dma(self, out, in_, eng="sp", extra_deps=(), noncontig=False):
        o, i = _ap(out), _ap(in_)
        nc = self.nc

        def fn(e):
            if noncontig:
                with nc.allow_non_contiguous_dma(reason="small strided setup load"):
                    return e.dma_start(out=o, in_=i)
            return e.dma_start(out=o, in_=i)

        return self.add(eng, fn, reads=self._keys([in_]), writes=self._keys([out]), is_dma=True, extra_deps=extra_deps)

    def mm(self, out, lhsT, rhs, start, stop):
        o, l, r = _ap(out), _ap(lhsT), _ap(rhs)

        def fn(e):
            return e.matmul(o, lhsT=l, rhs=r, start=start, stop=stop)

        return self.add("pe", fn, reads=self._keys([lhsT, rhs]), writes=self._keys([out]), pe_like=True)

    def tr(self, out, in_, ident):
        o, i, d = _ap(out), _ap(in_), _ap(ident)

        def fn(e):
            return e.transpose(out=o, in_=i, identity=d)

        return self.add("pe", fn, reads=self._keys([in_, ident]), writes=self._keys([out]), pe_like=True)

    def op(self, eng, name, outs, ins, **kw):
        oa = {k: _ap(v) for k, v in outs.items()}
        ia = {k: _ap(v) for k, v in ins.items()}

        def fn(e):
            return getattr(e, name)(**oa, **ia, **kw)

        return self.add(eng, fn, reads=self._keys(list(ins.values())), writes=self._keys(list(outs.values())))

    def emit(self, block):
        nc = self.nc
        ops = self.ops
        signaling = set()
        for op in ops:
            for d in op.deps:
                dop = ops[d]
                if dop.eng == "pe" and op.eng == "pe":
                    continue
                signaling.add(d)
        PG = 6
        nphase = self.phase // PG + 1
        sem_ctx = []
        esem = {}
        for e in ENGS:
            for p in range(nphase):
                cm = nc.semaphore(f"s_{e}_{p}")
                sem_ctx.append(cm)
                esem[(e, p)] = cm.__enter__()
        dsem = {}
        for e in ("sp", "pool"):
            for j in range(ND_SEM[e]):
                cm = nc.semaphore(f"d_{e}_{j}")
                sem_ctx.append(cm)
                dsem[(e, j)] = cm.__enter__()
        cnt = {}
        dcnt = {e: 0 for e in ("sp", "pool")}
        prev_same_sem = {}
        for op in ops:
            if op.is_dma:
                n = dcnt[op.eng]
                dcnt[op.eng] += 1
                nd = ND_SEM[op.eng]
                sem = dsem[(op.eng, n % nd)]
                val = 16 * (n // nd + 1)
                op.sig = (sem, val, 16)
                key = (op.eng, n % nd)
                if key in prev_same_sem:
                    op.deps.add(prev_same_sem[key])
                prev_same_sem[key] = op.idx
            elif op.idx in signaling:
                k = (op.eng, op.phase // PG)
                cnt[k] = cnt.get(k, 0) + 1
                op.sig = (esem[k], cnt[k], 1)
        per_eng = {e: [o for o in ops if o.eng == e] for e in ENGS}
        final_dma = [(dsem[k], 16 * ((dcnt[k[0]] - 1 - k[1]) // ND_SEM[k[0]] + 1)) for k in dsem if dcnt[k[0]] > k[1]]

        def run(e_name, eng):
            waited = {}
            for op in per_eng[e_name]:
                for d in sorted(op.deps):
                    dop = ops[d]
                    if dop.eng == "pe" and op.eng == "pe":
                        continue
                    sem, val, _ = dop.sig
                    sid = id(sem)
                    if waited.get(sid, 0) >= val:
                        continue
                    eng.wait_ge(sem, val)
                    waited[sid] = val
                ins = op.fn(eng)
                if op.sig is not None:
                    ins.then_inc(op.sig[0], op.sig[2])
            if e_name == "sp":
                for sem, val in final_dma:
                    eng.wait_ge(sem, val)

        @block.sync
        def _(e):
            run("sp", e)

        @block.scalar
        def _(e):
            run("act", e)

        @block.gpsimd
        def _(e):
            run("pool", e)

        @block.tensor
        def _(e):
            run("pe", e)

        @block.vector
        def _(e):
            run("dve", e)

        return sem_ctx


class Arena:
    def __init__(self, ap, nbytes):
        self.ap = ap
        self.nbytes = nbytes
        self.off = 0
        self.uid = 0

    def reset(self):
        self.off = 0

    def alloc(self, shape, dt, name):
        n = 1
        for s in shape:
            n *= s
        esz = 2 if dt == BF16 else 4
        nb = (n * esz + 31) // 32 * 32
        assert self.off + nb <= self.nbytes, f"SBUF arena overflow at {name}: {self.off}+{nb}>{self.nbytes}"
        a = self.ap[:, self.off // 4:(self.off + nb) // 4]
        self.off += nb
        if dt != F32:
            a = a.bitcast(dt)
        a = a[:, 0:n]
        if len(shape) == 2:
            a = a.rearrange("p (a b) -> p a b", b=shape[1])
        elif len(shape) == 3:
            a = a.rearrange("p (a b c) -> p a b c", b=shape[1], c=shape[2])
        self.uid += 1
        return Buf(a, f"{name}#{self.uid}")


def build_program(LP, LS, NPAST, debug=False):
    nc = bass.Bass("TRN2", target_bir_lowering=False)

    def din(name, shape, dt=F32):
        return nc.dram_tensor(name, shape, dt, kind="ExternalInput").ap()

    def dout(name, shape, dt=F32):
        return nc.dram_tensor(name, shape, dt, kind="ExternalOutput").ap()

    def dscr(name, shape, dt=BF16):
        return nc.dram_tensor(name, shape, dt, kind=("ExternalOutput" if debug else "Internal")).ap()

    I = {}
    I["x_prompt"] = din("x_prompt", [LP, D])
    I["x_sample"] = din("x_sample", [LS, D])
    I["cache_ckv"] = din("cache_ckv", [NPAST, KVL])
    I["cache_krope"] = din("cache_krope", [NPAST, ROPE])
    I["state_re"] = din("state_re", [128, 64])
    I["state_im"] = din("state_im", [128, 64])
    for nm, shp in [("norm_mix", [D]), ("w_in", [D, IN_COLS]), ("norm_q", [QL]), ("w_uq", [QL, 3072]),
                    ("norm_kv", [KVL]), ("w_uk", [KVL, D]), ("w_uv", [KVL, D]),
                    ("s5_a_re", [128, 64]), ("s5_a_im", [128, 64]), ("s5_log_dt", [128, 64]),
                    ("s5_b_re", [128, 64, 16]), ("s5_b_im", [128, 64, 16]),
                    ("s5_c_re", [128, 16, 64]), ("s5_c_im", [128, 16, 64]), ("s5_d", [D]),
                    ("w_glu_a", [D, D]), ("w_glu_b", [D, D]), ("w_o", [D, D]), ("norm_mlp", [D]),
                    ("w_up", [D, DFF]), ("w_down", [DFF, D]), ("norm_final", [D])]:
        I[nm] = din(nm, shp)

    O = {}
    O["y_p"] = dout("y_p", [LP, D])
    O["y_s"] = dout("y_s", [LS, D])
    O["ckv_p"] = dout("ckv_p", [LP, KVL])
    O["kr_p"] = dout("kr_p", [LP, ROPE])
    O["sre_p"] = dout("sre_p", [128, 64])
    O["sim_p"] = dout("sim_p", [128, 64])
    O["ckv_s"] = dout("ckv_s", [LS, KVL])
    O["kr_s"] = dout("kr_s", [LS, ROPE])
    O["sre_s"] = dout("sre_s", [128, 64])
    O["sim_s"] = dout("sim_s", [128, 64])

    W = {}
    for nm in ["w_in", "w_uq", "w_uk", "w_uv", "w_glu_a", "w_glu_b", "w_o", "w_up", "w_down"]:
        W[nm] = nc.dram_tensor(nm + "_bf", list(I[nm].shape), BF16, kind="Internal").ap()

    LMAX = max(LP, LS)
    NKMAX = max(LP, NPAST + LS)
    S = {}
    S["cqnT"] = dscr("cqnT_d", [QL, LMAX])
    S["ckvT"] = dscr("ckvT_d", [KVL, NKMAX])
    S["krT"] = dscr("krT_d", [ROPE, NKMAX])
    S["uT"] = dscr("uT_d", [D, LMAX])
    S["gaT"] = dscr("gaT_d", [D, LMAX])
    S["gbT"] = dscr("gbT_d", [D, LMAX])
    S["QT"] = dscr("QT_d", [D, LMAX])
    S["QrT"] = dscr("QrT_d", [NH * ROPE, LMAX])
    S["KT"] = dscr("KT_d", [D, NKMAX])
    S["V"] = dscr("V_d", [NKMAX, D])
    S["attnT"] = dscr("attnT_d", [D, LMAX])
    S["gT"] = dscr("gT_d", [D, LMAX])
    S["mT"] = dscr("mT_d", [D, LMAX])

    ARENA_BYTES = 204800
    stack = []
    arena_cm = nc.sbuf_tensor("arena", [128, ARENA_BYTES // 4], F32)
    arena_t = arena_cm.__enter__()
    stack.append(arena_cm)
    const_cm = nc.sbuf_tensor("consts", [128, 1024], F32)
    const_t = const_cm.__enter__()
    stack.append(const_cm)
    ps = []
    for i in range(8):
        cm = nc.psum_tensor(f"ps{i}", [128, 512], F32)
        t = cm.__enter__()
        stack.append(cm)
        ps.append(Buf(t[:], f"ps{i}"))

    cx = Ctx(nc)
    ar = Arena(arena_t[:], ARENA_BYTES)
    car = Arena(const_t[:], 4096)

    ident = car.alloc([128], BF16, "ident")
    cx.op("pool", "memset", {"ap": ident}, {}, constant=1.0)
    cx.op("pool", "affine_select", {"out": ident}, {"in_": ident}, pattern=[[-1, 128]], compare_op=ALU.is_equal,
          fill=0.0, base=0, channel_multiplier=1)
    ones = car.alloc([128], BF16, "ones")
    cx.op("pool", "memset", {"ap": ones}, {}, constant=1.0)

    wdeps = {}

    def cast_weight(nm, col_chunks=None):
        rows, cols = I[nm].shape
        wdeps[nm] = []
        if col_chunks is not None:
            for (c0_, c1_) in col_chunks:
                wdeps[nm].append(cx.dma(W[nm][:, c0_:c1_], I[nm][:, c0_:c1_], eng="pool"))
            return
        rc = max(128, (2048 * 2048) // cols // 128 * 128)
        for r0_ in range(0, rows, rc):
            r1_ = min(rows, r0_ + rc)
            wdeps[nm].append(cx.dma(W[nm][r0_:r1_, :], I[nm][r0_:r1_, :], eng="pool"))

    cast_weight("w_in", [(0, 1344), (1344, 1344 + 1536), (1344 + 1536, 1344 + 3072), (1344 + 3072, 1344 + 4608), (1344 + 4608, 7488)])
    deferred_casts = [["w_uq", "w_uk", "w_uv"], ["w_glu_a", "w_glu_b", "w_o", "w_up", "w_down"]]

    def wload(dst, nm, src_ap, eng="sp"):
        return cx.dma(dst, src_ap, eng=eng, extra_deps=wdeps[nm])

    def rstd_from_ss(ss, n, tmp, out):
        cx.op("act", "activation", {"out": tmp}, {"in_": ss}, func=AF.Sqrt, scale=1.0 / n, bias=EPS)
        cx.op("dve", "reciprocal", {"out": out}, {"in_": tmp})

    def range_reduce(eng, x, ki, kf):
        cx.op(eng, "tensor_scalar", {"out": ki}, {"in0": x}, scalar1=float(1.0 / TWO_PI), scalar2=None, op0=ALU.mult)
        cx.op(eng, "tensor_copy", {"out": kf}, {"in_": ki})
        cx.op(eng, "scalar_tensor_tensor", {"out": x}, {"in0": kf, "in1": x}, scalar=float(-TWO_PI), op0=ALU.mult,
              op1=ALU.add)
        cx.op(eng, "tensor_scalar", {"out": x}, {"in0": x}, scalar1=3.1415925, scalar2=-3.1415925, op0=ALU.min, op1=ALU.max)

    seqs = [
        dict(name="p", L=LP, pos0=0, NK=LP, x=I["x_prompt"], ckv_out=O["ckv_p"], kr_out=O["kr_p"], y=O["y_p"],
             sre=O["sre_p"], sim=O["sim_p"], cache=False),
        dict(name="s", L=LS, pos0=NPAST, NK=NPAST + LS, x=I["x_sample"], ckv_out=O["ckv_s"], kr_out=O["kr_s"],
             y=O["y_s"], sre=O["sre_s"], sim=O["sim_s"], cache=True),
    ]

    for sq in seqs:
        L = sq["L"]
        NK = sq["NK"]
        KOFF = NK - L
        pos0 = sq["pos0"]
        TB = min(512, L)
        NTB = L // TB
        TT = min(128, L)
        NTT = L // TT

        cx.barrier()
        ar.reset()
        HL = min(1024, L)
        NHALF = L // HL
        ntt_h = HL // TT
        g1T = ar.alloc([KC], F32, "g1T")
        cx.dma(g1T, I["norm_mix"].rearrange("(k p) -> p k", p=128), noncontig=True)
        gqT = ar.alloc([6], F32, "gqT")
        cx.dma(gqT, I["norm_q"].rearrange("(k p) -> p k", p=128), noncontig=True)
        gkv_b = ar.alloc([KVL], F32, "gkv_b")
        cx.dma(gkv_b, I["norm_kv"].partition_broadcast(128))
        wtok = ar.alloc([KC, 1344], BF16, "wtok")
        cx.dma(wtok, W["w_in"].rearrange("(k p) n -> p k n", p=128)[:, :, 0:1344], extra_deps=wdeps["w_in"][0:1])
        cosk = ar.alloc([NTT, 32], F32, "cosk")
        sink = ar.alloc([NTT, 32], F32, "sink")
        rki = ar.alloc([NTT, 32], I32, "rki")
        rkf = ar.alloc([NTT, 32], F32, "rkf")
        posk = ar.alloc([NTT], F32, "posk")
        invk = ar.alloc([32], F32, "invk")
        cx.op("pool", "iota", {"out": posk}, {}, pattern=[[TT, NTT]], base=pos0, channel_multiplier=1,
              allow_small_or_imprecise_dtypes=True)
        cx.op("pool", "iota", {"out": invk}, {}, pattern=[[1, 32]], base=0, channel_multiplier=0,
              allow_small_or_imprecise_dtypes=True)
        cx.op("act", "activation", {"out": invk}, {"in_": invk}, func=AF.Exp, scale=-math.log(10000.0) / 32.0)
        cx.op("dve", "tensor_tensor", {"out": sink}, {"in0": posk.ap.unsqueeze(2).to_broadcast([128, NTT, 32]) if False else Buf(posk.ap.unsqueeze(2).to_broadcast([128, NTT, 32]), posk.key),
                                                       "in1": Buf(invk.ap.unsqueeze(1).to_broadcast([128, NTT, 32]), invk.key)}, op=ALU.mult)
        cx.op("dve", "tensor_scalar", {"out": cosk}, {"in0": sink}, scalar1=float(math.pi / 2), scalar2=None, op0=ALU.add)
        range_reduce("dve", sink, rki, rkf)
        range_reduce("dve", cosk, rki, rkf)
        cx.op("act", "activation", {"out": sink}, {"in_": sink}, func=AF.Sin)
        cx.op("act", "activation", {"out": cosk}, {"in_": cosk}, func=AF.Sin)

        xnT = ar.alloc([KC, HL], BF16, "xnT")
        cqnT_h = ar.alloc([6, HL], BF16, "cqnT_h")
        ckvT_h = ar.alloc([4, HL], BF16, "ckvT_h")
        krT_h = ar.alloc([HL], BF16, "krT_h")
        xt = [ar.alloc([D], F32, f"xt{i}") for i in range(2)]
        xnb = [ar.alloc([D], BF16, f"xnb{i}") for i in range(2)]
        junk = ar.alloc([D], BF16, "junk")
        cqn = [ar.alloc([QL], BF16, f"cqn{i}") for i in range(2)]
        ckv32 = [ar.alloc([KVL], F32, f"ckv32{i}") for i in range(2)]
        ckvb = [ar.alloc([KVL], BF16, f"ckvb{i}") for i in range(2)]
        kr32 = [ar.alloc([ROPE], F32, f"kr32{i}") for i in range(2)]
        krt = [ar.alloc([4, 32], F32, f"krt{i}") for i in range(2)]
        krb = [ar.alloc([ROPE], BF16, f"krb{i}") for i in range(2)]
        stat = [ar.alloc([16], F32, f"stat{i}") for i in range(2)]
        wg = [ar.alloc([KC, 512], BF16, f"wg{i}") for i in range(2)]
        stg = [ar.alloc([HL], BF16, f"stg{i}") for i in range(3)]

        pT2 = [ps[4].bitcast(BF16), ps[5].bitcast(BF16)]
        pTq = ps[6].bitcast(BF16)
        pTk = ps[7].bitcast(BF16)
        xsrc = sq["x"]
        wv_in = W["w_in"].rearrange("(k p) n -> p k n", p=128)
        it = 0
        for h in range(NHALF):
            for tl in range(ntt_h):
                tt = h * ntt_h + tl
                r0 = tt * TT
                sl = it % 2
                it += 1
                X, XB, ST = xt[sl], xnb[sl], stat[sl]
                cx.dma(X[0:TT], xsrc[r0:r0 + TT, :])
                cx.op("act", "activation", {"out": junk[0:TT], "accum_out": ST[0:TT, 0:1]}, {"in_": X[0:TT]}, func=AF.Square)
                rstd_from_ss(ST[0:TT, 0:1], D, ST[0:TT, 1:2], ST[0:TT, 2:3])
                cx.op("act", "activation", {"out": XB[0:TT]}, {"in_": X[0:TT], "scale": ST[0:TT, 2:3]}, func=AF.Copy)
                for kc in range(KC):
                    cx.tr(pT2[kc // 8][:, (kc % 8) * 128:(kc % 8) * 128 + TT], XB[0:TT, kc * 128:(kc + 1) * 128], ident[0:TT, 0:TT])
                for hb in range(2):
                    cx.op("dve", "tensor_tensor",
                          {"out": xnT[:, hb * 8:(hb + 1) * 8, tl * TT:(tl + 1) * TT]},
                          {"in0": pT2[hb].rearrange("p (a b) -> p a b", b=128)[:, :, 0:TT],
                           "in1": Buf(g1T.ap[:, hb * 8:(hb + 1) * 8].unsqueeze(2).to_broadcast([128, 8, TT]), g1T.key)},
                          op=ALU.mult)
                groups = [(0, 512, ps[0]), (512, 256, ps[1]), (768, 512, ps[2]), (1280, 64, ps[3])]
                for (c0, cn, bank) in groups:
                    for kc in range(KC):
                        cx.mm(bank[0:TT, 0:cn], xnT[:, kc, tl * TT:(tl + 1) * TT], wtok[:, kc, c0:c0 + cn], kc == 0, kc == KC - 1)
                cx.op("act", "activation", {"out": junk[0:TT, 0:512], "accum_out": ST[0:TT, 3:4]}, {"in_": ps[0][0:TT, 0:512]}, func=AF.Square)
                cx.op("act", "activation", {"out": junk[0:TT, 0:256], "accum_out": ST[0:TT, 4:5]}, {"in_": ps[1][0:TT, 0:256]}, func=AF.Square)
                cx.op("dve", "tensor_tensor", {"out": ST[0:TT, 5:6]}, {"in0": ST[0:TT, 3:4], "in1": ST[0:TT, 4:5]}, op=ALU.add)
                rstd_from_ss(ST[0:TT, 5:6], QL, ST[0:TT, 6:7], ST[0:TT, 7:8])
                CQ = cqn[sl]
                cx.op("act", "activation", {"out": CQ[0:TT, 0:512]}, {"in_": ps[0][0:TT, 0:512], "scale": ST[0:TT, 7:8]}, func=AF.Copy)
                cx.op("act", "activation", {"out": CQ[0:TT, 512:768]}, {"in_": ps[1][0:TT, 0:256], "scale": ST[0:TT, 7:8]}, func=AF.Copy)
                for j in range(6):
                    cx.tr(pTq[:, j * 128:j * 128 + TT], CQ[0:TT, j * 128:(j + 1) * 128], ident[0:TT, 0:TT])
                cx.op("dve", "tensor_tensor", {"out": cqnT_h[:, :, tl * TT:(tl + 1) * TT]},
                      {"in0": pTq[:, 0:768].rearrange("p (a b) -> p a b", b=128)[:, :, 0:TT],
                       "in1": Buf(gqT.ap.unsqueeze(2).to_broadcast([128, 6, TT]), gqT.key)}, op=ALU.mult)
                cx.op("act", "activation", {"out": junk[0:TT, 0:512], "accum_out": ST[0:TT, 8:9]}, {"in_": ps[2][0:TT, 0:512]}, func=AF.Square)
                rstd_from_ss(ST[0:TT, 8:9], KVL, ST[0:TT, 9:10], ST[0:TT, 10:11])
                CK = ckv32[sl]
                cx.op("dve", "scalar_tensor_tensor", {"out": CK[0:TT]}, {"in0": ps[2][0:TT, 0:512], "in1": gkv_b[0:TT], "scalar": ST[0:TT, 10:11]},
                      op0=ALU.mult, op1=ALU.mult)
                cx.dma(sq["ckv_out"][r0:r0 + TT, :], CK[0:TT], eng="pool")
                CB = ckvb[sl]
                cx.op("act", "activation", {"out": CB[0:TT]}, {"in_": CK[0:TT]}, func=AF.Copy)
                for j in range(4):
                    cx.tr(pTk[:, j * 128:j * 128 + TT], CB[0:TT, j * 128:(j + 1) * 128], ident[0:TT, 0:TT])
                KR, KT_, KB = kr32[sl], krt[sl], krb[sl]
                x1 = ps[3][0:TT, 0:32]
                x2 = ps[3][0:TT, 32:64]
                cs = cosk[0:TT, tt, :]
                sn = sink[0:TT, tt, :]
                cx.op("dve", "tensor_tensor", {"out": KT_[0:TT, 0, :]}, {"in0": x1, "in1": cs}, op=ALU.mult)
                cx.op("dve", "tensor_tensor", {"out": KT_[0:TT, 1, :]}, {"in0": x2, "in1": sn}, op=ALU.mult)
                cx.op("dve", "tensor_tensor", {"out": KT_[0:TT, 2, :]}, {"in0": x1, "in1": sn}, op=ALU.mult)
                cx.op("dve", "tensor_tensor", {"out": KT_[0:TT, 3, :]}, {"in0": x2, "in1": cs}, op=ALU.mult)
                cx.op("dve", "tensor_tensor", {"out": KR[0:TT, 0:32]}, {"in0": KT_[0:TT, 0, :], "in1": KT_[0:TT, 1, :]}, op=ALU.subtract)
                cx.op("dve", "tensor_tensor", {"out": KR[0:TT, 32:64]}, {"in0": KT_[0:TT, 2, :], "in1": KT_[0:TT, 3, :]}, op=ALU.add)
                cx.dma(sq["kr_out"][r0:r0 + TT, :], KR[0:TT], eng="pool")
                cx.op("dve", "tensor_copy", {"out": KB[0:TT]}, {"in_": KR[0:TT]})
                cx.tr(pTk[0:64, 512:512 + TT], KB[0:TT, :], ident[0:TT, 0:TT])
                cx.op("dve", "tensor_copy", {"out": ckvT_h[:, :, tl * TT:(tl + 1) * TT]},
                      {"in_": pTk[:, 0:512].rearrange("p (a b) -> p a b", b=128)[:, :, 0:TT]})
                cx.op("dve", "tensor_copy", {"out": krT_h[0:64, tl * TT:(tl + 1) * TT]}, {"in_": pTk[0:64, 512:512 + TT]})
            c0 = h * HL
            cx.dma(S["cqnT"].rearrange("(k p) t -> p k t", p=128)[:, :, c0:c0 + HL], cqnT_h, eng="pool")
            cx.dma(S["ckvT"].rearrange("(k p) t -> p k t", p=128)[:, :, KOFF + c0:KOFF + c0 + HL], ckvT_h, eng="pool")
            cx.dma(S["krT"][:, KOFF + c0:KOFF + c0 + HL], krT_h[0:64], eng="pool")
            ntb_h = HL // TB
            bank_i = 0
            stg_i = 0
            for sb in range(12):
                WG = wg[sb % 2]
                cx.dma(WG, wv_in[:, :, 1344 + sb * 512:1344 + (sb + 1) * 512], extra_deps=wdeps["w_in"][1 + sb // 3:2 + sb // 3])
                if deferred_casts and sb == 2 and len(deferred_casts) == 2:
                    for nm_ in deferred_casts.pop(0):
                        cast_weight(nm_)
                for j in range(4):
                    cb = sb * 4 + j
                    SG = stg[stg_i % 3]
                    stg_i += 1
                    for tb in range(ntb_h):
                        bank = ps[bank_i % 4]
                        bank_i += 1
                        for kc in range(KC):
                            cx.mm(bank[:, 0:TB], WG[:, kc, j * 128:(j + 1) * 128], xnT[:, kc, tb * TB:(tb + 1) * TB], kc == 0, kc == KC - 1)
                        if cb < 16:
                            cx.op("dve", "tensor_copy", {"out": SG[:, tb * TB:(tb + 1) * TB]}, {"in_": bank[:, 0:TB]})
                        else:
                            cx.op("act", "activation", {"out": SG[:, tb * TB:(tb + 1) * TB]}, {"in_": bank[:, 0:TB]}, func=AF.Sigmoid)
                    dst = S["uT"] if cb < 16 else (S["gaT"] if cb < 32 else S["gbT"])
                    rr = (cb % 16) * 128
                    cx.dma(dst[rr:rr + 128, c0:c0 + HL], SG, eng="pool")

        if debug == "A":
            break

        kblocks = [(k0, min(512, NK - k0)) for k0 in range(0, NK, 512)]
        ktiles = [(k0, min(128, NK - k0)) for k0 in range(0, NK, 128)]
        NKT = len(ktiles)

        if sq["cache"]:
            cx.barrier()
            ar.reset()
            c32 = [ar.alloc([KVL], F32, f"c32_{i}") for i in range(2)]
            k32 = [ar.alloc([ROPE], F32, f"k32_{i}") for i in range(2)]
            cb16 = [ar.alloc([KVL], BF16, f"cb16_{i}") for i in range(2)]
            kb16 = [ar.alloc([ROPE], BF16, f"kb16_{i}") for i in range(2)]
            cTs = ar.alloc([4, KOFF], BF16, "cTs")
            kTs = ar.alloc([KOFF], BF16, "kTs")
            pc = [ps[0].bitcast(BF16), ps[1].bitcast(BF16)]
            for tt in range(KOFF // 128):
                sl = tt % 2
                cx.dma(c32[sl], I["cache_ckv"][tt * 128:(tt + 1) * 128, :])
                cx.dma(k32[sl], I["cache_krope"][tt * 128:(tt + 1) * 128, :])
                cx.op("act", "activation", {"out": cb16[sl]}, {"in_": c32[sl]}, func=AF.Copy)
                cx.op("dve", "tensor_copy", {"out": kb16[sl]}, {"in_": k32[sl]})
                for j in range(4):
                    cx.tr(pc[sl][:, j * 128:(j + 1) * 128], cb16[sl][:, j * 128:(j + 1) * 128], ident)
                cx.tr(pc[sl][0:64, 512:640], kb16[sl], ident)
                cx.op("dve", "tensor_copy", {"out": cTs[:, :, tt * 128:(tt + 1) * 128]},
                      {"in_": pc[sl][:, 0:512].rearrange("p (a b) -> p a b", b=128)})
                cx.op("dve", "tensor_copy", {"out": kTs[0:64, tt * 128:(tt + 1) * 128]}, {"in_": pc[sl][0:64, 512:640]})
            cx.dma(S["ckvT"].rearrange("(k p) t -> p k t", p=128)[:, :, 0:KOFF], cTs, eng="pool")
            cx.dma(S["krT"][:, 0:KOFF], kTs[0:64], eng="pool")

        cx.barrier()
        ar.reset()
        cosF = ar.alloc([L], F32, "cosF")
        sinF = ar.alloc([L], F32, "sinF")
        mark = ar.off
        fki = ar.alloc([L], I32, "fki")
        fkf = ar.alloc([L], F32, "fkf")
        pcol = ar.alloc([2], F32, "pcol")
        cx.op("pool", "iota", {"out": pcol[:, 0:1]}, {}, pattern=[[0, 1]], base=0, channel_multiplier=1,
              allow_small_or_imprecise_dtypes=True)
        cx.op("dve", "tensor_scalar", {"out": pcol[:, 1:2]}, {"in0": pcol[:, 0:1]}, scalar1=32.0, scalar2=-32.0, op0=ALU.is_ge, op1=ALU.mult)
        cx.op("dve", "tensor_tensor", {"out": pcol[:, 0:1]}, {"in0": pcol[:, 0:1], "in1": pcol[:, 1:2]}, op=ALU.add)
        cx.op("act", "activation", {"out": pcol[:, 1:2]}, {"in_": pcol[:, 0:1]}, func=AF.Exp, scale=-math.log(10000.0) / 32.0)
        cx.op("pool", "iota", {"out": sinF}, {}, pattern=[[1, L]], base=pos0, channel_multiplier=0,
              allow_small_or_imprecise_dtypes=True)
        cx.op("dve", "tensor_scalar", {"out": sinF}, {"in0": sinF, "scalar1": pcol[:, 1:2]}, scalar2=None, op0=ALU.mult)
        cx.op("dve", "tensor_scalar", {"out": cosF}, {"in0": sinF}, scalar1=float(math.pi / 2), scalar2=None, op0=ALU.add)
        range_reduce("dve", sinF, fki, fkf)
        range_reduce("dve", cosF, fki, fkf)
        cx.op("act", "activation", {"out": sinF}, {"in_": sinF}, func=AF.Sin)
        cx.op("act", "activation", {"out": cosF}, {"in_": cosF}, func=AF.Sin)
        cx.barrier()
        ar.off = mark
        cqnT = ar.alloc([6, L], BF16, "cqnT")
        cx.dma(cqnT, S["cqnT"].rearrange("(k p) t -> p k t", p=128)[:, :, 0:L])
        wq = [ar.alloc([6, 768], BF16, f"wq{i}") for i in range(2)]
        wrot = [ar.alloc([6, 4, 64], BF16, f"wrot{i}") for i in range(2)]
        qst = [ar.alloc([L], BF16, f"qst{i}") for i in range(2)]
        qrst = [ar.alloc([L], BF16, f"qrst{i}") for i in range(2)]
        rt = [ar.alloc([2, TB], F32, f"rt{i}") for i in range(2)]
        bi_ = 0
        wquv = W["w_uq"].rearrange("(k p) n -> p k n", p=128)
        for hg in range(4):
            WQ, WR = wq[hg % 2], wrot[hg % 2]
            wload(WQ, "w_uq", wquv[:, :, hg * 768:(hg + 1) * 768])
            for hh in range(4):
                cx.op("pool", "tensor_scalar", {"out": WR[:, :, hh, 0:32]}, {"in0": WQ[:, :, hh * 192 + 160:hh * 192 + 192]},
                      scalar1=-1.0, scalar2=None, op0=ALU.mult)
                cx.op("pool", "tensor_copy", {"out": WR[:, :, hh, 32:64]}, {"in_": WQ[:, :, hh * 192 + 128:hh * 192 + 160]})
            for hh in range(4):
                h_ = hg * 4 + hh
                QS, QRS = qst[h_ % 2], qrst[h_ % 2]
                for tb in range(NTB):
                    t0 = tb * TB
                    bn = ps[bi_ % 2]
                    ba = ps[2 + (bi_ % 2) * 2]
                    bb = ps[3 + (bi_ % 2) * 2]
                    RT = rt[bi_ % 2]
                    bi_ += 1
                    for kc in range(6):
                        cx.mm(bn[:, 0:TB], WQ[:, kc, hh * 192:hh * 192 + 128], cqnT[:, kc, t0:t0 + TB], kc == 0, kc == 5)
                    cx.op("act", "activation", {"out": QS[:, t0:t0 + TB]}, {"in_": bn[:, 0:TB]}, func=AF.Copy)
                    for kc in range(6):
                        cx.mm(ba[0:64, 0:TB], WQ[:, kc, hh * 192 + 128:hh * 192 + 192], cqnT[:, kc, t0:t0 + TB], kc == 0, kc == 5)
                    for kc in range(6):
                        cx.mm(bb[0:64, 0:TB], WR[:, kc, hh, :], cqnT[:, kc, t0:t0 + TB], kc == 0, kc == 5)
                    cx.op("dve", "tensor_tensor", {"out": RT[0:64, 0, :]}, {"in0": ba[0:64, 0:TB], "in1": cosF[0:64, t0:t0 + TB]}, op=ALU.mult)
                    cx.op("dve", "tensor_tensor", {"out": RT[0:64, 1, :]}, {"in0": bb[0:64, 0:TB], "in1": sinF[0:64, t0:t0 + TB]}, op=ALU.mult)
                    cx.op("pool", "tensor_tensor", {"out": QRS[0:64, t0:t0 + TB]}, {"in0": RT[0:64, 0, :], "in1": RT[0:64, 1, :]}, op=ALU.add)
                cx.dma(S["QT"][h_ * 128:(h_ + 1) * 128, 0:L], QS, eng="pool")
                cx.dma(S["QrT"][h_ * 64:(h_ + 1) * 64, 0:L], QRS[0:64], eng="pool")
        cx.barrier()
        ar.reset()
        ckvT = ar.alloc([4, NK], BF16, "ckvT")
        cx.dma(ckvT, S["ckvT"].rearrange("(k p) t -> p k t", p=128)[:, :, 0:NK])
        wk = ar.alloc([4, D], BF16, "wk")
        wv = ar.alloc([4, D], BF16, "wv")
        wload(wk, "w_uk", W["w_uk"].rearrange("(k p) n -> p k n", p=128))
        wload(wv, "w_uv", W["w_uv"].rearrange("(k p) n -> p k n", p=128))
        kst = [ar.alloc([NK], BF16, f"kst{i}") for i in range(2)]
        vst = [ar.alloc([D], BF16, f"vst{i}") for i in range(2)]
        for h_ in range(NH):
            KS = kst[h_ % 2]
            for (k0, kn) in kblocks:
                bn = ps[bi_ % 4]
                bi_ += 1
                for kc in range(4):
                    cx.mm(bn[:, 0:kn], wk[:, kc, h_ * 128:(h_ + 1) * 128], ckvT[:, kc, k0:k0 + kn], kc == 0, kc == 3)
                cx.op("act" if (bi_ % 2) else "dve", "activation" if (bi_ % 2) else "tensor_copy",
                      {"out": KS[:, k0:k0 + kn]}, {"in_": bn[:, 0:kn]}, **({"func": AF.Copy} if (bi_ % 2) else {}))
            cx.dma(S["KT"][h_ * 128:(h_ + 1) * 128, 0:NK], KS, eng="pool")
        for ti, (k0, kn) in enumerate(ktiles):
            VS = vst[ti % 2]
            for cb in range(4):
                bn = ps[bi_ % 4]
                bi_ += 1
                for kc in range(4):
                    cx.mm(bn[0:kn, :], ckvT[:, kc, k0:k0 + kn], wv[:, kc, cb * 512:(cb + 1) * 512], kc == 0, kc == 3)
                if cb % 2:
                    cx.op("act", "activation", {"out": VS[0:kn, cb * 512:(cb + 1) * 512]}, {"in_": bn[0:kn, :]}, func=AF.Copy)
                else:
                    cx.op("dve", "tensor_copy", {"out": VS[0:kn, cb * 512:(cb + 1) * 512]}, {"in_": bn[0:kn, :]})
            cx.dma(S["V"][k0:k0 + kn, :], VS[0:kn], eng="pool")
        if debug == "Q":
            break

        cx.barrier()
        ar.reset()
        krT = ar.alloc([NK], BF16, "krT")
        cx.dma(krT[0:64], S["krT"][:, 0:NK])
        QTh = [ar.alloc([L], BF16, f"QTh{i}") for i in range(2)]
        QrTh = [ar.alloc([L], BF16, f"QrTh{i}") for i in range(2)]
        KTh = [ar.alloc([NK], BF16, f"KTh{i}") for i in range(2)]
        Vh = [ar.alloc([NKT, 128], BF16, f"Vh{i}") for i in range(2)]
        pts = [ar.alloc([512], BF16, f"pt{i}") for i in range(4)]
        rcp = [ar.alloc([512], F32, f"rcp{i}") for i in range(2)]
        ost = [ar.alloc([L], BF16, f"ost{i}") for i in range(2)]
        nfull = NK // 128

        def attn_loads(h_):
            sl = h_ % 2
            cx.dma(QTh[sl], S["QT"][h_ * 128:(h_ + 1) * 128, 0:L])
            cx.dma(QrTh[sl][0:64], S["QrT"][h_ * 64:(h_ + 1) * 64, 0:L])
            cx.dma(KTh[sl], S["KT"][h_ * 128:(h_ + 1) * 128, 0:NK])
            if nfull:
                cx.dma(Vh[sl][:, 0:nfull, :], S["V"][0:nfull * 128, h_ * 128:(h_ + 1) * 128].rearrange("(kt p) c -> p kt c", p=128))
            if NK % 128:
                rem = NK % 128
                cx.dma(Vh[sl][0:rem, nfull, :], S["V"][nfull * 128:NK, h_ * 128:(h_ + 1) * 128])

        attn_loads(0)
        attn_loads(1)
        jobs = []
        for h_ in range(NH):
            for qb in range(NTB):
                q0 = qb * TB
                if sq["cache"]:
                    tiles = [(k0, kn, False) for (k0, kn) in ktiles]
                else:
                    tiles = [(k0, kn, k0 >= q0) for (k0, kn) in ktiles if k0 < q0 + TB]
                for i, (k0, kn, diag) in enumerate(tiles):
                    jobs.append(dict(h=h_, qb=qb, q0=q0, i=i, n=len(tiles), k0=k0, kn=kn, diag=diag,
                                     qi=h_ * NTB + qb, first_of_head=(qb == 0 and i == 0),
                                     last_of_head=(qb == NTB - 1 and i == len(tiles) - 1)))
        LOOK = 2

        def s_stage(ji, jb):
            sl = jb["h"] % 2
            q0, k0, kn = jb["q0"], jb["k0"], jb["kn"]
            c0 = (k0 - q0) if jb["diag"] else 0
            SB = ps[ji % 4]
            PT = pts[ji % 4]
            cx.mm(SB[0:kn, c0:TB], KTh[sl][:, k0:k0 + kn], QTh[sl][:, q0 + c0:q0 + TB], True, False)
            cx.mm(SB[0:kn, c0:TB], krT[0:64, k0:k0 + kn], QrTh[sl][0:64, q0 + c0:q0 + TB], False, True)
            cx.op("act", "activation", {"out": PT[0:kn, c0:TB]}, {"in_": SB[0:kn, c0:TB]}, func=AF.Exp, scale=SCALE)
            if jb["diag"]:
                cx.op("dve", "memset", {"ap": PT[64:128, c0:c0 + 64]}, {}, constant=0.0)

        def pv_stage(ji, jb):
            sl = jb["h"] % 2
            q0, k0, kn = jb["q0"], jb["k0"], jb["kn"]
            c0 = (k0 - q0) if jb["diag"] else 0
            PT = pts[ji % 4]
            OB = ps[4 + 2 * (jb["qi"] % 2)]
            SM = ps[5 + 2 * (jb["qi"] % 2)]
            RC = rcp[jb["qi"] % 2]
            last = (jb["i"] == jb["n"] - 1)
            cx.mm(OB[:, c0:TB], Vh[sl][0:kn, k0 // 128, :], PT[0:kn, c0:TB], jb["i"] == 0, last)
            cx.mm(SM[:, c0:TB], ones[0:kn, :], PT[0:kn, c0:TB], jb["i"] == 0, last)
            if last:
                cx.op("dve", "reciprocal", {"out": RC[:, 0:TB]}, {"in_": SM[:, 0:TB]})
                cx.op("dve", "tensor_tensor", {"out": ost[sl][:, q0:q0 + TB]}, {"in0": OB[:, 0:TB], "in1": RC[:, 0:TB]}, op=ALU.mult)
            if jb["last_of_head"]:
                cx.dma(S["attnT"][jb["h"] * 128:(jb["h"] + 1) * 128, 0:L], ost[sl], eng="pool")
                if deferred_casts and len(deferred_casts) == 1 and deferred_casts[0]:
                    cast_weight(deferred_casts[0].pop(0))
                if jb["h"] + 2 < NH:
                    attn_loads(jb["h"] + 2)

        for ji in range(len(jobs) + LOOK):
            if ji < len(jobs):
                s_stage(ji, jobs[ji])
            if ji >= LOOK:
                pv_stage(ji - LOOK, jobs[ji - LOOK])
        if debug == "T":
            break

        cx.barrier()
        ar.reset()
        TBs = TB
        NA = TBs // 16
        v2 = lambda a: a.rearrange("(ct gl) p -> (gl p) ct", gl=2)
        are = ar.alloc([64], F32, "are")
        aim = ar.alloc([64], F32, "aim")
        dtt = ar.alloc([64], F32, "dtt")
        cx.dma(are, v2(I["s5_a_re"]), noncontig=True)
        cx.dma(aim, v2(I["s5_a_im"]), noncontig=True)
        cx.dma(dtt, v2(I["s5_log_dt"]), noncontig=True)
        dcy = ar.alloc([64], F32, "dcy")
        thr = ar.alloc([64], F32, "thr")
        sth = ar.alloc([64], F32, "sth")
        cth = ar.alloc([64], F32, "cth")
        tki = ar.alloc([64], I32, "tki")
        tkf = ar.alloc([64], F32, "tkf")
        t64 = [ar.alloc([64], F32, f"t64_{i}") for i in range(6)]
        cx.op("act", "activation", {"out": dtt}, {"in_": dtt}, func=AF.Exp)
        cx.op("dve", "tensor_tensor", {"out": dcy}, {"in0": are, "in1": dtt}, op=ALU.mult)
        cx.op("act", "activation", {"out": dcy}, {"in_": dcy}, func=AF.Exp)
        cx.op("dve", "tensor_tensor", {"out": thr}, {"in0": aim, "in1": dtt}, op=ALU.mult)
        cx.op("dve", "tensor_scalar", {"out": cth}, {"in0": thr}, scalar1=float(math.pi / 2), scalar2=None, op0=ALU.add)
        range_reduce("dve", thr, tki, tkf)
        range_reduce("dve", cth, tki, tkf)
        cx.op("act", "activation", {"out": sth}, {"in_": thr}, func=AF.Sin)
        cx.op("act", "activation", {"out": cth}, {"in_": cth}, func=AF.Sin)
        abr, abi, den, nr_, cre, cim = t64
        cx.op("dve", "tensor_tensor", {"out": abr}, {"in0": dcy, "in1": cth}, op=ALU.mult)
        cx.op("dve", "tensor_tensor", {"out": abi}, {"in0": dcy, "in1": sth}, op=ALU.mult)
        cx.op("dve", "tensor_tensor", {"out": den}, {"in0": are, "in1": are}, op=ALU.mult)
        cx.op("dve", "tensor_tensor", {"out": nr_}, {"in0": aim, "in1": aim}, op=ALU.mult)
        cx.op("dve", "tensor_tensor", {"out": den}, {"in0": den, "in1": nr_}, op=ALU.add)
        cx.op("dve", "reciprocal", {"out": den}, {"in_": den})
        cx.op("dve", "tensor_scalar", {"out": nr_}, {"in0": abr}, scalar1=-1.0, scalar2=None, op0=ALU.add)
        cx.op("dve", "tensor_tensor", {"out": cre}, {"in0": nr_, "in1": are}, op=ALU.mult)
        cx.op("dve", "tensor_tensor", {"out": cim}, {"in0": abi, "in1": aim}, op=ALU.mult)
        cx.op("dve", "tensor_tensor", {"out": cre}, {"in0": cre, "in1": cim}, op=ALU.add)
        cx.op("dve", "tensor_tensor", {"out": cre}, {"in0": cre, "in1": den}, op=ALU.mult)
        cx.op("dve", "tensor_tensor", {"out": cim}, {"in0": abi, "in1": are}, op=ALU.mult)
        cx.op("dve", "tensor_tensor", {"out": abr}, {"in0": nr_, "in1": aim}, op=ALU.mult)
        cx.op("dve", "tensor_tensor", {"out": cim}, {"in0": cim, "in1": abr}, op=ALU.subtract)
        cx.op("dve", "tensor_tensor", {"out": cim}, {"in0": cim, "in1": den}, op=ALU.mult)
        v3 = lambda a: a.rearrange("(ct gl) p m -> (gl p) ct m", gl=2)
        BTr = ar.alloc([64, 128], BF16, "BTr")
        BTi = ar.alloc([64, 128], BF16, "BTi")
        CTr = ar.alloc([64, 128], BF16, "CTr")
        CTi = ar.alloc([64, 128], BF16, "CTi")
        mark = ar.off
        bre = ar.alloc([64, 16], F32, "bre")
        bim = ar.alloc([64, 16], F32, "bim")
        bt1 = ar.alloc([64, 16], F32, "bt1")
        bt2 = ar.alloc([64, 16], F32, "bt2")
        cx.dma(bre, v3(I["s5_b_re"]))
        cx.dma(bim, v3(I["s5_b_im"]))
        bcast = lambda b: Buf(b.ap.unsqueeze(2).to_broadcast([128, 64, 16]), b.key)
        xpr = ar.alloc([64, 128], BF16, "xpr")
        xpi = ar.alloc([64, 128], BF16, "xpi")
        cx.op("pool", "memset", {"ap": xpr}, {}, constant=0.0)
        cx.op("pool", "memset", {"ap": xpi}, {}, constant=0.0)
        cx.op("dve", "tensor_tensor", {"out": bt1}, {"in0": bre, "in1": bcast(cre)}, op=ALU.mult)
        cx.op("dve", "tensor_tensor", {"out": bt2}, {"in0": bim, "in1": bcast(cim)}, op=ALU.mult)
        cx.op("dve", "tensor_tensor", {"out": bt1}, {"in0": bt1, "in1": bt2}, op=ALU.subtract)
        cx.op("dve", "tensor_tensor", {"out": bt2}, {"in0": bim, "in1": bcast(cre)}, op=ALU.mult)
        cx.op("dve", "tensor_tensor", {"out": bim}, {"in0": bre, "in1": bcast(cim)}, op=ALU.mult)
        cx.op("dve", "tensor_tensor", {"out": bt2}, {"in0": bt2, "in1": bim}, op=ALU.add)
        for (src, dst) in ((bt1, xpr), (bt2, xpi)):
            s4 = src.rearrange("p (c j) m -> p c j m", j=4)
            d4 = dst.rearrange("p (c j) n -> p c j n", j=4)
            for j in range(4):
                for gl in range(2):
                    cx.op("dve", "tensor_copy", {"out": d4[gl * 64:(gl + 1) * 64, :, j, 32 * j + 16 * gl:32 * j + 16 * gl + 16]},
                          {"in_": s4[gl * 64:(gl + 1) * 64, :, j, :]})
        pb = [ps[0].bitcast(BF16), ps[1].bitcast(BF16)]
        gi_ = 0
        for (src, dst) in ((xpr, BTr), (xpi, BTi)):
            for c8 in range(8):
                P_ = pb[gi_ % 2]
                gi_ += 1
                for i in range(8):
                    cx.tr(P_[:, i * 128:(i + 1) * 128], src[:, c8 * 8 + i, :], ident)
                cx.op("dve", "tensor_copy", {"out": dst[:, c8 * 8:(c8 + 1) * 8, :]}, {"in_": P_.rearrange("p (a b) -> p a b", b=128)})
        cx.barrier()
        ar.off = mark
        ypr = ar.alloc([64, 128], F32, "ypr")
        ypi = ar.alloc([64, 128], F32, "ypi")
        cx.op("pool", "memset", {"ap": ypr[0:32]}, {}, constant=0.0)
        cx.op("pool", "memset", {"ap": ypi[0:32]}, {}, constant=0.0)
        for (srcd, dst) in ((I["s5_c_re"], ypr), (I["s5_c_im"], ypi)):
            c4 = srcd.rearrange("(ct gl) m p -> gl m ct p", gl=2)
            for gl in range(2):
                cx.dma(dst[gl * 16:(gl + 1) * 16, :, gl * 64:(gl + 1) * 64], c4[gl])
        ybr = ar.alloc([64, 128], BF16, "ybr")
        ybi = ar.alloc([64, 128], BF16, "ybi")
        cx.op("act", "activation", {"out": ybr[0:32]}, {"in_": ypr[0:32]}, func=AF.Copy)
        cx.op("act", "activation", {"out": ybi[0:32]}, {"in_": ypi[0:32]}, func=AF.Copy, scale=-1.0)
        cx.op("pool", "memset", {"ap": CTr}, {}, constant=0.0)
        cx.op("pool", "memset", {"ap": CTi}, {}, constant=0.0)
        for (src, dst) in ((ybr, CTr), (ybi, CTi)):
            d4 = dst.rearrange("p (c j) n -> p c j n", j=4)
            for c16 in range(4):
                P_ = pb[gi_ % 2]
                gi_ += 1
                for i in range(16):
                    cx.tr(P_[:, i * 32:(i + 1) * 32], src[0:32, c16 * 16 + i, :], ident[0:32, 0:32])
                p4 = P_[:, 0:512].rearrange("p (c j n) -> p c j n", j=4, n=32)
                for j in range(4):
                    cx.op("dve", "tensor_copy", {"out": d4[:, c16 * 4:(c16 + 1) * 4, j, 32 * j:32 * j + 32]}, {"in_": p4[:, :, j, :]})
        cx.barrier()
        ar.off = mark
        dT = ar.alloc([KC], F32, "dT")
        cx.dma(dT, I["s5_d"].rearrange("(k p) -> p k", p=128), noncontig=True)
        NAB = NA + 16
        mult = ar.alloc([NAB], F32, "mult")
        cx.op("pool", "iota", {"out": mult[:, 0:NA]}, {}, pattern=[[16, NA]], base=0, channel_multiplier=0, allow_small_or_imprecise_dtypes=True)
        cx.op("pool", "iota", {"out": mult[:, NA:NAB]}, {}, pattern=[[1, 16]], base=1, channel_multiplier=0, allow_small_or_imprecise_dtypes=True)
        sAB = ar.alloc([64, NAB], F32, "sAB")
        cAB = ar.alloc([64, NAB], F32, "cAB")
        car_r = ar.alloc([64], F32, "car_r")
        car_i = ar.alloc([64], F32, "car_i")
        mark2 = ar.off
        aki = ar.alloc([64, NAB], I32, "aki")
        akf = ar.alloc([64, NAB], F32, "akf")
        cx.op("dve", "tensor_tensor", {"out": sAB}, {"in0": Buf(thr.ap.unsqueeze(2).to_broadcast([128, 64, NAB]), thr.key),
                                                       "in1": Buf(mult.ap.unsqueeze(1).to_broadcast([128, 64, NAB]), mult.key)}, op=ALU.mult)
        cx.op("dve", "tensor_scalar", {"out": cAB}, {"in0": sAB}, scalar1=float(math.pi / 2), scalar2=None, op0=ALU.add)
        range_reduce("dve", sAB, aki, akf)
        range_reduce("dve", cAB, aki, akf)
        cx.op("act", "activation", {"out": sAB}, {"in_": sAB}, func=AF.Sin)
        cx.op("act", "activation", {"out": cAB}, {"in_": cAB}, func=AF.Sin)
        cx.barrier()
        ar.off = mark2
        if sq["cache"]:
            cx.dma(car_r, v2(I["state_re"]), noncontig=True)
            cx.dma(car_i, v2(I["state_im"]), noncontig=True)
        else:
            cx.op("pool", "memset", {"ap": car_r}, {}, constant=0.0)
            cx.op("pool", "memset", {"ap": car_i}, {}, constant=0.0)
        uch = [ar.alloc([L], BF16, f"uch{i}") for i in range(2)]
        tabc4 = ar.alloc([4, NA, 16], F32, "tabc4")
        tabs4 = ar.alloc([4, NA, 16], F32, "tabs4")
        tw = [ar.alloc([NA, 16], F32, f"tw{j}") for j in range(2)]
        def alloc4(shape, dt, name):
            big = ar.alloc([4] + shape, dt, name)
            parts = [Buf(big.ap[:, j], f"{big.key}/{j}") for j in range(4)]
            allb = Buf(big.ap, [p.key for p in parts])
            return parts, allb

        A1, A1a = alloc4([TBs], F32, "A1")
        A2, A2a = alloc4([TBs], F32, "A2")
        B1, B1a = alloc4([TBs], F32, "B1")
        B2, B2a = alloc4([TBs], F32, "B2")
        A3, A3a = alloc4([TBs], F32, "A3")
        A4, A4a = alloc4([TBs], F32, "A4")
        hrb, hrba = alloc4([TBs], BF16, "hrb")
        hib, hiba = alloc4([TBs], BF16, "hib")
        yv = [ar.alloc([TBs], F32, f"yv{i}") for i in range(1)] * 2
        y2 = [ar.alloc([TBs], F32, f"y2{i}") for i in range(1)] * 2
        gst = [ar.alloc([L], BF16, f"gst{i}") for i in range(1)] * 2
        yi_ = 0
        units = [(ch, tb) for ch in range(KC) for tb in range(NTB)]
        Call = tabc4.rearrange("p j a b -> p j (a b)")
        Sall = tabs4.rearrange("p j a b -> p j (a b)")

        def xset(ui):
            return (A1, A1a, A2, A2a) if ui % 2 == 0 else (B1, B1a, B2, B2a)

        def stage1(ui):
            ch, tb = units[ui]
            U = uch[ch % 2]
            if tb == 0:
                cx.dma(U, S["uT"][ch * 128:(ch + 1) * 128, 0:L])
            X1, X1a, X2, X2a = xset(ui)
            t0 = tb * TB
            for j in range(4):
                ct = ch * 4 + j
                Pr = ps[(j % 3) * 2]
                Pi = ps[(j % 3) * 2 + 1]
                cx.mm(Pr[:, 0:TB], BTr[:, ct, :], U[:, t0:t0 + TB], True, True)
                cx.mm(Pi[:, 0:TB], BTi[:, ct, :], U[:, t0:t0 + TB], True, True)
                cx.op("act", "activation", {"out": X1[j]}, {"in_": Pr[:, 0:TB]}, func=AF.Copy)
                cx.op("act", "activation", {"out": X2[j]}, {"in_": Pi[:, 0:TB]}, func=AF.Copy)

        stage1(0)
        for ui, (ch, tb) in enumerate(units):
            U = uch[ch % 2]
            GS = gst[ch % 2]
            X1, X1a, X2, X2a = xset(ui)
            t0 = tb * TB
            if tb == 0:
                for j in range(4):
                    ct = ch * 4 + j
                    cA = Buf(cAB.ap[:, ct, 0:NA].unsqueeze(2).to_broadcast([128, NA, 16]), cAB.key)
                    sA = Buf(sAB.ap[:, ct, 0:NA].unsqueeze(2).to_broadcast([128, NA, 16]), sAB.key)
                    cB = Buf(cAB.ap[:, ct, NA:NAB].unsqueeze(1).to_broadcast([128, NA, 16]), cAB.key)
                    sB = Buf(sAB.ap[:, ct, NA:NAB].unsqueeze(1).to_broadcast([128, NA, 16]), sAB.key)
                    TC = tabc4[:, j]
                    TS2 = tabs4[:, j]
                    cx.op("pool", "tensor_tensor", {"out": TC}, {"in0": cA, "in1": cB}, op=ALU.mult)
                    cx.op("pool", "tensor_tensor", {"out": tw[0]}, {"in0": sA, "in1": sB}, op=ALU.mult)
                    cx.op("pool", "tensor_tensor", {"out": TC}, {"in0": TC, "in1": tw[0]}, op=ALU.subtract)
                    cx.op("pool", "tensor_tensor", {"out": TS2}, {"in0": sA, "in1": cB}, op=ALU.mult)
                    cx.op("pool", "tensor_tensor", {"out": tw[1]}, {"in0": cA, "in1": sB}, op=ALU.mult)
                    cx.op("pool", "tensor_tensor", {"out": TS2}, {"in0": TS2, "in1": tw[1]}, op=ALU.add)

            if ui + 1 < len(units):
                stage1(ui + 1)
            YB = ps[6 + (yi_ % 2)]
            cx.op("dve", "tensor_tensor", {"out": A3a}, {"in0": X1a, "in1": Sall}, op=ALU.mult)
            cx.op("dve", "tensor_tensor", {"out": A4a}, {"in0": X2a, "in1": Call}, op=ALU.mult)
            cx.op("dve", "tensor_tensor", {"out": X1a}, {"in0": X1a, "in1": Call}, op=ALU.mult)
            cx.op("dve", "tensor_tensor", {"out": X2a}, {"in0": X2a, "in1": Sall}, op=ALU.mult)
            cx.op("dve", "tensor_tensor", {"out": X1a}, {"in0": X1a, "in1": X2a}, op=ALU.add)
            cx.op("dve", "tensor_tensor", {"out": A4a}, {"in0": A4a, "in1": A3a}, op=ALU.subtract)
            for j in range(4):
                ct = ch * 4 + j
                dk = Buf(dcy.ap[:, ct:ct + 1].to_broadcast([128, TB]), dcy.key)
                cx.op("dve", "tensor_tensor_scan", {"out": X2[j]}, {"data0": dk, "data1": X1[j], "initial": car_r[:, ct:ct + 1]}, op0=ALU.mult, op1=ALU.add)
                cx.op("dve", "tensor_tensor_scan", {"out": A3[j]}, {"data0": dk, "data1": A4[j], "initial": car_i[:, ct:ct + 1]}, op0=ALU.mult, op1=ALU.add)
            cx.op("dve", "tensor_tensor", {"out": X1a}, {"in0": X2a, "in1": Call}, op=ALU.mult)
            cx.op("dve", "tensor_tensor", {"out": A4a}, {"in0": A3a, "in1": Sall}, op=ALU.mult)
            cx.op("dve", "tensor_tensor", {"out": X1a}, {"in0": X1a, "in1": A4a}, op=ALU.subtract)
            cx.op("act", "activation", {"out": hrba}, {"in_": X1a}, func=AF.Copy)
            cx.op("act", "activation", {"out": car_r[:, ch * 4:(ch + 1) * 4]}, {"in_": X1a[:, :, TB - 1]}, func=AF.Copy)
            cx.op("dve", "tensor_tensor", {"out": A4a}, {"in0": X2a, "in1": Sall}, op=ALU.mult)
            cx.op("dve", "tensor_tensor", {"out": X2a}, {"in0": A3a, "in1": Call}, op=ALU.mult)
            cx.op("dve", "tensor_tensor", {"out": A4a}, {"in0": A4a, "in1": X2a}, op=ALU.add)
            cx.op("act", "activation", {"out": hiba}, {"in_": A4a}, func=AF.Copy)
            cx.op("act", "activation", {"out": car_i[:, ch * 4:(ch + 1) * 4]}, {"in_": A4a[:, :, TB - 1]}, func=AF.Copy)
            for j in range(4):
                ct = ch * 4 + j
                cx.mm(YB[:, 0:TB], CTr[:, ct, :], hrb[j], j == 0, False)
                cx.mm(YB[:, 0:TB], CTi[:, ct, :], hib[j], False, j == 3)
            Y, Y2 = yv[yi_ % 2], y2[yi_ % 2]
            yi_ += 1
            cx.op("dve", "scalar_tensor_tensor", {"out": Y}, {"in0": U[:, t0:t0 + TB], "scalar": dT[:, ch:ch + 1], "in1": YB[:, 0:TB]}, op0=ALU.mult, op1=ALU.add)
            cx.op("act", "activation", {"out": Y2}, {"in_": Y}, func=AF.Square)
            cx.op("dve", "tensor_scalar", {"out": Y2}, {"in0": Y2}, scalar1=0.044715, scalar2=1.0, op0=ALU.mult, op1=ALU.add)
            cx.op("pool", "tensor_tensor", {"out": Y2}, {"in0": Y2, "in1": Y}, op=ALU.mult)
            cx.op("act", "activation", {"out": Y2}, {"in_": Y2}, func=AF.Sigmoid, scale=2.0 * GELU_C)
            cx.op("pool", "tensor_tensor", {"out": GS[:, t0:t0 + TB]}, {"in0": Y2, "in1": Y}, op=ALU.mult)

            if tb == NTB - 1:
                cx.dma(S["gT"][ch * 128:(ch + 1) * 128, 0:L], GS, eng="pool")
        cx.dma(v2(sq["sre"]), car_r, eng="pool", noncontig=True)
        cx.dma(v2(sq["sim"]), car_i, eng="pool", noncontig=True)
        if debug == "S":
            break

        cx.barrier()
        ar.reset()
        HL2 = min(2048, L)
        NH2 = L // HL2
        gTh = ar.alloc([KC, HL2], BF16, "gTh")
        wa = [ar.alloc([KC, 512], BF16, f"wa{i}") for i in range(2)]
        wb_ = [ar.alloc([KC, 512], BF16, f"wb{i}") for i in range(2)]
        gat = [ar.alloc([TB], BF16, f"gat{i}") for i in range(3)]
        gbt = [ar.alloc([TB], BF16, f"gbt{i}") for i in range(3)]
        att = [ar.alloc([TB], BF16, f"att{i}") for i in range(3)]
        sgt = [ar.alloc([TB], F32, f"sgt{i}") for i in range(2)]
        s5t = [ar.alloc([TB], BF16, f"s5t{i}") for i in range(2)]
        m1t = [ar.alloc([TB], BF16, f"m1t{i}") for i in range(2)]
        mst = [ar.alloc([HL2], BF16, f"mst{i}") for i in range(2)]
        wav = W["w_glu_a"].rearrange("(k p) n -> p k n", p=128)
        wbv = W["w_glu_b"].rearrange("(k p) n -> p k n", p=128)
        li = 0
        bi_ = 0
        for hf in range(NH2):
            c0 = hf * HL2
            cx.dma(gTh, S["gT"].rearrange("(k p) t -> p k t", p=128)[:, :, c0:c0 + HL2])
            for sb in range(4):
                WA, WB = wa[sb % 2], wb_[sb % 2]
                wload(WA, "w_glu_a", wav[:, :, sb * 512:(sb + 1) * 512])
                wload(WB, "w_glu_b", wbv[:, :, sb * 512:(sb + 1) * 512])
                for j in range(4):
                    cb = sb * 4 + j
                    MS = mst[cb % 2]
                    for tb in range(HL2 // TB):
                        t0 = tb * TB
                        g0 = c0 + t0
                        k3 = li % 3
                        k2 = li % 2
                        li += 1
                        cx.dma(gat[k3], S["gaT"][cb * 128:(cb + 1) * 128, g0:g0 + TB])
                        cx.dma(gbt[k3], S["gbT"][cb * 128:(cb + 1) * 128, g0:g0 + TB])
                        cx.dma(att[k3], S["attnT"][cb * 128:(cb + 1) * 128, g0:g0 + TB])
                        BA = ps[(bi_ % 4) * 2]
                        BB = ps[(bi_ % 4) * 2 + 1]
                        bi_ += 1
                        for kc in range(KC):
                            cx.mm(BA[:, 0:TB], WA[:, kc, j * 128:(j + 1) * 128], gTh[:, kc, t0:t0 + TB], kc == 0, kc == KC - 1)
                        for kc in range(KC):
                            cx.mm(BB[:, 0:TB], WB[:, kc, j * 128:(j + 1) * 128], gTh[:, kc, t0:t0 + TB], kc == 0, kc == KC - 1)
                        cx.op("act", "activation", {"out": sgt[k2]}, {"in_": BB[:, 0:TB]}, func=AF.Sigmoid)
                        cx.op("dve", "tensor_tensor", {"out": s5t[k2]}, {"in0": BA[:, 0:TB], "in1": sgt[k2]}, op=ALU.mult)
                        cx.op("pool", "tensor_tensor", {"out": m1t[k2]}, {"in0": gat[k3], "in1": att[k3]}, op=ALU.mult)
                        cx.op("pool", "tensor_tensor", {"out": s5t[k2]}, {"in0": s5t[k2], "in1": gbt[k3]}, op=ALU.mult)
                        cx.op("pool", "tensor_tensor", {"out": MS[:, t0:t0 + TB]}, {"in0": m1t[k2], "in1": s5t[k2]}, op=ALU.add)
                    cx.dma(S["mT"][cb * 128:(cb + 1) * 128, c0:c0 + HL2], MS, eng="pool")
        if debug == "G":
            break

        cx.barrier()
        ar.reset()
        TS_ = min(512, L)
        NT4 = TS_ // TT
        g2T = ar.alloc([KC], F32, "g2T")
        cx.dma(g2T, I["norm_mlp"].rearrange("(k p) -> p k", p=128), noncontig=True)
        gfin = ar.alloc([D], F32, "gfin")
        cx.dma(gfin, I["norm_final"].partition_broadcast(128))
        mTs = ar.alloc([KC, TS_], BF16, "mTs")
        hbuf = [ar.alloc([D], F32, f"hbuf{i}") for i in range(NT4)]
        hnb = [ar.alloc([D], BF16, f"hnb{i}") for i in range(2)]
        junk2 = ar.alloc([D], BF16, "junk2")
        hnT = mTs
        hid = ar.alloc([64, TS_], BF16, "hid")
        wo_ = [ar.alloc([4, 512], BF16, f"wo{i}") for i in range(3)]
        wu_ = [ar.alloc([KC, 256], BF16, f"wu{i}") for i in range(2)]
        wd_ = [ar.alloc([8, 512], BF16, f"wd{i}") for i in range(2)]
        rl = [ar.alloc([TS_], F32, f"rl{i}") for i in range(2)]
        st2 = ar.alloc([NT4, 8], F32, "st2")
        wov = W["w_o"].rearrange("(k p) n -> p k n", p=128)
        wuv = W["w_up"].rearrange("(k p) n -> p k n", p=128)
        wdv = W["w_down"].rearrange("(k p) n -> p k n", p=128)
        pT2 = [ps[4].bitcast(BF16), ps[5].bitcast(BF16)]
        wi_o = 0
        wi_d = 0
        bi_ = 0
        for st_ in range(L // TS_):
            s0 = st_ * TS_
            cx.dma(mTs, S["mT"].rearrange("(k p) t -> p k t", p=128)[:, :, s0:s0 + TS_])
            for t_ in range(NT4):
                cx.dma(hbuf[t_][0:TT], sq["x"][s0 + t_ * TT:s0 + (t_ + 1) * TT, :])
            for cb in range(4):
                for kg in range(4):
                    WO = wo_[wi_o % 3]
                    wi_o += 1
                    wload(WO, "w_o", wov[:, kg * 4:(kg + 1) * 4, cb * 512:(cb + 1) * 512])
                    for t_ in range(NT4):
                        for i in range(4):
                            kc = kg * 4 + i
                            cx.mm(ps[t_][0:TT, :], mTs[:, kc, t_ * TT:(t_ + 1) * TT], WO[:, i, :], kc == 0, kc == KC - 1)
                for t_ in range(NT4):
                    cx.op("dve", "tensor_tensor", {"out": hbuf[t_][0:TT, cb * 512:(cb + 1) * 512]},
                          {"in0": ps[t_][0:TT, :], "in1": hbuf[t_][0:TT, cb * 512:(cb + 1) * 512]}, op=ALU.add)
            for t_ in range(NT4):
                HB = hnb[t_ % 2]
                cx.op("act", "activation", {"out": junk2[0:TT], "accum_out": st2[0:TT, t_, 0:1]}, {"in_": hbuf[t_][0:TT]}, func=AF.Square)
                rstd_from_ss(st2[0:TT, t_, 0:1], D, st2[0:TT, t_, 1:2], st2[0:TT, t_, 2:3])
                cx.op("act", "activation", {"out": HB[0:TT]}, {"in_": hbuf[t_][0:TT], "scale": st2[0:TT, t_, 2:3]}, func=AF.Copy)
                for kc in range(KC):
                    cx.tr(pT2[kc // 8][:, (kc % 8) * 128:(kc % 8) * 128 + TT], HB[0:TT, kc * 128:(kc + 1) * 128], ident[0:TT, 0:TT])
                for hb in range(2):
                    cx.op("dve", "tensor_tensor",
                          {"out": hnT[:, hb * 8:(hb + 1) * 8, t_ * TT:(t_ + 1) * TT]},
                          {"in0": pT2[hb].rearrange("p (a b) -> p a b", b=128)[:, :, 0:TT],
                           "in1": Buf(g2T.ap[:, hb * 8:(hb + 1) * 8].unsqueeze(2).to_broadcast([128, 8, TT]), g2T.key)},
                          op=ALU.mult)
            for sb in range(32):
                WU = wu_[sb % 2]
                wload(WU, "w_up", wuv[:, :, sb * 256:(sb + 1) * 256])
                for j in range(2):
                    fb = sb * 2 + j
                    bn = ps[bi_ % 4]
                    RL = rl[bi_ % 2]
                    bi_ += 1
                    for kc in range(KC):
                        cx.mm(bn[:, 0:TS_], WU[:, kc, j * 128:(j + 1) * 128], hnT[:, kc, :], kc == 0, kc == KC - 1)
                    cx.op("act", "activation", {"out": RL}, {"in_": bn[:, 0:TS_]}, func=AF.Relu)
                    cx.op("pool" if (fb % 2) else "dve", "tensor_tensor", {"out": hid[:, fb, :]}, {"in0": RL, "in1": RL}, op=ALU.mult)
            for cb in range(4):
                for kg in range(8):
                    WD = wd_[wi_d % 2]
                    wi_d += 1
                    wload(WD, "w_down", wdv[:, kg * 8:(kg + 1) * 8, cb * 512:(cb + 1) * 512])
                    for t_ in range(NT4):
                        for i in range(8):
                            fk = kg * 8 + i
                            cx.mm(ps[t_][0:TT, :], hid[:, fk, t_ * TT:(t_ + 1) * TT], WD[:, i, :], fk == 0, fk == 63)
                for t_ in range(NT4):
                    cx.op("dve", "tensor_tensor", {"out": hbuf[t_][0:TT, cb * 512:(cb + 1) * 512]},
                          {"in0": ps[t_][0:TT, :], "in1": hbuf[t_][0:TT, cb * 512:(cb + 1) * 512]}, op=ALU.add)
            for t_ in range(NT4):
                cx.op("act", "activation", {"out": junk2[0:TT], "accum_out": st2[0:TT, t_, 3:4]}, {"in_": hbuf[t_][0:TT]}, func=AF.Square)
                rstd_from_ss(st2[0:TT, t_, 3:4], D, st2[0:TT, t_, 4:5], st2[0:TT, t_, 5:6])
                cx.op("dve", "scalar_tensor_tensor", {"out": hbuf[t_][0:TT]}, {"in0": hbuf[t_][0:TT], "scalar": st2[0:TT, t_, 5:6], "in1": gfin[0:TT]},
                      op0=ALU.mult, op1=ALU.mult)
                cx.dma(sq["y"][s0 + t_ * TT:s0 + (t_ + 1) * TT, :], hbuf[t_][0:TT], eng="pool")
    with nc.Block() as block:
        sems = cx.emit(block)
    for cm in reversed(sems):
        cm.__exit__(None, None, None)
    for cm in reversed(stack):
        cm.__exit__(None, None, None)
    return nc


LP_FULL, LS_FULL, NPAST_FULL = 4096, 32, 2048
_NC_CACHE = {}


def make_in_maps(inputs, LP, LS, NPAST, ncores):
    maps = []
    f = lambda a: np.ascontiguousarray(np.asarray(a, dtype=np.float32))
    for b in range(ncores):
        m = {
            "x_prompt": f(inputs["x_prompt"][b, :LP]),
            "x_sample": f(inputs["x_sample"][b, :LS]),
            "cache_ckv": f(inputs["cache_ckv"][0, b, :NPAST]),
            "cache_krope": f(inputs["cache_krope"][0, b, :NPAST]),
            "state_re": f(inputs["state_s5_re"][0, b]),
            "state_im": f(inputs["state_s5_im"][0, b]),
        }
        for nm in ["norm_mix", "w_in", "norm_q", "w_uq", "norm_kv", "s5_a_re", "s5_a_im", "s5_log_dt", "s5_b_re",
                   "s5_b_im", "s5_c_re", "s5_c_im", "s5_d", "w_glu_a", "w_glu_b", "w_o", "norm_mlp", "w_up", "w_down"]:
            m[nm] = f(inputs[nm][0])
        m["w_uk"] = f(np.asarray(inputs["w_uk"][0]).reshape(KVL, D))
        m["w_uv"] = f(np.asarray(inputs["w_uv"][0]).reshape(KVL, D))
        m["norm_final"] = f(inputs["norm_final"])
        maps.append(m)
    return maps


def kernel(**inputs):
    n = 8
    key = (LP_FULL, LS_FULL, NPAST_FULL)
    if key not in _NC_CACHE:
        _NC_CACHE[key] = build_program(*key)
    nc = _NC_CACHE[key]
    maps = make_in_maps(inputs, LP_FULL, LS_FULL, NPAST_FULL, n)
    res = run_bass_kernel_spmd(nc, maps, core_ids=list(range(n)))
    R = res.results
    st = lambda k: np.stack([np.asarray(R[b][k], dtype=np.float32) for b in range(n)], axis=0)
    return (st("y_p"), st("y_s"), st("ckv_p")[None], st("kr_p")[None], st("sre_p")[None], st("sim_p")[None],
            st("ckv_s")[None], st("kr_s")[None], st("sre_s")[None], st("sim_s")[None])
```
